# Optimizing a Trainium2 kernel written in Bass

```python
import jax, jax.numpy as jnp
from jax import lax
import numpy as np

D_MODEL = 1024
BATCH = 32
SEQ = 2048
DEPTH = 2

CTX_LEN = 256
GRID_W = 64

D_LRU = 512
LRU_BLOCKS = 8
LRU_BLOCK = D_LRU // LRU_BLOCKS
LRU_CONV = 4
LRU_CONV_LEFT = 2
LRU_C = 8.0
D_FOURIER = 256
D_POOL = 256
POOL_WINDOWS = (2, 4, 8, 16)
POOL_GROUP = D_POOL // len(POOL_WINDOWS)
D_SCONV = 256
SCONV_WIDTH = 3
SCONV_LEFT = 1
N_BRANCH = 4
D_FF = 4 * D_MODEL
N_MOD = 6
EPS = 1e-6

OFF_LRU_X = 0
OFF_LRU_G = OFF_LRU_X + D_LRU
OFF_FOURIER = OFF_LRU_G + D_LRU
OFF_POOL = OFF_FOURIER + D_FOURIER
OFF_SCONV = OFF_POOL + D_POOL
OFF_GATE = OFF_SCONV + 3 * D_SCONV
N_IN = OFF_GATE + N_BRANCH * D_MODEL

kernel_name = "hybrid_lru_fourier_pool_conv_dit"


def rmsnorm(x, g):
    xf = x.astype(jnp.float32)
    y = xf * lax.rsqrt(jnp.mean(xf * xf, axis=-1, keepdims=True) + EPS)
    return (y * g.astype(jnp.float32)).astype(x.dtype)


def modulate(u, shift, scale):
    return u * (1.0 + scale) + shift


def depthwise_conv(x, w, left):
    k, ch = w.shape
    return lax.conv_general_dilated(
        x, w[:, None, :].astype(x.dtype), window_strides=(1,),
        padding=[(left, k - 1 - left)],
        dimension_numbers=("NWC", "WIO", "NWC"), feature_group_count=ch)


def rglru_bidir(xa, w_a, b_a, w_x, b_x, lam, h0):
    bn, length, _ = xa.shape
    xf = xa.astype(jnp.float32)
    xs = jnp.stack([xf, xf[:, ::-1]], axis=0)
    xh = xs.reshape(2, bn, length, LRU_BLOCKS, LRU_BLOCK)
    r = jax.nn.sigmoid(jnp.einsum('dblhi,dhij->dblhj', xh, w_a.astype(jnp.float32))
                       .reshape(2, bn, length, D_LRU) + b_a.astype(jnp.float32)[:, None, None])
    gi = jax.nn.sigmoid(jnp.einsum('dblhi,dhij->dblhj', xh, w_x.astype(jnp.float32))
                        .reshape(2, bn, length, D_LRU) + b_x.astype(jnp.float32)[:, None, None])
    log_a = -LRU_C * r * jax.nn.softplus(-lam.astype(jnp.float32))[:, None, None]
    a = jnp.exp(log_a)
    b = jnp.sqrt(-jnp.expm1(2.0 * log_a)) * (gi * xs)
    b = b.at[:, :, 0].add(a[:, :, 0] * h0)

    def combine(e1, e2):
        a1, b1 = e1
        a2, b2 = e2
        return a1 * a2, a2 * b1 + b2

    _, h = lax.associative_scan(combine, (a, b), axis=2)
    return h


def fourier_mix(u):
    return jnp.fft.fft2(u.astype(jnp.float32), axes=(1, 2), norm="ortho").real.astype(u.dtype)


def multiscale_pool(u, pool_w, pool_scale):
    length = u.shape[2]
    uf = u.astype(jnp.float32)
    cs = jnp.pad(jnp.cumsum(uf, axis=2), ((0, 0), (0, 0), (1, 0), (0, 0)))
    t = jnp.arange(length)
    outs = []
    for gidx, w in enumerate(POOL_WINDOWS):
        lo = jnp.maximum(t - w // 2, 0)
        hi = jnp.minimum(t + w // 2, length)
        sl = slice(gidx * POOL_GROUP, (gidx + 1) * POOL_GROUP)
        csg = cs[..., sl]
        mean = (jnp.take(csg, hi, axis=2) - jnp.take(csg, lo, axis=2)) / (hi - lo).astype(jnp.float32)[:, None]
        outs.append(mean - uf[..., sl])
    p = jnp.concatenate(outs, axis=-1)
    shp = p.shape
    p = jnp.einsum('bnlgi,gij->bnlgj', p.reshape(shp[:-1] + (len(POOL_WINDOWS), POOL_GROUP)),
                   pool_w.astype(jnp.float32)).reshape(shp)
    return (p * pool_scale.astype(jnp.float32)).astype(u.dtype)


def token_mixer(u, pool_rows, h0, w_in, conv_w, conv_b, w_a, b_a, w_x, b_x, lam,
                pool_w, pool_scale, sconv_w, w_br_lru, w_br_fourier, w_br_pool,
                w_br_sconv, w_out):
    bn, length, _ = u.shape
    z = u @ w_in
    xa = depthwise_conv(z[..., OFF_LRU_X:OFF_LRU_G], conv_w, LRU_CONV_LEFT) + conv_b
    h = rglru_bidir(xa, w_a, b_a, w_x, b_x, lam, h0)
    h_last = h[:, :, -1]
    y_lru = ((h[0] + h[1][:, ::-1]) * jax.nn.gelu(z[..., OFF_LRU_G:OFF_FOURIER].astype(jnp.float32))).astype(u.dtype)
    y_fourier = fourier_mix(z[..., OFF_FOURIER:OFF_POOL])
    zp = z[..., OFF_POOL:OFF_SCONV].reshape(bn, pool_rows, length // pool_rows, D_POOL)
    y_pool = multiscale_pool(zp, pool_w, pool_scale).reshape(bn, length, D_POOL)
    zs = z[..., OFF_SCONV:OFF_GATE]
    gb, gc, hs = zs[..., :D_SCONV], zs[..., D_SCONV:2 * D_SCONV], zs[..., 2 * D_SCONV:]
    y_sconv = gb * depthwise_conv(gc * hs, sconv_w, SCONV_LEFT)
    gates = jax.nn.sigmoid(z[..., OFF_GATE:].reshape(bn, length, N_BRANCH, D_MODEL))
    merged = (gates[:, :, 0] * (y_lru @ w_br_lru)
              + gates[:, :, 1] * (y_fourier @ w_br_fourier)
              + gates[:, :, 2] * (y_pool @ w_br_pool)
              + gates[:, :, 3] * (y_sconv @ w_br_sconv))
    return merged @ w_out, h_last


def sqrelu_mlp(u, w1, w2):
    return jnp.square(jax.nn.relu(u @ w1)) @ w2


def setup_inputs(seed: int = 0) -> dict:
    key = jax.random.key(seed)
    ks = jax.random.split(key, 32)

    def nrm(k, shape, scale):
        return jax.random.normal(k, shape, jnp.float32) * scale

    u_lam = jax.random.uniform(ks[14], (DEPTH, 2, D_LRU), jnp.float32, 0.9, 0.999)
    a_lam = u_lam ** (1.0 / LRU_C)
    return {
        "x": nrm(ks[0], (BATCH, SEQ, D_MODEL), 1.0),
        "c": nrm(ks[1], (BATCH, D_MODEL), 1.0),
        "ctx": nrm(ks[2], (BATCH, CTX_LEN, D_MODEL), 1.0),
        "c_ctx": nrm(ks[3], (D_MODEL,), 1.0),
        "w_mod": nrm(ks[4], (DEPTH, D_MODEL, N_MOD * D_MODEL), 0.5 * D_MODEL ** -0.5),
        "b_mod": nrm(ks[5], (DEPTH, N_MOD * D_MODEL), 0.02),
        "g_norm1": 1.0 + nrm(ks[6], (DEPTH, D_MODEL), 0.02),
        "g_norm2": 1.0 + nrm(ks[7], (DEPTH, D_MODEL), 0.02),
        "w_in": nrm(ks[8], (DEPTH, D_MODEL, N_IN), D_MODEL ** -0.5),
        "lru_conv_w": nrm(ks[9], (DEPTH, LRU_CONV, D_LRU), LRU_CONV ** -0.5),
        "lru_conv_b": nrm(ks[10], (DEPTH, D_LRU), 0.02),
        "lru_w_a": nrm(ks[11], (DEPTH, 2, LRU_BLOCKS, LRU_BLOCK, LRU_BLOCK), LRU_BLOCK ** -0.5),
        "lru_b_a": nrm(ks[12], (DEPTH, 2, D_LRU), 0.1),
        "lru_w_x": nrm(ks[13], (DEPTH, 2, LRU_BLOCKS, LRU_BLOCK, LRU_BLOCK), LRU_BLOCK ** -0.5),
        "lru_b_x": nrm(ks[15], (DEPTH, 2, D_LRU), 0.1),
        "lru_lam": jnp.log(a_lam) - jnp.log1p(-a_lam),
        "pool_w": nrm(ks[16], (DEPTH, len(POOL_WINDOWS), POOL_GROUP, POOL_GROUP), POOL_GROUP ** -0.5),
        "pool_scale": 1.0 + nrm(ks[17], (DEPTH, D_POOL), 0.1),
        "sconv_w": nrm(ks[18], (DEPTH, SCONV_WIDTH, D_SCONV), SCONV_WIDTH ** -0.5),
        "w_br_lru": nrm(ks[19], (DEPTH, D_LRU, D_MODEL), D_LRU ** -0.5),
        "w_br_fourier": nrm(ks[20], (DEPTH, D_FOURIER, D_MODEL), D_FOURIER ** -0.5),
        "w_br_pool": nrm(ks[21], (DEPTH, D_POOL, D_MODEL), D_POOL ** -0.5),
        "w_br_sconv": nrm(ks[22], (DEPTH, D_SCONV, D_MODEL), D_SCONV ** -0.5),
        "w_out": nrm(ks[23], (DEPTH, D_MODEL, D_MODEL), D_MODEL ** -0.5),
        "w_ff1": nrm(ks[24], (DEPTH, D_MODEL, D_FF), D_MODEL ** -0.5),
        "w_ff2": nrm(ks[25], (DEPTH, D_FF, D_MODEL), D_FF ** -0.5),
        "g_final": 1.0 + nrm(ks[26], (D_MODEL,), 0.02),
    }


def reference(x, c, ctx, c_ctx, w_mod, b_mod, g_norm1, g_norm2, w_in, lru_conv_w, lru_conv_b,
              lru_w_a, lru_b_a, lru_w_x, lru_b_x, lru_lam, pool_w, pool_scale, sconv_w,
              w_br_lru, w_br_fourier, w_br_pool, w_br_sconv, w_out, w_ff1, w_ff2, g_final):
    bn = x.shape[0]
    rows = x.shape[1] // GRID_W
    for l in range(DEPTH):
        mixer_params = (w_in[l], lru_conv_w[l], lru_conv_b[l], lru_w_a[l], lru_b_a[l],
                        lru_w_x[l], lru_b_x[l], lru_lam[l], pool_w[l], pool_scale[l], sconv_w[l],
                        w_br_lru[l], w_br_fourier[l], w_br_pool[l], w_br_sconv[l], w_out[l])
        mod = (jax.nn.silu(c) @ w_mod[l] + b_mod[l]).reshape(bn, N_MOD, 1, D_MODEL)
        mod_c = (jax.nn.silu(c_ctx) @ w_mod[l] + b_mod[l]).reshape(N_MOD, D_MODEL)
        uc = modulate(rmsnorm(ctx, g_norm1[l]), mod_c[0], mod_c[1])
        h_zero = jnp.zeros((2, bn, D_LRU), jnp.float32)
        if l < DEPTH - 1:
            yc, h_ctx = token_mixer(uc, 1, h_zero, *mixer_params)
            ctx = ctx + mod_c[2] * yc
            ctx = ctx + mod_c[5] * sqrelu_mlp(
                modulate(rmsnorm(ctx, g_norm2[l]), mod_c[3], mod_c[4]), w_ff1[l], w_ff2[l])
        else:
            xa_c = depthwise_conv(uc @ w_in[l][:, OFF_LRU_X:OFF_LRU_G], lru_conv_w[l], LRU_CONV_LEFT) + lru_conv_b[l]
            h_ctx = rglru_bidir(xa_c, lru_w_a[l], lru_b_a[l], lru_w_x[l], lru_b_x[l], lru_lam[l], h_zero)[:, :, -1]
        ux = modulate(rmsnorm(x, g_norm1[l]), mod[:, 0], mod[:, 1])
        yx, _ = token_mixer(ux, rows, h_ctx, *mixer_params)
        x = x + mod[:, 2] * yx
        x = x + mod[:, 5] * sqrelu_mlp(
            modulate(rmsnorm(x, g_norm2[l]), mod[:, 3], mod[:, 4]), w_ff1[l], w_ff2[l])
    return rmsnorm(x, g_final)
```

```python
import math
from contextlib import ExitStack
import numpy as np
import ml_dtypes
import concourse.bass as bass
import concourse.mybir as mybir
from concourse.ap import AP
from concourse.bass_utils import run_bass_kernel_spmd

F32 = mybir.dt.float32
BF16 = mybir.dt.bfloat16
AF = mybir.ActivationFunctionType
ALU = mybir.AluOpType

D = 1024
SEQ = 2048
CTX = 256
NIN = 6400
DFF = 4096
EPS = 1e-6
ENGS = ("pe", "dve", "act", "pool", "sp")
SB_BASE = 16512
SB_END = 229376


class Res:
    __slots__ = ("name", "last_w", "readers", "dma_sem", "dma_cnt")

    def __init__(self, name, init_w=None):
        self.name = name
        self.last_w = init_w
        self.readers = []
        self.dma_sem = None
        self.dma_cnt = 0


class Op:
    __slots__ = ("eng", "fn", "deps", "is_dma", "sem_res", "dma_val", "sig", "sig_idx")

    def __init__(self, eng, fn, is_dma):
        self.eng = eng
        self.fn = fn
        self.deps = []
        self.is_dma = is_dma
        self.sem_res = None
        self.dma_val = 0
        self.sig = False
        self.sig_idx = 0


class Prog:
    def __init__(self, nc):
        self.nc = nc
        self.ops = []
        self.per_eng = {e: [] for e in ENGS}
        self.dma_res = []
        self.res = {}

    def R(self, *key):
        r = self.res.get(key)
        if r is None:
            r = Res("_".join(str(k) for k in key))
            self.res[key] = r
        return r

    def op(self, eng, meth, kw, reads=(), writes=(), dma_res=None):
        o = Op(eng, (meth, kw), dma_res is not None)
        seen = set()
        deps = o.deps
        for r in reads:
            d = r.last_w
            if d is not None and id(d) not in seen:
                seen.add(id(d))
                deps.append(d)
        for w in writes:
            d = w.last_w
            if d is not None and id(d) not in seen:
                seen.add(id(d))
                deps.append(d)
            for d in w.readers:
                if id(d) not in seen:
                    seen.add(id(d))
                    deps.append(d)
        for r in reads:
            r.readers.append(o)
        for w in writes:
            w.last_w = o
            w.readers = []
        if dma_res is not None:
            o.sem_res = dma_res
            dma_res.dma_cnt += 1
            o.dma_val = 16 * dma_res.dma_cnt
            if dma_res.dma_sem is None:
                dma_res.dma_sem = True
                self.dma_res.append(dma_res)
        self.per_eng[eng].append(o)
        self.ops.append(o)
        return o

    def emit(self, final_wait_ops=()):
        nc = self.nc
        for o in self.ops:
            for d in o.deps:
                if not d.is_dma and not (d.eng == "pe" and o.eng == "pe"):
                    d.sig = True
        for e in ENGS:
            c = 0
            for o in self.per_eng[e]:
                if o.sig and not o.is_dma:
                    c += 1
                    o.sig_idx = c
        with ExitStack() as st:
            esem = {e: st.enter_context(nc.semaphore("s_" + e)) for e in ENGS}
            for i, r in enumerate(self.dma_res):
                r.dma_sem = st.enter_context(nc.semaphore("d%d" % i))
            block = st.enter_context(nc.Block())

            def run_engine(e, engobj):
                waited = {}
                for o in self.per_eng[e]:
                    for d in o.deps:
                        if d.is_dma:
                            key = id(d.sem_res)
                            sem = d.sem_res.dma_sem
                            val = d.dma_val
                        else:
                            if d.eng == "pe" and e == "pe":
                                continue
                            key = d.eng
                            sem = esem[d.eng]
                            val = d.sig_idx
                        if waited.get(key, 0) >= val:
                            continue
                        waited[key] = val
                        engobj.wait_ge(sem, val)
                    ins = getattr(engobj, o.fn[0])(**o.fn[1])
                    if o.is_dma:
                        ins.then_inc(o.sem_res.dma_sem, 16)
                    elif o.sig:
                        ins.then_inc(esem[e], 1)
                if e == "sp":
                    for o in final_wait_ops:
                        engobj.wait_ge(o.sem_res.dma_sem, o.dma_val)

            block.tensor(lambda eng: run_engine("pe", eng))
            block.vector(lambda eng: run_engine("dve", eng))
            block.scalar(lambda eng: run_engine("act", eng))
            block.gpsimd(lambda eng: run_engine("pool", eng))
            block.sync(lambda eng: run_engine("sp", eng))


def _dft_tables(n):
    idx = np.arange(n, dtype=np.int64)
    m = (idx[:, None] * idx[None, :]) % n
    ang = 2.0 * np.pi * m.astype(np.float64) / n
    return np.cos(ang), np.sin(ang)


def _pool_matrix(w, dom):
    P = np.zeros((dom, dom), np.float64)
    for t in range(dom):
        lo = max(t - w // 2, 0)
        hi = min(t + w // 2, dom)
        P[lo:hi, t] += 1.0 / (hi - lo)
        P[t, t] -= 1.0
    return P


_CONST_CACHE = {}


def _consts():
    if _CONST_CACHE:
        return _CONST_CACHE
    bf = ml_dtypes.bfloat16
    cc, sc = _dft_tables(256)
    _CONST_CACHE["dft_c"] = np.concatenate([cc, sc], axis=1).astype(bf)
    cl, sl = _dft_tables(SEQ)
    _CONST_CACHE["dft_l"] = np.stack([cl, -sl]).astype(bf)
    cl, sl = _dft_tables(CTX)
    _CONST_CACHE["dft_lc"] = np.stack([cl, -sl]).astype(bf)
    pl = np.zeros((4, 128, 128), np.float64)
    pc = np.zeros((4, 2, 2, 128, 128), np.float64)
    for g, w in enumerate((2, 4, 8, 16)):
        p64 = _pool_matrix(w, 64)
        pl[g, :64, :64] = p64
        pl[g, 64:, 64:] = p64
        p256 = _pool_matrix(w, 256)
        for to in range(2):
            for ti in range(2):
                pc[g, to, ti] = p256[ti * 128:(ti + 1) * 128, to * 128:(to + 1) * 128]
    _CONST_CACHE["pm_lat"] = pl.astype(bf)
    _CONST_CACHE["pm_ctx"] = pc.astype(bf)
    return _CONST_CACHE


class VecLayout:
    def __init__(self, depth):
        self.col = {}
        n = 0

        def add(name, cnt):
            nonlocal n
            self.col[name] = n
            n += cnt
        for l in range(depth):
            add(("g1", l), 8)
            add(("g2", l), 8)
            add(("bmod", l), 48)
            add(("cw", l), 16)
            add(("cb", l), 4)
            add(("ba", l), 8)
            add(("bx", l), 8)
            add(("lam", l), 8)
            add(("pscale", l), 2)
            add(("sw", l), 6)
        add(("gf",), 8)
        self.n = n


def pack_vecs(inp, depth):
    vl = VecLayout(depth)
    v = np.zeros((128, vl.n), np.float32)

    def put(key, arr, cnt):
        a = np.asarray(arr, np.float32).reshape(cnt, 128).T
        v[:, vl.col[key]:vl.col[key] + cnt] = a
    for l in range(depth):
        put(("g1", l), inp["g_norm1"][l], 8)
        put(("g2", l), inp["g_norm2"][l], 8)
        put(("bmod", l), inp["b_mod"][l], 48)
        put(("cw", l), inp["lru_conv_w"][l], 16)
        put(("cb", l), inp["lru_conv_b"][l], 4)
        put(("ba", l), inp["lru_b_a"][l], 8)
        put(("bx", l), inp["lru_b_x"][l], 8)
        put(("lam", l), inp["lru_lam"][l], 8)
        put(("pscale", l), inp["pool_scale"][l], 2)
        put(("sw", l), inp["sconv_w"][l], 6)
    put(("gf",), inp["g_final"], 8)
    return v, vl


class Cfg:
    def __init__(self, NB=4, DEPTH=2, taps=()):
        self.NB = NB
        self.DEPTH = DEPTH
        self.taps = tuple(taps)


def build_program(cfg):
    NB, DEPTH = cfg.NB, cfg.DEPTH
    NM = NB + 1
    nc = bass.Bass("TRN2", target_bir_lowering=False)
    vl = VecLayout(DEPTH)

    def din(name, shape, dt=F32):
        return nc.dram_tensor(name, list(shape), dt, kind="ExternalInput").ap()

    x_d = din("x", [NB, SEQ, D])
    ctx_d = din("ctx", [NB, CTX, D])
    ct_d = din("cT", [128, 8 * NM])
    vecs_d = din("vecs", [128, vl.n])
    wmod_d = din("w_mod", [DEPTH, D, 6 * D])
    win_d = din("w_in", [DEPTH, D, NIN])
    wa_d = din("lru_w_a", [DEPTH, 2, 8, 64, 64])
    wx_d = din("lru_w_x", [DEPTH, 2, 8, 64, 64])
    pw_d = din("pool_w", [DEPTH, 4, 64, 64])
    wbr_d = [din("w_br_lru", [DEPTH, 512, D]), din("w_br_fourier", [DEPTH, 256, D]),
             din("w_br_pool", [DEPTH, 256, D]), din("w_br_sconv", [DEPTH, 256, D])]
    wout_d = din("w_out", [DEPTH, D, D])
    wff1_d = din("w_ff1", [DEPTH, D, DFF])
    wff2_d = din("w_ff2", [DEPTH, DFF, D])
    dftc_d = din("dft_c", [256, 512], BF16)
    dftl_d = din("dft_l", [2, SEQ, SEQ], BF16)
    dftlc_d = din("dft_lc", [2, CTX, CTX], BF16)
    pml_d = din("pm_lat", [4, 128, 128], BF16)
    pmc_d = din("pm_ctx", [4, 2, 2, 128, 128], BF16)
    out_d = nc.dram_tensor("out", [NB, SEQ, D], F32, kind="ExternalOutput").ap()
    tap_d = {}
    final_ops = []

    P = Prog(nc)
    R = P.R

    def MM(out, lhsT, rhs, start, stop, reads, writes, **kw):
        return P.op("pe", "matmul", dict(out=out, lhsT=lhsT, rhs=rhs, start=start, stop=stop, **kw), reads, writes)

    def TR(out, in_, reads, writes):
        return P.op("pe", "transpose", dict(out=out, in_=in_, identity=IDENT[:]), list(reads) + [R("IDENT")], writes)

    def ACT(out, in_, func, reads, writes, bias=None, scale=None):
        kw = dict(out=out, in_=in_, func=func)
        if bias is not None:
            kw["bias"] = bias
        if scale is not None:
            kw["scale"] = scale
        return P.op("act", "activation", kw, reads, writes)

    def CP(eng, out, in_, reads, writes):
        if eng == "act":
            return ACT(out, in_, AF.Copy, reads, writes)
        return P.op(eng, "tensor_copy", dict(out=out, in_=in_), reads, writes)

    def TT(eng, out, in0, in1, op, reads, writes):
        return P.op(eng, "tensor_tensor", dict(out=out, in0=in0, in1=in1, op=op), reads, writes)

    def TS(eng, out, in0, s1, s2, op0, op1, reads, writes):
        kw = dict(out=out, in0=in0, scalar1=s1, scalar2=s2, op0=op0)
        if op1 is not None:
            kw["op1"] = op1
        return P.op(eng, "tensor_scalar", kw, reads, writes)

    def STT(eng, out, in0, scalar, in1, op0, op1, reads, writes):
        return P.op(eng, "scalar_tensor_tensor", dict(out=out, in0=in0, scalar=scalar, in1=in1, op0=op0, op1=op1), reads, writes)

    def SCAN(out, a, b, init, reads, writes):
        return P.op("dve", "tensor_tensor_scan", dict(out=out, data0=a, data1=b, initial=init, op0=ALU.mult, op1=ALU.add), reads, writes)

    def DMA(eng, out, in_, reads, writes, dma_res):
        return P.op(eng, "dma_start", dict(out=out, in_=in_), reads, writes, dma_res=dma_res)

    def MEMSET(eng, ap, val, writes):
        return P.op(eng, "memset", dict(ap=ap, constant=val), (), writes)

    sb_off = [SB_BASE]

    def sb(name, shape, dt, off=None):
        nbytes = int(np.prod(shape[1:])) * (4 if dt == F32 else 2)
        if off is None:
            off = sb_off[0]
            sb_off[0] += (nbytes + 31) // 32 * 32
            assert sb_off[0] <= SB_END, ("SBUF overflow", name, sb_off[0])
        return nc.alloc_sbuf_tensor_at(name, list(shape), dt, offset=off)

    XR = sb("XR", [128, 8, SEQ], F32)
    U = sb("U", [128, 8, SEQ], BF16)
    y_base = sb_off[0]
    Y = sb("Y", [128, 10, SEQ], BF16)
    YT = [sb("YT%d" % i, [128, SEQ], F32, off=y_base + (4 + 2 * i) * SEQ * 2) for i in range(3)]
    wr_base = sb_off[0]
    WR = [sb("WR%d" % i, [128, 8, 256], BF16) for i in range(4)]
    WRH = [sb("WRH%d" % i, [128, 8, 128], BF16, off=wr_base + i * 2048) for i in range(8)]
    WRQ = [sb("WRQ%d" % i, [128, 4, 256], BF16, off=wr_base + i * 2048) for i in range(8)]
    IDENT = sb("IDENT", [128, 128], F32)
    ONES = sb("ONES", [128, 128], BF16)
    VEC = sb("VEC", [128, vl.n], F32)
    CT = sb("CT", [128, 8, NM], F32)
    MODT = sb("MODT", [128, DEPTH, 48, NM], F32)
    GS = sb("GS", [128, DEPTH, NM, 2, 8], F32)
    G32 = sb("G32", [128, DEPTH * 16 + 8], F32)
    NSP = sb("NSP", [128, DEPTH, 2, 8], F32)
    NBIAS = sb("NBIAS", [128, 4], F32)
    HB = sb("HB", [128, DEPTH, 2, 8], F32)
    WG = sb("WG", [128, DEPTH, 2, 2, 4, 128], BF16)
    PWB = sb("PWB", [128, DEPTH, 2, 128], BF16)
    DFTC = sb("DFTC", [128, 2, 512], BF16)
    HCTX = sb("HCTX", [128, DEPTH, NB, 2, 4], F32)
    ARENA_BYTES = 36864
    arena0 = sb_off[0]
    sb_off[0] += ARENA_BYTES
    assert sb_off[0] <= SB_END, ("SBUF overflow", sb_off[0])
    PS = nc.alloc_psum_tensor("PS", [128, 4096], F32)

    abuf_cache = {}

    class ABuf:
        def __init__(self, name, shape, dt, off):
            self.esz = 4 if dt == F32 else 2
            self.off = off
            self.W = shape[-1]
            nbytes = int(np.prod(shape[1:])) * self.esz
            assert off + nbytes <= ARENA_BYTES, name
            key = (name, tuple(shape), str(dt), off)
            if key not in abuf_cache:
                abuf_cache[key] = sb(name + "_%d" % len(abuf_cache), shape, dt, off=arena0 + off)
            self.t = abuf_cache[key]

        def res(self, c0, c1, i=0):
            b0 = (self.off + (i * self.W + c0) * self.esz) // 2048
            b1 = (self.off + (i * self.W + c1) * self.esz - 1) // 2048
            return [R("ar", b) for b in range(b0, b1 + 1)]

    def bk(b, c0, c1, p0=0, p1=128):
        return PS[p0:p1, b * 512 + c0:b * 512 + c1]

    def bres(i):
        return R("bank", i)

    bank_rr = [0]

    def next_bank():
        b = bank_rr[0]
        bank_rr[0] = (b + 1) % 8
        return b

    grp_rr = [0]

    def next_group():
        g = grp_rr[0]
        grp_rr[0] = 1 - g
        return g

    wr_rr = [0]
    uniq = [0]

    def vcol(key, j=0):
        c = vl.col[key] + j
        return VEC[:, c:c + 1]

    ev_rr = [0]

    def evac_eng():
        ev_rr[0] ^= 1
        return "act" if ev_rr[0] else "dve"

    def load_w(src_ap, kc, ncols):
        h = wr_rr[0]
        if ncols > 128 and kc <= 4:
            wr_rr[0] = (h + 1) % 8
            t, res = WRQ[h], [R("wrh", h)]
        elif ncols > 128:
            if h % 2:
                h = (h + 1) % 8
            wr_rr[0] = (h + 2) % 8
            t, res = WR[h // 2], [R("wrh", h), R("wrh", h + 1)]
        else:
            wr_rr[0] = (h + 1) % 8
            t, res = WRH[h], [R("wrh", h)]
        DMA("pool", t[:, 0:kc, 0:ncols], src_ap, (), res, res[0])
        return (t, res)

    def wsrc(w2d, r0, kc, c0, ncols):
        return w2d[r0:r0 + kc * 128, c0:c0 + ncols].rearrange("(k p) c -> p k c", p=128)

    def tap(name, src_ap, shape, reads):
        if name not in cfg.taps:
            return
        t = nc.dram_tensor("tap_" + name, list(shape), src_ap.dtype, kind="ExternalOutput").ap()
        tap_d[name] = t
        o = DMA("sp", t, src_ap, reads, [R("tap", name)], R("tap", name))
        final_ops.append(o)

    def tiles_of(T):
        return [(o, min(512, T - o)) for o in range(0, T, 512)]

    def setup():
        DMA("sp", VEC[:], vecs_d, (), [R("VEC")], R("VEC"))
        DMA("sp", CT[:], ct_d.rearrange("p (k b) -> p k b", b=NM), (), [R("CT")], R("CT"))
        DMA("sp", DFTC[:], dftc_d.rearrange("(k p) c -> p k c", p=128), (), [R("DFTC")], R("DFTC"))
        MEMSET("pool", IDENT[:], 1.0, [R("IDENT")])
        P.op("pool", "affine_select", dict(out=IDENT[:], in_=IDENT[:], pattern=[[-1, 128]], compare_op=ALU.is_equal,
                                           fill=0.0, base=0, channel_multiplier=1), [R("IDENT")], [R("IDENT")])
        MEMSET("dve", ONES[:], 1.0, [R("ONES")])
        MEMSET("dve", NBIAS[:, 0:1], 1.0, [R("NBIAS")])
        MEMSET("dve", NBIAS[:, 1:2], 1024.0 * EPS, [R("NBIAS")])
        MEMSET("dve", WG[:], 0.0, [R("WG")])
        MEMSET("dve", PWB[:], 0.0, [R("PWB")])
        for l in range(DEPTH):
            for d in range(2):
                for gi, wd in enumerate((wa_d, wx_d)):
                    for half in range(2):
                        src = wd[l, d].rearrange("(c two) i j -> two i c j", two=2)[half]
                        dst = WG[half * 64:(half + 1) * 64, l, d, gi, :, half * 64:(half + 1) * 64]
                        DMA("pool", dst, src, (), [R("WG")], R("WG"))
            for half in range(2):
                src = pw_d[l].rearrange("(c two) i j -> two i c j", two=2)[half]
                dst = PWB[half * 64:(half + 1) * 64, l, :, half * 64:(half + 1) * 64]
                DMA("pool", dst, src, (), [R("PWB")], R("PWB"))
        for l in range(DEPTH):
            c0 = vl.col[("g1", l)]
            TS("dve", G32[:, l * 16:l * 16 + 16], VEC[:, c0:c0 + 16], 32.0, None, ALU.mult, None, [R("VEC")], [R("G32")])
        c0 = vl.col[("gf",)]
        TS("dve", G32[:, DEPTH * 16:DEPTH * 16 + 8], VEC[:, c0:c0 + 8], 32.0, None, ALU.mult, None, [R("VEC")], [R("G32")])
        for l in range(DEPTH):
            c0 = vl.col[("lam", l)]
            ACT(NSP[:, l, 0, :], VEC[:, c0:c0 + 8], AF.Exp, [R("VEC")], [R("NSP")], scale=-1.0)
            ACT(NSP[:, l, 0, :], NSP[:, l, 0, :], AF.Ln, [R("NSP"), R("NBIAS")], [R("NSP")], bias=NBIAS[:, 0:1], scale=1.0)
            TS("dve", NSP[:, l, 1, :], NSP[:, l, 0, :], -8.0, None, ALU.mult, None, [R("NSP")], [R("NSP")])
            TS("dve", NSP[:, l, 0, :], NSP[:, l, 0, :], -4.0, None, ALU.mult, None, [R("NSP")], [R("NSP")])
            cba = vl.col[("ba", l)]
            TS("dve", HB[:, l, 0, :], VEC[:, cba:cba + 8], 0.5, None, ALU.mult, None, [R("VEC")], [R("HB")])
            cbx = vl.col[("bx", l)]
            TS("dve", HB[:, l, 1, :], VEC[:, cbx:cbx + 8], 0.5, None, ALU.mult, None, [R("VEC")], [R("HB")])
        ACT(CT[:], CT[:], AF.Silu, [R("CT")], [R("CT")])
        WM = [ABuf("WM%d" % i, [128, 8, 256], F32, i * 8192) for i in range(3)]
        wm_rr = 0
        for l in range(DEPTH):
            pb = next_bank()
            for jt in range(24):
                wm = WM[wm_rr % 3]
                wm_rr += 1
                DMA("sp", wm.t[:], wsrc(wmod_d[l], 0, 8, jt * 256, 256), (), wm.res(0, 2048), wm.res(0, 2048)[0])
                for oc in range(2):
                    mk = jt * 2 + oc
                    for k in range(8):
                        MM(bk(pb, mk * 8, mk * 8 + NM), wm.t[:, k, oc * 128:(oc + 1) * 128], CT[:, k, :], k == 0, k == 7,
                           wm.res(0, 2048) + [R("CT")], [bres(pb)])
            c0 = vl.col[("bmod", l)]
            full = bk(pb, 0, 384)
            src = AP(full.tensor, full.offset, [list(full.ap[0]), [8, 48], [1, NM]])
            bm = VEC[:, c0:c0 + 48]
            bmb = AP(bm.tensor, bm.offset, [list(bm.ap[0]), [1, 48], [0, NM]])
            TT("dve", MODT[:, l, :, :], src, bmb, ALU.add, [bres(pb), R("VEC")], [R("MODT")])
            for b in range(NM):
                for w in range(2):
                    m = 1 if w == 0 else 4
                    STT("dve", GS[:, l, b, w, :], MODT[:, l, m * 8:(m + 1) * 8, b], 1.0, G32[:, l * 16 + w * 8:l * 16 + w * 8 + 8], ALU.add, ALU.mult,
                        [R("MODT"), R("G32")], [R("GS")])

    def modv(l, m, k, b):
        return MODT[:, l, m * 8 + k, b:b + 1]

    def proj(ws, oc_col, kc, rhs_fn, rhs_res_fn, tiles, grp):
        for k in range(kc):
            for ti, (o, w) in enumerate(tiles):
                b = grp * 4 + ti
                MM(bk(b, 0, w), ws[0][:, k, oc_col:oc_col + 128], rhs_fn(k, o, w), k == 0, k == kc - 1,
                   ws[1] + rhs_res_fn(k, ti), [bres(b)])

    def u_rhs(k, o, w):
        return U[:, k, o:o + w]

    def u_res(k, ti):
        return [R("U", k, ti)]

    def load_tokens(src2d, T):
        STG = [ABuf("STG%d" % i, [128, 1024], F32, i * 4096) for i in range(8)]
        rr = 0
        for ti, (o, w) in enumerate(tiles_of(T)):
            ntc = w // 128
            st = []
            for tc in range(ntc):
                s = STG[rr % 8]
                rr += 1
                DMA("sp", s.t[:], src2d[o + tc * 128:o + (tc + 1) * 128, :], (), s.res(0, 1024), s.res(0, 1024)[0])
                st.append(s)
            for k in range(8):
                b = next_bank()
                for tc in range(ntc):
                    TR(bk(b, tc * 128, (tc + 1) * 128), st[tc].t[:, k * 128:(k + 1) * 128], st[tc].res(0, 1024), [bres(b)])
                CP(evac_eng(), XR[:, k, o:o + w], bk(b, 0, w), [bres(b)], [R("XR", k, ti)])

    class Norm:
        def __init__(self, T, gs_fn, sh_fn, out_inplace=False):
            self.tl = tiles_of(T)
            self.gs_fn, self.sh_fn, self.inplace = gs_fn, sh_fn, out_inplace
            self.SQ = ABuf("SQ", [128, 8, 512], BF16, 16384)
            self.RS = [ABuf("RS%d" % i, [128, 512], F32, 24576 + i * 2048) for i in range(2)]
            self.TM = [ABuf("TM%d" % i, [128, 512], F32, 28672 + i * 2048) for i in range(2)]

        def a(self, ti):
            o, w = self.tl[ti]
            SQ = self.SQ
            b = next_bank()
            for k in range(8):
                ACT(SQ.t[:, k, 0:w], XR[:, k, o:o + w], AF.Square, [R("XR", k, ti)], SQ.res(0, w, k))
                MM(bk(b, 0, w), ONES[:], SQ.t[:, k, 0:w], k == 0, k == 7, SQ.res(0, w, k) + [R("ONES")], [bres(b)])
            rs = self.RS[ti % 2]
            ACT(rs.t[:, 0:w], bk(b, 0, w), AF.Sqrt, [bres(b), R("NBIAS")], rs.res(0, w), bias=NBIAS[:, 1:2], scale=1.0)
            P.op("dve", "reciprocal", dict(out=rs.t[:, 0:w], in_=rs.t[:, 0:w]), rs.res(0, w), rs.res(0, w))

        def b(self, ti):
            o, w = self.tl[ti]
            rs = self.RS[ti % 2]
            for k in range(8):
                if self.inplace:
                    STT("dve", XR[:, k, o:o + w], XR[:, k, o:o + w], self.gs_fn(k), rs.t[:, 0:w], ALU.mult, ALU.mult,
                        [R("XR", k, ti), R("GS"), R("G32")] + rs.res(0, w), [R("XR", k, ti)])
                else:
                    tm = self.TM[k % 2]
                    STT("dve", tm.t[:, 0:w], XR[:, k, o:o + w], self.gs_fn(k), rs.t[:, 0:w], ALU.mult, ALU.mult,
                        [R("XR", k, ti), R("GS")] + rs.res(0, w), tm.res(0, w))
                    ACT(U[:, k, o:o + w], tm.t[:, 0:w], AF.Identity, tm.res(0, w) + [R("MODT")], [R("U", k, ti)], bias=self.sh_fn(k), scale=1.0)

        def run_all(self):
            for ti in range(len(self.tl)):
                self.a(ti)
                self.b(ti)

    def tail_schedule(nt, work, nrm, head=None):
        seq = [("w", 0)]
        for ti in range(1, nt):
            seq.append(("w", ti))
            seq.append(("n", ti - 1))
            if head is not None and ti - 1 >= 2:
                seq.append(("h", ti - 3))
        seq.append(("n", nt - 1))
        done_h = max(0, nt - 3) if head is not None else 0
        if head is not None:
            if nt - 1 >= 2:
                seq.append(("h", nt - 3))
                done_h = nt - 2
            for ti in range(done_h, nt):
                seq.append(("h", ti))
        for kind, ti in seq:
            if kind == "w":
                work(ti)
            elif kind == "n":
                nrm.a(ti)
                nrm.b(ti)
            else:
                head(ti)

    def lru(l, T, seqs, save_state):
        XA = ABuf("XA", [128, SEQ], F32, 0)
        XAB = ABuf("XAB", [128, SEQ], BF16, 8192)
        AT = [ABuf("LA%d" % i, [128, SEQ], F32, 12288 + i * 8192) for i in range(3)]
        tiles = tiles_of(T)
        nt = len(tiles)
        full = not save_state

        class TB:
            pass
        temps = []
        for i in range(3):
            tb = TB()
            tb.t = AT[i].t
            tb.r = AT[i].res(0, T)
            temps.append(tb)
        for i in range(3):
            tb = TB()
            tb.t = YT[i]
            tb.r = [R("Y", 4 + 2 * i + h, ti) for h in range(2) for ti in range(4)]
            temps.append(tb)

        def rev(t, a, n):
            x = t[:, a:a + n]
            return AP(x.tensor, x.offset + n - 1, [list(x.ap[0]), [-1, n]])

        def grp_ap(g, a, n):
            return PS[:, g * 2048 + a:g * 2048 + a + n]

        def grp_res(g):
            return [bres(g * 4 + ti) for ti in range(nt)]
        wtiles = {}

        def get_w(kind, cp):
            key = (kind, cp)
            if key not in wtiles:
                col = (0 if kind == "x" else 512) + cp * 256
                wtiles[key] = load_w(wsrc(win_d[l], 0, 8, col, 256), 8, 256)
            return wtiles[key]
        xr = XA.res(0, T)
        for cp_ in range(2):
            get_w("x", cp_)
            if full:
                get_w("g", cp_)

        def prelude_x(c):
            cp, oc = c // 2, c % 2
            g = next_group()
            proj(get_w("x", cp), oc * 128, 8, u_rhs, u_res, tiles, g)
            for (o, L, h0f, bi) in seqs:
                ACT(XA.t[:, o:o + L], grp_ap(g, o, L), AF.Identity, grp_res(g) + [R("VEC")], xr, bias=vcol(("cb", l), c), scale=vcol(("cw", l), 2 * 4 + c))
                for j, sh in ((0, -2), (1, -1), (3, 1)):
                    if sh < 0:
                        oo, io, n = o - sh, o, L + sh
                    else:
                        oo, io, n = o, o + sh, L - sh
                    STT("dve", XA.t[:, oo:oo + n], grp_ap(g, io, n), vcol(("cw", l), j * 4 + c), XA.t[:, oo:oo + n], ALU.mult, ALU.add,
                        grp_res(g) + xr + [R("VEC")], xr)
            CP("act", XAB.t[:, 0:T], XA.t[:, 0:T], xr, XAB.res(0, T))
            if c == 0:
                tap("xa0_l%d_T%d" % (l, T), XA.t[:, 0:T], [128, T], xr)

        def prelude_g(c):
            if not full:
                return
            cp, oc = c // 2, c % 2
            g = next_group()
            proj(get_w("g", cp), oc * 128, 8, u_rhs, u_res, tiles, g)
            tb = temps[5]
            pz, pr = grp_ap(g, 0, T), grp_res(g)
            tw_ = tb.t[:, 0:T]
            yres = [R("Y", c, ti) for ti in range(nt)]
            ACT(tw_, pz, AF.Square, pr, tb.r, scale=0.21145921592590665)
            STT("dve", tw_, tw_, 1.0, pz, ALU.add, ALU.mult, tb.r + pr, tb.r)
            ACT(tw_, tw_, AF.Tanh, tb.r, tb.r, scale=0.7978845608028654)
            STT("dve", Y[:, c, 0:T], tw_, 1.0, pz, ALU.add, ALU.mult, tb.r + pr, yres)

        def gate_act(c, d):
            tR, tI, tS = temps[3 * d], temps[3 * d + 1], temps[3 * d + 2]
            ga, gb_ = next_group(), next_group()
            for gi, gg in ((0, ga), (1, gb_)):
                for ti, (o, w) in enumerate(tiles):
                    MM(bk(gg * 4 + ti, 0, w), WG[:, l, d, gi, c, :], XAB.t[:, o:o + w], True, True, [R("WG")] + XAB.res(0, T), [bres(gg * 4 + ti)])
            dc = d * 4 + c
            ACT(tI.t[:, 0:T], grp_ap(gb_, 0, T), AF.Tanh, grp_res(gb_) + [R("HB")], tI.r, bias=HB[:, l, 1, dc:dc + 1], scale=0.5)
            ACT(tR.t[:, 0:T], grp_ap(ga, 0, T), AF.Tanh, grp_res(ga) + [R("HB")], tR.r, bias=HB[:, l, 0, dc:dc + 1], scale=0.5)
            STT("dve", tI.t[:, 0:T], tI.t[:, 0:T], 1.0, XA.t[:, 0:T], ALU.add, ALU.mult, tI.r + xr, tI.r)
            ACT(tS.t[:, 0:T], tR.t[:, 0:T], AF.Exp, tR.r + [R("NSP")], tS.r, bias=NSP[:, l, 1, dc:dc + 1], scale=NSP[:, l, 1, dc:dc + 1])
            ACT(tR.t[:, 0:T], tR.t[:, 0:T], AF.Exp, tR.r + [R("NSP")], tR.r, bias=NSP[:, l, 0, dc:dc + 1], scale=NSP[:, l, 0, dc:dc + 1])
            ACT(tS.t[:, 0:T], tS.t[:, 0:T], AF.Sqrt, tS.r, tS.r, bias=0.25, scale=-0.25)

        def scan_dve(c, d):
            tR, tI, tS = temps[3 * d], temps[3 * d + 1], temps[3 * d + 2]
            TT("pool" if d == 1 else "dve", tI.t[:, 0:T], tI.t[:, 0:T], tS.t[:, 0:T], ALU.mult, tI.r + tS.r, tI.r)
            for (o, L, h0f, bi) in seqs:
                if d == 0:
                    SCAN(tS.t[:, o:o + L], tR.t[:, o:o + L], tI.t[:, o:o + L], h0f(d, c), tR.r + tI.r + [R("HCTX")], tS.r)
                    last = tS.t[:, o + L - 1:o + L]
                else:
                    SCAN(rev(tS.t, o, L), rev(tR.t, o, L), rev(tI.t, o, L), h0f(d, c), tR.r + tI.r + [R("HCTX")], tS.r)
                    last = tS.t[:, o:o + 1]
                if bi is not None:
                    CP("dve", HCTX[:, l, bi, d, c:c + 1], last, tS.r, [R("HCTX")])

        def finish(c):
            if not full:
                return
            hf, hb = temps[2], temps[5]
            yres = [R("Y", c, ti) for ti in range(nt)]
            TT("dve", hb.t[:, 0:T], hb.t[:, 0:T], hf.t[:, 0:T], ALU.add, hb.r + hf.r, hb.r)
            STT("dve", Y[:, c, 0:T], hb.t[:, 0:T], 0.5, Y[:, c, 0:T], ALU.mult, ALU.mult, hb.r + yres, yres)

        prelude_x(0)
        prelude_g(0)
        for c in range(4):
            gate_act(c, 0)
            scan_dve(c, 0)
            gate_act(c, 1)
            if c < 3:
                prelude_x(c + 1)
            scan_dve(c, 1)
            finish(c)
            if c < 3:
                prelude_g(c + 1)
        if full:
            tap("ylru_l%d_T%d" % (l, T), Y[:, 0:4, 0:T], [128, 4, T], [R("Y", c, ti) for c in range(4) for ti in range(len(tiles))])

    def fourier(l, T, seqs, is_ctx):
        ZF = ABuf("ZF", [128, 2, SEQ], BF16, 0)
        AB = ABuf("AB", [128, 16, 512], BF16, 8192)
        RING = [ABuf("FR%d" % i, [128, SEQ], BF16, 24576 + i * 4096) for i in range(3)]
        tiles = tiles_of(T)
        ws = load_w(wsrc(win_d[l], 0, 8, 1024, 256), 8, 256)
        for oc in range(2):
            g = next_group()
            proj(ws, oc * 128, 8, u_rhs, u_res, tiles, g)
            for ti, (o, w) in enumerate(tiles):
                CP(evac_eng(), ZF.t[:, oc, o:o + w], bk(g * 4 + ti, 0, w), [bres(g * 4 + ti)], ZF.res(o, o + w, oc))
        rr = 0
        for (o, L, _h, _b) in seqs:
            ntc = L // 128
            scale = 1.0 / math.sqrt(L * 256.0)
            for tc in range(ntc):
                b = next_bank()
                to = o + tc * 128
                for kc in range(2):
                    MM(bk(b, 0, 512), ZF.t[:, kc, to:to + 128], DFTC[:, kc, :], kc == 0, kc == 1, ZF.res(to, to + 128, kc) + [R("DFTC")], [bres(b)])
                CP(evac_eng(), AB.t[:, tc, :], bk(b, 0, 512), [bres(b)], AB.res(0, 512, tc))
            nlt = (L + 511) // 512
            lw = min(L, 512)
            if is_ctx:
                cbanks = [next_bank() for _ in range(2)]
                bmap = lambda cc, lt: cbanks[cc]
            else:
                bmap = lambda cc, lt: cc * 4 + lt
            tab = dftlc_d if is_ctx else dftl_d
            for tc in range(ntc):
                for tb in range(2):
                    ring = RING[rr % 3]
                    rr += 1
                    DMA("sp", ring.t[:, 0:L], tab[tb, tc * 128:(tc + 1) * 128, :], (), ring.res(0, L), ring.res(0, L)[0])
                    for cc in range(2):
                        for lt in range(nlt):
                            b = bmap(cc, lt)
                            MM(bk(b, 0, lw), AB.t[:, tc, tb * 256 + cc * 128:tb * 256 + (cc + 1) * 128], ring.t[:, lt * 512:lt * 512 + lw],
                               tc == 0 and tb == 0, tc == ntc - 1 and tb == 1, AB.res(0, 512, tc) + ring.res(0, L), [bres(b)])
            for cc in range(2):
                for lt in range(nlt):
                    b = bmap(cc, lt)
                    yo = o + lt * 512
                    ACT(Y[:, 4 + cc, yo:yo + lw], bk(b, 0, lw), AF.Copy, [bres(b)], [R("Y", 4 + cc, yo // 512)], scale=scale)
        tap("yfou_l%d_T%d" % (l, T), Y[:, 4:6, 0:T], [128, 2, T], [R("Y", 4 + c, ti) for c in range(2) for ti in range(len(tiles))])

    def pool_mix(l, T, seqs, is_ctx):
        ZPT = ABuf("ZPT", [128, 16, 256], BF16, 0)
        PP = ABuf("PP", [128, 2, SEQ], BF16, 8192)
        tiles = tiles_of(T)
        if is_ctx:
            PM = ABuf("PMC", [128, 16, 128], BF16, 16384)
            DMA("sp", PM.t[:], pmc_d.rearrange("g a b p c -> p (g a b) c"), (), PM.res(0, 128 * 16), PM.res(0, 128 * 16)[0])
        else:
            PM = ABuf("PML", [128, 4, 128], BF16, 16384)
            DMA("sp", PM.t[:], pml_d.rearrange("g p c -> p g c"), (), PM.res(0, 128 * 4), PM.res(0, 128 * 4)[0])
        pm_res = PM.res(0, 128 * (16 if is_ctx else 4))
        ws = load_w(wsrc(win_d[l], 0, 8, 1280, 256), 8, 256)
        nchunk = T // 128
        for tc in range(nchunk):
            b = next_bank()
            for k in range(8):
                MM(bk(b, 0, 256), U[:, k, tc * 128:(tc + 1) * 128], ws[0][:, k, 0:256], k == 0, k == 7, ws[1] + [R("U", k, tc // 4)], [bres(b)])
            CP(evac_eng(), ZPT.t[:, tc, :], bk(b, 0, 256), [bres(b)], ZPT.res(0, 256, tc))
        for ti, (o, w) in enumerate(tiles):
            for cc in range(2):
                b = next_bank()
                for tl in range(w // 128):
                    tc = o // 128 + tl
                    if is_ctx:
                        base = (tc // 2) * 2
                        srcs = [(base + ti_in, tc % 2, ti_in) for ti_in in range(2)]
                    else:
                        srcs = [(tc, 0, 0)]
                    for gh in range(2):
                        gidx = cc * 2 + gh
                        for si, (tci, to_, ti_) in enumerate(srcs):
                            pm = PM.t[:, gidx * 4 + to_ * 2 + ti_, :] if is_ctx else PM.t[:, gidx, :]
                            MM(bk(b, tl * 128, (tl + 1) * 128, gh * 64, (gh + 1) * 64), ZPT.t[:, tci, gidx * 64:(gidx + 1) * 64], pm,
                               si == 0, si == len(srcs) - 1, ZPT.res(0, 256, tci) + pm_res, [bres(b)], tile_position=(0, gh * 64))
                CP(evac_eng(), PP.t[:, cc, o:o + w], bk(b, 0, w), [bres(b)], PP.res(o, o + w, cc))
        for cc in range(2):
            for ti, (o, w) in enumerate(tiles):
                b = next_bank()
                MM(bk(b, 0, w), PWB[:, l, cc, :], PP.t[:, cc, o:o + w], True, True, PP.res(o, o + w, cc) + [R("PWB")], [bres(b)])
                ACT(Y[:, 6 + cc, o:o + w], bk(b, 0, w), AF.Copy, [bres(b), R("VEC")], [R("Y", 6 + cc, ti)], scale=vcol(("pscale", l), cc))
        tap("ypool_l%d_T%d" % (l, T), Y[:, 6:8, 0:T], [128, 2, T], [R("Y", 6 + c, ti) for c in range(2) for ti in range(len(tiles))])

    def sconv(l, T, seqs):
        HS = ABuf("HS", [128, 2, SEQ], F32, 0)
        CV = ABuf("CV", [128, SEQ], F32, 16384)
        tiles = tiles_of(T)
        ws = load_w(wsrc(win_d[l], 0, 8, 1536 + 512, 256), 8, 256)
        for oc in range(2):
            g = next_group()
            proj(ws, oc * 128, 8, u_rhs, u_res, tiles, g)
            for ti, (o, w) in enumerate(tiles):
                CP(evac_eng(), HS.t[:, oc, o:o + w], bk(g * 4 + ti, 0, w), [bres(g * 4 + ti)], HS.res(o, o + w, oc))
        ws = load_w(wsrc(win_d[l], 0, 8, 1536 + 256, 256), 8, 256)
        for oc in range(2):
            g = next_group()
            proj(ws, oc * 128, 8, u_rhs, u_res, tiles, g)
            for ti, (o, w) in enumerate(tiles):
                TT("dve", HS.t[:, oc, o:o + w], HS.t[:, oc, o:o + w], bk(g * 4 + ti, 0, w), ALU.mult,
                   HS.res(o, o + w, oc) + [bres(g * 4 + ti)], HS.res(o, o + w, oc))
        ws = load_w(wsrc(win_d[l], 0, 8, 1536, 256), 8, 256)
        for oc in range(2):
            for (o, L, _h, _b) in seqs:
                mr = HS.res(o, o + L, oc)
                cr = CV.res(o, o + L)
                TS("dve", CV.t[:, o:o + L], HS.t[:, oc, o:o + L], vcol(("sw", l), 1 * 2 + oc), None, ALU.mult, None, mr + [R("VEC")], cr)
                STT("dve", CV.t[:, o + 1:o + L], HS.t[:, oc, o:o + L - 1], vcol(("sw", l), 0 * 2 + oc), CV.t[:, o + 1:o + L], ALU.mult, ALU.add,
                    mr + cr + [R("VEC")], cr)
                STT("dve", CV.t[:, o:o + L - 1], HS.t[:, oc, o + 1:o + L], vcol(("sw", l), 2 * 2 + oc), CV.t[:, o:o + L - 1], ALU.mult, ALU.add,
                    mr + cr + [R("VEC")], cr)
            g = next_group()
            proj(ws, oc * 128, 8, u_rhs, u_res, tiles, g)
            for ti, (o, w) in enumerate(tiles):
                TT("dve", Y[:, 8 + oc, o:o + w], CV.t[:, o:o + w], bk(g * 4 + ti, 0, w), ALU.mult, CV.res(o, o + w) + [bres(g * 4 + ti)], [R("Y", 8 + oc, ti)])
        tap("ysc_l%d_T%d" % (l, T), Y[:, 8:10, 0:T], [128, 2, T], [R("Y", 8 + c, ti) for c in range(2) for ti in range(len(tiles))])

    def pass_b(l, T, mi, nrm2, head_fn):
        MG = ABuf("MG", [128, 4, SEQ], BF16, 0)
        MACC = ABuf("MACC", [128, SEQ], F32, 16384)
        GT = ABuf("GT", [128, SEQ], BF16, 24576)
        TTB = [ABuf("TTB%d" % i, [128, 512], F32, 28672 + i * 2048) for i in range(2)]
        tiles = tiles_of(T)
        ybase = (0, 4, 6, 8)
        ykc = (4, 2, 2, 2)
        for half in range(2):
            for jj in range(4):
                j = half * 4 + jj
                for i in range(4):
                    wsg = load_w(wsrc(win_d[l], 0, 8, 2304 + i * 1024 + j * 128, 128), 8, 128)
                    wsb = load_w(wsrc(wbr_d[i][l], 0, ykc[i], j * 128, 128), ykc[i], 128)
                    ga = next_group()
                    proj(wsg, 0, 8, u_rhs, u_res, tiles, ga)
                    for ti, (o, w) in enumerate(tiles):
                        ACT(GT.t[:, o:o + w], bk(ga * 4 + ti, 0, w), AF.Sigmoid, [bres(ga * 4 + ti)], GT.res(o, o + w))
                    gb_ = next_group()
                    proj(wsb, 0, ykc[i], (lambda k, o, w, i=i: Y[:, ybase[i] + k, o:o + w]), (lambda k, ti, i=i: [R("Y", ybase[i] + k, ti)]), tiles, gb_)
                    for ti, (o, w) in enumerate(tiles):
                        b = gb_ * 4 + ti
                        if i == 0:
                            TT("dve", MACC.t[:, o:o + w], GT.t[:, o:o + w], bk(b, 0, w), ALU.mult, GT.res(o, o + w) + [bres(b)], MACC.res(o, o + w))
                        else:
                            tt = TTB[ti % 2]
                            TT("dve", tt.t[:, 0:w], GT.t[:, o:o + w], bk(b, 0, w), ALU.mult, GT.res(o, o + w) + [bres(b)], tt.res(0, w))
                            if i < 3:
                                TT("dve", MACC.t[:, o:o + w], MACC.t[:, o:o + w], tt.t[:, 0:w], ALU.add, MACC.res(o, o + w) + tt.res(0, w), MACC.res(o, o + w))
                            else:
                                TT("dve", MG.t[:, jj, o:o + w], MACC.t[:, o:o + w], tt.t[:, 0:w], ALU.add, MACC.res(o, o + w) + tt.res(0, w), MG.res(o, o + w, jj))
            if half == 0:
                tap("mg_l%d_T%d" % (l, T), MG.t[:, :, 0:T], [128, 4, T], [r for jj in range(4) for r in MG.res(0, T, jj)])
            if half == 1:
                wso = [load_w(wsrc(wout_d[l], 512, 4, jt * 256, 256), 4, 256) for jt in range(4)]

                def wo_tile(ti):
                    o, w = tiles[ti]
                    for jo in range(8):
                        ws = wso[jo // 2]
                        b = next_bank()
                        for k in range(4):
                            MM(bk(b, 0, w), ws[0][:, k, (jo % 2) * 128:(jo % 2) * 128 + 128], MG.t[:, k, o:o + w], k == 0, k == 3,
                               ws[1] + MG.res(o, o + w, k), [bres(b)])
                        STT("dve", XR[:, jo, o:o + w], bk(b, 0, w), modv(l, 2, jo, mi), XR[:, jo, o:o + w], ALU.mult, ALU.add,
                            [bres(b), R("XR", jo, ti), R("MODT")], [R("XR", jo, ti)])
                tail_schedule(len(tiles), wo_tile, nrm2, head=head_fn)
                break
            for jt in range(4):
                ws = load_w(wsrc(wout_d[l], half * 512, 4, jt * 256, 256), 4, 256)
                for oc in range(2):
                    jo = jt * 2 + oc
                    g = next_group()
                    proj(ws, oc * 128, 4, (lambda k, o, w: MG.t[:, k, o:o + w]), (lambda k, ti: MG.res(tiles[ti][0], tiles[ti][0] + tiles[ti][1], k)), tiles, g)
                    for ti, (o, w) in enumerate(tiles):
                        STT("dve", XR[:, jo, o:o + w], bk(g * 4 + ti, 0, w), modv(l, 2, jo, mi), XR[:, jo, o:o + w], ALU.mult, ALU.add,
                            [bres(g * 4 + ti), R("XR", jo, ti), R("MODT")], [R("XR", jo, ti)])

    def make_ffn(l, T, mi):
        RT = [ABuf("RT%d" % i, [128, 512], F32, i * 2048) for i in range(4)]
        tiles = tiles_of(T)
        rr = [0]
        head_ws = []

        def relu2(b, w, hc, o, ti):
            rt = RT[rr[0] % 4]
            rr[0] += 1
            ACT(rt.t[:, 0:w], bk(b, 0, w), AF.Relu, [bres(b)], rt.res(0, w))
            TT("dve", Y[:, hc, o:o + w], rt.t[:, 0:w], rt.t[:, 0:w], ALU.mult, rt.res(0, w), [R("Y", hc, ti)])

        def head(ti):
            if not head_ws:
                for jt in range(4):
                    head_ws.append(load_w(wsrc(wff1_d[l], 0, 8, jt * 256, 256), 8, 256))
            o, w = tiles[ti]
            for hc in range(8):
                ws = head_ws[hc // 2]
                b = next_bank()
                for k in range(8):
                    MM(bk(b, 0, w), ws[0][:, k, (hc % 2) * 128:(hc % 2) * 128 + 128], U[:, k, o:o + w], k == 0, k == 7,
                       ws[1] + [R("U", k, ti)], [bres(b)])
                relu2(b, w, hc, o, ti)

        def rest(tail_norm):
            for g4 in range(4):
                if g4 > 0:
                    for jt in range(4):
                        ws = load_w(wsrc(wff1_d[l], 0, 8, g4 * 1024 + jt * 256, 256), 8, 256)
                        for oc in range(2):
                            hc = jt * 2 + oc
                            g = next_group()
                            proj(ws, oc * 128, 8, u_rhs, u_res, tiles, g)
                            for ti, (o, w) in enumerate(tiles):
                                relu2(g * 4 + ti, w, hc, o, ti)
                if g4 < 3:
                    for jt in range(4):
                        ws = load_w(wsrc(wff2_d[l], g4 * 1024, 8, jt * 256, 256), 8, 256)
                        for oc in range(2):
                            jo = jt * 2 + oc
                            g = next_group()
                            proj(ws, oc * 128, 8, (lambda k, o, w: Y[:, k, o:o + w]), (lambda k, ti: [R("Y", k, ti)]), tiles, g)
                            for ti, (o, w) in enumerate(tiles):
                                STT("dve", XR[:, jo, o:o + w], bk(g * 4 + ti, 0, w), modv(l, 5, jo, mi), XR[:, jo, o:o + w], ALU.mult, ALU.add,
                                    [bres(g * 4 + ti), R("XR", jo, ti), R("MODT")], [R("XR", jo, ti)])
                else:
                    wst = [load_w(wsrc(wff2_d[l], g4 * 1024, 8, jt * 256, 256), 8, 256) for jt in range(4)]

                    def ff2_tile(ti):
                        o, w = tiles[ti]
                        for jo in range(8):
                            ws = wst[jo // 2]
                            b = next_bank()
                            for k in range(8):
                                MM(bk(b, 0, w), ws[0][:, k, (jo % 2) * 128:(jo % 2) * 128 + 128], Y[:, k, o:o + w], k == 0, k == 7,
                                   ws[1] + [R("Y", k, ti)], [bres(b)])
                            STT("dve", XR[:, jo, o:o + w], bk(b, 0, w), modv(l, 5, jo, mi), XR[:, jo, o:o + w], ALU.mult, ALU.add,
                                [bres(b), R("XR", jo, ti), R("MODT")], [R("XR", jo, ti)])
                    tail_schedule(len(tiles), ff2_tile, tail_norm)
        return head, rest

    def store_tokens(dst2d, T):
        OST = [ABuf("OST%d" % i, [128, 1024], F32, i * 4096) for i in range(4)]
        rr = 0
        for ti, (o, w) in enumerate(tiles_of(T)):
            for tc in range(w // 128):
                ost = OST[rr % 4]
                rr += 1
                for kh in range(2):
                    b = next_bank()
                    for kk in range(4):
                        k = kh * 4 + kk
                        TR(bk(b, kk * 128, (kk + 1) * 128), XR[:, k, o + tc * 128:o + (tc + 1) * 128], [R("XR", k, ti)], [bres(b)])
                    CP(evac_eng(), ost.t[:, kh * 512:(kh + 1) * 512], bk(b, 0, 512), [bres(b)], ost.res(kh * 512, (kh + 1) * 512))
                uniq[0] += 1
                op = DMA("sp", dst2d[o + tc * 128:o + (tc + 1) * 128, :], ost.t[:], ost.res(0, 1024), [R("outd", uniq[0])], ost.res(0, 1024)[0])
                final_ops.append(op)

    def mk_norm(T, l, mi, which):
        if which == "final":
            return Norm(T, (lambda k: G32[:, DEPTH * 16 + k:DEPTH * 16 + k + 1]), None, out_inplace=True)
        m_sh = 0 if which == 0 else 3
        return Norm(T, (lambda k: GS[:, l, mi, which, k:k + 1]), (lambda k: modv(l, m_sh, k, mi)))

    def layer(l, T, seqs, mi, is_ctx, state_only, next_norm):
        nt = len(tiles_of(T))

        def tapx(nm):
            tap("%s_l%d_T%d" % (nm, l, T), XR[:, :, 0:T], [128, 8, T], [R("XR", k, ti) for k in range(8) for ti in range(nt)])
        tap("u1_l%d_T%d" % (l, T), U[:, :, 0:T], [128, 8, T], [R("U", k, ti) for k in range(8) for ti in range(nt)])
        lru(l, T, seqs, save_state=state_only)
        if state_only:
            return
        fourier(l, T, seqs, is_ctx)
        pool_mix(l, T, seqs, is_ctx)
        sconv(l, T, seqs)
        ffn_head, ffn_rest = make_ffn(l, T, mi)
        pass_b(l, T, mi, mk_norm(T, l, mi, 1), ffn_head)
        tapx("xmid")
        ffn_rest(next_norm)
        tapx("xout")

    setup()
    tap("modt", MODT[:], [128, DEPTH, 48, NM], [R("MODT")])
    TC = NB * CTX
    load_tokens(ctx_d.rearrange("b t d -> (b t) d"), TC)
    mk_norm(TC, 0, NB, 0).run_all()
    for l in range(DEPTH):
        seqs = [(b * CTX, CTX, (lambda d, c: 0.0), b) for b in range(NB)]
        last = (l == DEPTH - 1)
        layer(l, TC, seqs, NB, True, last, None if last else mk_norm(TC, l + 1, NB, 0))
    tap("hctx", HCTX[:], [128, DEPTH, NB, 2, 4], [R("HCTX")])
    for b in range(NB):
        load_tokens(x_d[b], SEQ)
        mk_norm(SEQ, 0, b, 0).run_all()
        for l in range(DEPTH):
            seqs = [(0, SEQ, (lambda d, c, l=l, b=b: HCTX[:, l, b, d, c:c + 1]), None)]
            nxt = mk_norm(SEQ, l + 1, b, 0) if l + 1 < DEPTH else mk_norm(SEQ, 0, b, "final")
            layer(l, SEQ, seqs, b, False, False, nxt)
        store_tokens(out_d[b], SEQ)
    P.emit(final_wait_ops=final_ops)
    return nc, tap_d


def make_in_maps(inputs, n_cores, NB, DEPTH):
    cst = _consts()
    vecs, _ = pack_vecs(inputs, DEPTH)
    maps = []
    for ci in range(n_cores):
        bs = slice(ci * NB, (ci + 1) * NB)
        call = np.concatenate([np.asarray(inputs["c"][bs], np.float32), np.asarray(inputs["c_ctx"], np.float32)[None, :]], axis=0)
        NM = NB + 1
        ct = np.ascontiguousarray(call.reshape(NM, 8, 128).transpose(2, 1, 0).reshape(128, 8 * NM))
        m = {
            "x": np.ascontiguousarray(inputs["x"][bs], dtype=np.float32),
            "ctx": np.ascontiguousarray(inputs["ctx"][bs], dtype=np.float32),
            "cT": ct,
            "vecs": vecs,
        }
        for name in ("w_mod", "w_in", "lru_w_a", "lru_w_x", "pool_w", "w_br_lru", "w_br_fourier", "w_br_pool", "w_br_sconv", "w_out", "w_ff1", "w_ff2"):
            m[name] = np.ascontiguousarray(np.asarray(inputs[name], np.float32)[:DEPTH])
        for name in ("dft_c", "dft_l", "dft_lc", "pm_lat", "pm_ctx"):
            m[name] = cst[name]
        maps.append(m)
    return maps


def kernel(**inputs):
    n_cores, NB, DEPTH = 8, 4, 2
    inputs = {k: np.asarray(v) for k, v in inputs.items()}
    nc, _ = build_program(Cfg(NB=NB, DEPTH=DEPTH))
    maps = make_in_maps(inputs, n_cores, NB, DEPTH)
    res = run_bass_kernel_spmd(nc, maps, core_ids=list(range(n_cores)))
    out = np.concatenate([np.asarray(r["out"], dtype=np.float32) for r in res.results], axis=0)
    return out
```

```python
import math
from contextlib import ExitStack
import numpy as np
import ml_dtypes
import concourse.bass as bass
import concourse.mybir as mybir
from concourse.ap import AP
from concourse.bass_utils import run_bass_kernel_spmd

F32 = mybir.dt.float32
BF16 = mybir.dt.bfloat16
AF = mybir.ActivationFunctionType
ALU = mybir.AluOpType

D = 1024
SEQ = 2048
CTX = 256
NIN = 6400
DFF = 4096
EPS = 1e-6
ENGS = ("pe", "dve", "act", "pool", "sp")
SB_BASE = 16512
SB_END = 229376


class Res:
    __slots__ = ("name", "last_w", "readers", "dma_sem", "dma_cnt")

    def __init__(self, name, init_w=None):
        self.name = name
        self.last_w = init_w
        self.readers = []
        self.dma_sem = None
        self.dma_cnt = 0


class Op:
    __slots__ = ("eng", "fn", "deps", "is_dma", "sem_res", "dma_val", "sig", "sig_idx")

    def __init__(self, eng, fn, is_dma):
        self.eng = eng
        self.fn = fn
        self.deps = []
        self.is_dma = is_dma
        self.sem_res = None
        self.dma_val = 0
        self.sig = False
        self.sig_idx = 0


class Prog:
    def __init__(self, nc):
        self.nc = nc
        self.ops = []
        self.per_eng = {e: [] for e in ENGS}
        self.dma_res = []
        self.res = {}

    def R(self, *key):
        r = self.res.get(key)
        if r is None:
            r = Res("_".join(str(k) for k in key))
            self.res[key] = r
        return r

    def op(self, eng, meth, kw, reads=(), writes=(), dma_res=None):
        o = Op(eng, (meth, kw), dma_res is not None)
        seen = set()
        deps = o.deps
        for r in reads:
            d = r.last_w
            if d is not None and id(d) not in seen:
                seen.add(id(d))
                deps.append(d)
        for w in writes:
            d = w.last_w
            if d is not None and id(d) not in seen:
                seen.add(id(d))
                deps.append(d)
            for d in w.readers:
                if id(d) not in seen:
                    seen.add(id(d))
                    deps.append(d)
        for r in reads:
            r.readers.append(o)
        for w in writes:
            w.last_w = o
            w.readers = []
        if dma_res is not None:
            o.sem_res = dma_res
            dma_res.dma_cnt += 1
            o.dma_val = 16 * dma_res.dma_cnt
            if dma_res.dma_sem is None:
                dma_res.dma_sem = True
                self.dma_res.append(dma_res)
        self.per_eng[eng].append(o)
        self.ops.append(o)
        return o

    def emit(self, final_wait_ops=()):
        nc = self.nc
        for o in self.ops:
            for d in o.deps:
                if not d.is_dma and not (d.eng == "pe" and o.eng == "pe"):
                    d.sig = True
        for e in ENGS:
            c = 0
            for o in self.per_eng[e]:
                if o.sig and not o.is_dma:
                    c += 1
                    o.sig_idx = c
        with ExitStack() as st:
            esem = {e: st.enter_context(nc.semaphore("s_" + e)) for e in ENGS}
            for i, r in enumerate(self.dma_res):
                r.dma_sem = st.enter_context(nc.semaphore("d%d" % i))
            block = st.enter_context(nc.Block())

            def run_engine(e, engobj):
                waited = {}
                for o in self.per_eng[e]:
                    for d in o.deps:
                        if d.is_dma:
                            key = id(d.sem_res)
                            sem = d.sem_res.dma_sem
                            val = d.dma_val
                        else:
                            if d.eng == "pe" and e == "pe":
                                continue
                            key = d.eng
                            sem = esem[d.eng]
                            val = d.sig_idx
                        if waited.get(key, 0) >= val:
                            continue
                        waited[key] = val
                        engobj.wait_ge(sem, val)
                    ins = getattr(engobj, o.fn[0])(**o.fn[1])
                    if o.is_dma:
                        ins.then_inc(o.sem_res.dma_sem, 16)
                    elif o.sig:
                        ins.then_inc(esem[e], 1)
                if e == "sp":
                    for o in final_wait_ops:
                        engobj.wait_ge(o.sem_res.dma_sem, o.dma_val)

            block.tensor(lambda eng: run_engine("pe", eng))
            block.vector(lambda eng: run_engine("dve", eng))
            block.scalar(lambda eng: run_engine("act", eng))
            block.gpsimd(lambda eng: run_engine("pool", eng))
            block.sync(lambda eng: run_engine("sp", eng))


def _dft_tables(n):
    idx = np.arange(n, dtype=np.int64)
    m = (idx[:, None] * idx[None, :]) % n
    ang = 2.0 * np.pi * m.astype(np.float64) / n
    return np.cos(ang), np.sin(ang)


def _pool_matrix(w, dom):
    P = np.zeros((dom, dom), np.float64)
    for t in range(dom):
        lo = max(t - w // 2, 0)
        hi = min(t + w // 2, dom)
        P[lo:hi, t] += 1.0 / (hi - lo)
        P[t, t] -= 1.0
    return P


_CONST_CACHE = {}


def _consts():
    if _CONST_CACHE:
        return _CONST_CACHE
    bf = ml_dtypes.bfloat16
    cc, sc = _dft_tables(256)
    _CONST_CACHE["dft_c"] = np.concatenate([cc, sc], axis=1).astype(bf)
    cl, sl = _dft_tables(SEQ)
    _CONST_CACHE["dft_l"] = np.stack([cl, -sl]).astype(bf)
    cl, sl = _dft_tables(CTX)
    _CONST_CACHE["dft_lc"] = np.stack([cl, -sl]).astype(bf)
    pl = np.zeros((4, 128, 128), np.float64)
    pc = np.zeros((4, 2, 2, 128, 128), np.float64)
    for g, w in enumerate((2, 4, 8, 16)):
        p64 = _pool_matrix(w, 64)
        pl[g, :64, :64] = p64
        pl[g, 64:, 64:] = p64
        p256 = _pool_matrix(w, 256)
        for to in range(2):
            for ti in range(2):
                pc[g, to, ti] = p256[ti * 128:(ti + 1) * 128, to * 128:(to + 1) * 128]
    _CONST_CACHE["pm_lat"] = pl.astype(bf)
    _CONST_CACHE["pm_ctx"] = pc.astype(bf)
    return _CONST_CACHE


class VecLayout:
    def __init__(self, depth):
        self.col = {}
        n = 0

        def add(name, cnt):
            nonlocal n
            self.col[name] = n
            n += cnt
        for l in range(depth):
            add(("g1", l), 8)
            add(("g2", l), 8)
            add(("bmod", l), 48)
            add(("cw", l), 16)
            add(("cb", l), 4)
            add(("ba", l), 8)
            add(("bx", l), 8)
            add(("lam", l), 8)
            add(("pscale", l), 2)
            add(("sw", l), 6)
        add(("gf",), 8)
        self.n = n


def pack_vecs(inp, depth):
    vl = VecLayout(depth)
    v = np.zeros((128, vl.n), np.float32)

    def put(key, arr, cnt):
        a = np.asarray(arr, np.float32).reshape(cnt, 128).T
        v[:, vl.col[key]:vl.col[key] + cnt] = a
    for l in range(depth):
        put(("g1", l), inp["g_norm1"][l], 8)
        put(("g2", l), inp["g_norm2"][l], 8)
        put(("bmod", l), inp["b_mod"][l], 48)
        put(("cw", l), inp["lru_conv_w"][l], 16)
        put(("cb", l), inp["lru_conv_b"][l], 4)
        put(("ba", l), inp["lru_b_a"][l], 8)
        put(("bx", l), inp["lru_b_x"][l], 8)
        put(("lam", l), inp["lru_lam"][l], 8)
        put(("pscale", l), inp["pool_scale"][l], 2)
        put(("sw", l), inp["sconv_w"][l], 6)
    put(("gf",), inp["g_final"], 8)
    return v, vl


class Cfg:
    def __init__(self, NB=4, DEPTH=2, taps=()):
        self.NB = NB
        self.DEPTH = DEPTH
        self.taps = tuple(taps)


def build_program(cfg):
    NB, DEPTH = cfg.NB, cfg.DEPTH
    NM = NB + 1
    nc = bass.Bass("TRN2", target_bir_lowering=False)
    vl = VecLayout(DEPTH)

    def din(name, shape, dt=F32):
        return nc.dram_tensor(name, list(shape), dt, kind="ExternalInput").ap()

    x_d = din("x", [NB, SEQ, D])
    ctx_d = din("ctx", [NB, CTX, D])
    ct_d = din("cT", [128, 8 * NM])
    vecs_d = din("vecs", [128, vl.n])
    wmod_d = din("w_mod", [DEPTH, D, 6 * D])
    win_d = din("w_in", [DEPTH, D, NIN])
    wa_d = din("lru_w_a", [DEPTH, 2, 8, 64, 64])
    wx_d = din("lru_w_x", [DEPTH, 2, 8, 64, 64])
    pw_d = din("pool_w", [DEPTH, 4, 64, 64])
    wbr_d = [din("w_br_lru", [DEPTH, 512, D]), din("w_br_fourier", [DEPTH, 256, D]),
             din("w_br_pool", [DEPTH, 256, D]), din("w_br_sconv", [DEPTH, 256, D])]
    wout_d = din("w_out", [DEPTH, D, D])
    wff1_d = din("w_ff1", [DEPTH, D, DFF])
    wff2_d = din("w_ff2", [DEPTH, DFF, D])
    dftc_d = din("dft_c", [256, 512], BF16)
    dftl_d = din("dft_l", [2, SEQ, SEQ], BF16)
    dftlc_d = din("dft_lc", [2, CTX, CTX], BF16)
    pml_d = din("pm_lat", [4, 128, 128], BF16)
    pmc_d = din("pm_ctx", [4, 2, 2, 128, 128], BF16)
    out_d = nc.dram_tensor("out", [NB, SEQ, D], F32, kind="ExternalOutput").ap()
    tap_d = {}
    final_ops = []

    P = Prog(nc)
    R = P.R

    def MM(out, lhsT, rhs, start, stop, reads, writes, **kw):
        return P.op("pe", "matmul", dict(out=out, lhsT=lhsT, rhs=rhs, start=start, stop=stop, **kw), reads, writes)

    def TR(out, in_, reads, writes):
        return P.op("pe", "transpose", dict(out=out, in_=in_, identity=IDENT[:]), list(reads) + [R("IDENT")], writes)

    def ACT(out, in_, func, reads, writes, bias=None, scale=None):
        kw = dict(out=out, in_=in_, func=func)
        if bias is not None:
            kw["bias"] = bias
        if scale is not None:
            kw["scale"] = scale
        return P.op("act", "activation", kw, reads, writes)

    def CP(eng, out, in_, reads, writes):
        if eng == "act":
            return ACT(out, in_, AF.Copy, reads, writes)
        return P.op(eng, "tensor_copy", dict(out=out, in_=in_), reads, writes)

    def TT(eng, out, in0, in1, op, reads, writes):
        return P.op(eng, "tensor_tensor", dict(out=out, in0=in0, in1=in1, op=op), reads, writes)

    def TS(eng, out, in0, s1, s2, op0, op1, reads, writes):
        kw = dict(out=out, in0=in0, scalar1=s1, scalar2=s2, op0=op0)
        if op1 is not None:
            kw["op1"] = op1
        return P.op(eng, "tensor_scalar", kw, reads, writes)

    def STT(eng, out, in0, scalar, in1, op0, op1, reads, writes):
        return P.op(eng, "scalar_tensor_tensor", dict(out=out, in0=in0, scalar=scalar, in1=in1, op0=op0, op1=op1), reads, writes)

    def SCAN(out, a, b, init, reads, writes):
        return P.op("dve", "tensor_tensor_scan", dict(out=out, data0=a, data1=b, initial=init, op0=ALU.mult, op1=ALU.add), reads, writes)

    def DMA(eng, out, in_, reads, writes, dma_res):
        return P.op(eng, "dma_start", dict(out=out, in_=in_), reads, writes, dma_res=dma_res)

    def MEMSET(eng, ap, val, writes):
        return P.op(eng, "memset", dict(ap=ap, constant=val), (), writes)

    sb_off = [SB_BASE]

    def sb(name, shape, dt, off=None):
        nbytes = int(np.prod(shape[1:])) * (4 if dt == F32 else 2)
        if off is None:
            off = sb_off[0]
            sb_off[0] += (nbytes + 31) // 32 * 32
            assert sb_off[0] <= SB_END, ("SBUF overflow", name, sb_off[0])
        return nc.alloc_sbuf_tensor_at(name, list(shape), dt, offset=off)

    XR = sb("XR", [128, 8, SEQ], F32)
    U = sb("U", [128, 8, SEQ], BF16)
    y_base = sb_off[0]
    Y = sb("Y", [128, 10, SEQ], BF16)
    YT = [sb("YT%d" % i, [128, SEQ], F32, off=y_base + (4 + 2 * i) * SEQ * 2) for i in range(3)]
    wr_base = sb_off[0]
    WR = [sb("WR%d" % i, [128, 8, 256], BF16) for i in range(4)]
    WRH = [sb("WRH%d" % i, [128, 8, 128], BF16, off=wr_base + i * 2048) for i in range(8)]
    WRQ = [sb("WRQ%d" % i, [128, 4, 256], BF16, off=wr_base + i * 2048) for i in range(8)]
    IDENT = sb("IDENT", [128, 128], F32)
    ONES = sb("ONES", [128, 128], BF16)
    VEC = sb("VEC", [128, vl.n], F32)
    CT = sb("CT", [128, 8, NM], F32)
    MODT = sb("MODT", [128, DEPTH, 48, NM], F32)
    GS = sb("GS", [128, DEPTH, NM, 2, 8], F32)
    G32 = sb("G32", [128, DEPTH * 16 + 8], F32)
    NSP = sb("NSP", [128, DEPTH, 2, 8], F32)
    NBIAS = sb("NBIAS", [128, 4], F32)
    HB = sb("HB", [128, DEPTH, 2, 8], F32)
    WG = sb("WG", [128, DEPTH, 2, 2, 4, 128], BF16)
    PWB = sb("PWB", [128, DEPTH, 2, 128], BF16)
    DFTC = sb("DFTC", [128, 2, 512], BF16)
    HCTX = sb("HCTX", [128, DEPTH, NB, 2, 4], F32)
    ARENA_BYTES = 36864
    arena0 = sb_off[0]
    sb_off[0] += ARENA_BYTES
    assert sb_off[0] <= SB_END, ("SBUF overflow", sb_off[0])
    PS = nc.alloc_psum_tensor("PS", [128, 4096], F32)

    abuf_cache = {}

    class ABuf:
        def __init__(self, name, shape, dt, off):
            self.esz = 4 if dt == F32 else 2
            self.off = off
            self.W = shape[-1]
            nbytes = int(np.prod(shape[1:])) * self.esz
            assert off + nbytes <= ARENA_BYTES, name
            key = (name, tuple(shape), str(dt), off)
            if key not in abuf_cache:
                abuf_cache[key] = sb(name + "_%d" % len(abuf_cache), shape, dt, off=arena0 + off)
            self.t = abuf_cache[key]

        def res(self, c0, c1, i=0):
            b0 = (self.off + (i * self.W + c0) * self.esz) // 2048
            b1 = (self.off + (i * self.W + c1) * self.esz - 1) // 2048
            return [R("ar", b) for b in range(b0, b1 + 1)]

    def bk(b, c0, c1, p0=0, p1=128):
        return PS[p0:p1, b * 512 + c0:b * 512 + c1]

    def bres(i):
        return R("bank", i)

    bank_rr = [0]

    def next_bank():
        b = bank_rr[0]
        bank_rr[0] = (b + 1) % 8
        return b

    grp_rr = [0]

    def next_group():
        g = grp_rr[0]
        grp_rr[0] = 1 - g
        return g

    wr_rr = [0]
    uniq = [0]

    def vcol(key, j=0):
        c = vl.col[key] + j
        return VEC[:, c:c + 1]

    ev_rr = [0]

    def evac_eng():
        ev_rr[0] ^= 1
        return "act" if ev_rr[0] else "dve"

    def load_w(src_ap, kc, ncols):
        h = wr_rr[0]
        if ncols > 128 and kc <= 4:
            wr_rr[0] = (h + 1) % 8
            t, res = WRQ[h], [R("wrh", h)]
        elif ncols > 128:
            if h % 2:
                h = (h + 1) % 8
            wr_rr[0] = (h + 2) % 8
            t, res = WR[h // 2], [R("wrh", h), R("wrh", h + 1)]
        else:
            wr_rr[0] = (h + 1) % 8
            t, res = WRH[h], [R("wrh", h)]
        DMA("pool", t[:, 0:kc, 0:ncols], src_ap, (), res, res[0])
        return (t, res)

    def wsrc(w2d, r0, kc, c0, ncols):
        return w2d[r0:r0 + kc * 128, c0:c0 + ncols].rearrange("(k p) c -> p k c", p=128)

    def tap(name, src_ap, shape, reads):
        if name not in cfg.taps:
            return
        t = nc.dram_tensor("tap_" + name, list(shape), src_ap.dtype, kind="ExternalOutput").ap()
        tap_d[name] = t
        o = DMA("sp", t, src_ap, reads, [R("tap", name)], R("tap", name))
        final_ops.append(o)

    def tiles_of(T):
        return [(o, min(512, T - o)) for o in range(0, T, 512)]

    def setup():
        DMA("sp", VEC[:], vecs_d, (), [R("VEC")], R("VEC"))
        DMA("sp", CT[:], ct_d.rearrange("p (k b) -> p k b", b=NM), (), [R("CT")], R("CT"))
        DMA("sp", DFTC[:], dftc_d.rearrange("(k p) c -> p k c", p=128), (), [R("DFTC")], R("DFTC"))
        MEMSET("pool", IDENT[:], 1.0, [R("IDENT")])
        P.op("pool", "affine_select", dict(out=IDENT[:], in_=IDENT[:], pattern=[[-1, 128]], compare_op=ALU.is_equal,
                                           fill=0.0, base=0, channel_multiplier=1), [R("IDENT")], [R("IDENT")])
        MEMSET("dve", ONES[:], 1.0, [R("ONES")])
        MEMSET("dve", NBIAS[:, 0:1], 1.0, [R("NBIAS")])
        MEMSET("dve", NBIAS[:, 1:2], 1024.0 * EPS, [R("NBIAS")])
        MEMSET("dve", WG[:], 0.0, [R("WG")])
        MEMSET("dve", PWB[:], 0.0, [R("PWB")])
        for l in range(DEPTH):
            for d in range(2):
                for gi, wd in enumerate((wa_d, wx_d)):
                    for half in range(2):
                        src = wd[l, d].rearrange("(c two) i j -> two i c j", two=2)[half]
                        dst = WG[half * 64:(half + 1) * 64, l, d, gi, :, half * 64:(half + 1) * 64]
                        DMA("pool", dst, src, (), [R("WG")], R("WG"))
            for half in range(2):
                src = pw_d[l].rearrange("(c two) i j -> two i c j", two=2)[half]
                dst = PWB[half * 64:(half + 1) * 64, l, :, half * 64:(half + 1) * 64]
                DMA("pool", dst, src, (), [R("PWB")], R("PWB"))
        for l in range(DEPTH):
            c0 = vl.col[("g1", l)]
            TS("dve", G32[:, l * 16:l * 16 + 16], VEC[:, c0:c0 + 16], 32.0, None, ALU.mult, None, [R("VEC")], [R("G32")])
        c0 = vl.col[("gf",)]
        TS("dve", G32[:, DEPTH * 16:DEPTH * 16 + 8], VEC[:, c0:c0 + 8], 32.0, None, ALU.mult, None, [R("VEC")], [R("G32")])
        for l in range(DEPTH):
            c0 = vl.col[("lam", l)]
            ACT(NSP[:, l, 0, :], VEC[:, c0:c0 + 8], AF.Exp, [R("VEC")], [R("NSP")], scale=-1.0)
            ACT(NSP[:, l, 0, :], NSP[:, l, 0, :], AF.Ln, [R("NSP"), R("NBIAS")], [R("NSP")], bias=NBIAS[:, 0:1], scale=1.0)
            TS("dve", NSP[:, l, 1, :], NSP[:, l, 0, :], -8.0, None, ALU.mult, None, [R("NSP")], [R("NSP")])
            TS("dve", NSP[:, l, 0, :], NSP[:, l, 0, :], -4.0, None, ALU.mult, None, [R("NSP")], [R("NSP")])
            cba = vl.col[("ba", l)]
            TS("dve", HB[:, l, 0, :], VEC[:, cba:cba + 8], 0.5, None, ALU.mult, None, [R("VEC")], [R("HB")])
            cbx = vl.col[("bx", l)]
            TS("dve", HB[:, l, 1, :], VEC[:, cbx:cbx + 8], 0.5, None, ALU.mult, None, [R("VEC")], [R("HB")])
        ACT(CT[:], CT[:], AF.Silu, [R("CT")], [R("CT")])
        WM = [ABuf("WM%d" % i, [128, 8, 256], F32, i * 8192) for i in range(3)]
        wm_rr = 0
        for l in range(DEPTH):
            pb = next_bank()
            for jt in range(24):
                wm = WM[wm_rr % 3]
                wm_rr += 1
                DMA("sp", wm.t[:], wsrc(wmod_d[l], 0, 8, jt * 256, 256), (), wm.res(0, 2048), wm.res(0, 2048)[0])
                for oc in range(2):
                    mk = jt * 2 + oc
                    for k in range(8):
                        MM(bk(pb, mk * 8, mk * 8 + NM), wm.t[:, k, oc * 128:(oc + 1) * 128], CT[:, k, :], k == 0, k == 7,
                           wm.res(0, 2048) + [R("CT")], [bres(pb)])
            c0 = vl.col[("bmod", l)]
            full = bk(pb, 0, 384)
            src = AP(full.tensor, full.offset, [list(full.ap[0]), [8, 48], [1, NM]])
            bm = VEC[:, c0:c0 + 48]
            bmb = AP(bm.tensor, bm.offset, [list(bm.ap[0]), [1, 48], [0, NM]])
            TT("dve", MODT[:, l, :, :], src, bmb, ALU.add, [bres(pb), R("VEC")], [R("MODT")])
            for b in range(NM):
                for w in range(2):
                    m = 1 if w == 0 else 4
                    STT("dve", GS[:, l, b, w, :], MODT[:, l, m * 8:(m + 1) * 8, b], 1.0, G32[:, l * 16 + w * 8:l * 16 + w * 8 + 8], ALU.add, ALU.mult,
                        [R("MODT"), R("G32")], [R("GS")])

    def modv(l, m, k, b):
        return MODT[:, l, m * 8 + k, b:b + 1]

    def proj(ws, oc_col, kc, rhs_fn, rhs_res_fn, tiles, grp):
        for k in range(kc):
            for ti, (o, w) in enumerate(tiles):
                b = grp * 4 + ti
                MM(bk(b, 0, w), ws[0][:, k, oc_col:oc_col + 128], rhs_fn(k, o, w), k == 0, k == kc - 1,
                   ws[1] + rhs_res_fn(k, ti), [bres(b)])

    def u_rhs(k, o, w):
        return U[:, k, o:o + w]

    def u_res(k, ti):
        return [R("U", k, ti)]

    def load_tokens(src2d, T):
        STG = [ABuf("STG%d" % i, [128, 1024], F32, i * 4096) for i in range(8)]
        rr = 0
        for ti, (o, w) in enumerate(tiles_of(T)):
            ntc = w // 128
            st = []
            for tc in range(ntc):
                s = STG[rr % 8]
                rr += 1
                DMA("sp", s.t[:], src2d[o + tc * 128:o + (tc + 1) * 128, :], (), s.res(0, 1024), s.res(0, 1024)[0])
                st.append(s)
            for k in range(8):
                b = next_bank()
                for tc in range(ntc):
                    TR(bk(b, tc * 128, (tc + 1) * 128), st[tc].t[:, k * 128:(k + 1) * 128], st[tc].res(0, 1024), [bres(b)])
                CP(evac_eng(), XR[:, k, o:o + w], bk(b, 0, w), [bres(b)], [R("XR", k, ti)])

    class Norm:
        def __init__(self, T, gs_fn, sh_fn, out_inplace=False):
            self.tl = tiles_of(T)
            self.gs_fn, self.sh_fn, self.inplace = gs_fn, sh_fn, out_inplace
            self.SQ = ABuf("SQ", [128, 8, 512], BF16, 16384)
            self.RS = [ABuf("RS%d" % i, [128, 512], F32, 24576 + i * 2048) for i in range(2)]
            self.TM = [ABuf("TM%d" % i, [128, 512], F32, 28672 + i * 2048) for i in range(2)]

        def a(self, ti):
            o, w = self.tl[ti]
            SQ = self.SQ
            b = next_bank()
            for k in range(8):
                ACT(SQ.t[:, k, 0:w], XR[:, k, o:o + w], AF.Square, [R("XR", k, ti)], SQ.res(0, w, k))
                MM(bk(b, 0, w), ONES[:], SQ.t[:, k, 0:w], k == 0, k == 7, SQ.res(0, w, k) + [R("ONES")], [bres(b)])
            rs = self.RS[ti % 2]
            ACT(rs.t[:, 0:w], bk(b, 0, w), AF.Sqrt, [bres(b), R("NBIAS")], rs.res(0, w), bias=NBIAS[:, 1:2], scale=1.0)
            P.op("dve", "reciprocal", dict(out=rs.t[:, 0:w], in_=rs.t[:, 0:w]), rs.res(0, w), rs.res(0, w))

        def b(self, ti):
            o, w = self.tl[ti]
            rs = self.RS[ti % 2]
            for k in range(8):
                if self.inplace:
                    STT("dve", XR[:, k, o:o + w], XR[:, k, o:o + w], self.gs_fn(k), rs.t[:, 0:w], ALU.mult, ALU.mult,
                        [R("XR", k, ti), R("GS"), R("G32")] + rs.res(0, w), [R("XR", k, ti)])
                else:
                    tm = self.TM[k % 2]
                    STT("dve", tm.t[:, 0:w], XR[:, k, o:o + w], self.gs_fn(k), rs.t[:, 0:w], ALU.mult, ALU.mult,
                        [R("XR", k, ti), R("GS")] + rs.res(0, w), tm.res(0, w))
                    ACT(U[:, k, o:o + w], tm.t[:, 0:w], AF.Identity, tm.res(0, w) + [R("MODT")], [R("U", k, ti)], bias=self.sh_fn(k), scale=1.0)

        def run_all(self):
            for ti in range(len(self.tl)):
                self.a(ti)
                self.b(ti)

    def tail_schedule(nt, work, nrm, head=None):
        seq = [("w", 0)]
        for ti in range(1, nt):
            seq.append(("w", ti))
            seq.append(("n", ti - 1))
            if head is not None and ti - 1 >= 2:
                seq.append(("h", ti - 3))
        seq.append(("n", nt - 1))
        done_h = max(0, nt - 3) if head is not None else 0
        if head is not None:
            if nt - 1 >= 2:
                seq.append(("h", nt - 3))
                done_h = nt - 2
            for ti in range(done_h, nt):
                seq.append(("h", ti))
        for kind, ti in seq:
            if kind == "w":
                work(ti)
            elif kind == "n":
                nrm.a(ti)
                nrm.b(ti)
            else:
                head(ti)

    def lru(l, T, seqs, save_state):
        XA = ABuf("XA", [128, SEQ], F32, 0)
        XAB = ABuf("XAB", [128, SEQ], BF16, 8192)
        AT = [ABuf("LA%d" % i, [128, SEQ], F32, 12288 + i * 8192) for i in range(3)]
        tiles = tiles_of(T)
        nt = len(tiles)
        full = not save_state

        class TB:
            pass
        temps = []
        for i in range(3):
            tb = TB()
            tb.t = AT[i].t
            tb.r = AT[i].res(0, T)
            temps.append(tb)
        for i in range(3):
            tb = TB()
            tb.t = YT[i]
            tb.r = [R("Y", 4 + 2 * i + h, ti) for h in range(2) for ti in range(4)]
            temps.append(tb)

        def rev(t, a, n):
            x = t[:, a:a + n]
            return AP(x.tensor, x.offset + n - 1, [list(x.ap[0]), [-1, n]])

        def grp_ap(g, a, n):
            return PS[:, g * 2048 + a:g * 2048 + a + n]

        def grp_res(g):
            return [bres(g * 4 + ti) for ti in range(nt)]
        wtiles = {}

        def get_w(kind, cp):
            key = (kind, cp)
            if key not in wtiles:
                col = (0 if kind == "x" else 512) + cp * 256
                wtiles[key] = load_w(wsrc(win_d[l], 0, 8, col, 256), 8, 256)
            return wtiles[key]
        xr = XA.res(0, T)
        for cp_ in range(2):
            get_w("x", cp_)
            if full:
                get_w("g", cp_)

        def prelude_x(c):
            cp, oc = c // 2, c % 2
            g = next_group()
            proj(get_w("x", cp), oc * 128, 8, u_rhs, u_res, tiles, g)
            for (o, L, h0f, bi) in seqs:
                ACT(XA.t[:, o:o + L], grp_ap(g, o, L), AF.Identity, grp_res(g) + [R("VEC")], xr, bias=vcol(("cb", l), c), scale=vcol(("cw", l), 2 * 4 + c))
                for j, sh in ((0, -2), (1, -1), (3, 1)):
                    if sh < 0:
                        oo, io, n = o - sh, o, L + sh
                    else:
                        oo, io, n = o, o + sh, L - sh
                    STT("dve", XA.t[:, oo:oo + n], grp_ap(g, io, n), vcol(("cw", l), j * 4 + c), XA.t[:, oo:oo + n], ALU.mult, ALU.add,
                        grp_res(g) + xr + [R("VEC")], xr)
            CP("act", XAB.t[:, 0:T], XA.t[:, 0:T], xr, XAB.res(0, T))
            if c == 0:
                tap("xa0_l%d_T%d" % (l, T), XA.t[:, 0:T], [128, T], xr)

        def prelude_g(c):
            if not full:
                return
            cp, oc = c // 2, c % 2
            g = next_group()
            proj(get_w("g", cp), oc * 128, 8, u_rhs, u_res, tiles, g)
            tb = temps[0]
            pz, pr = grp_ap(g, 0, T), grp_res(g)
            tw_ = tb.t[:, 0:T]
            yres = [R("Y", c, ti) for ti in range(nt)]
            ACT(tw_, pz, AF.Square, pr, tb.r, scale=0.21145921592590665)
            STT("dve", tw_, tw_, 1.0, pz, ALU.add, ALU.mult, tb.r + pr, tb.r)
            ACT(tw_, tw_, AF.Tanh, tb.r, tb.r, scale=0.7978845608028654)
            STT("dve", Y[:, c, 0:T], tw_, 1.0, pz, ALU.add, ALU.mult, tb.r + pr, yres)

        def gate_act(c, d):
            tR, tI, tS = temps[3 * d], temps[3 * d + 1], temps[3 * d + 2]
            ga, gb_ = next_group(), next_group()
            for gi, gg in ((0, ga), (1, gb_)):
                for ti, (o, w) in enumerate(tiles):
                    MM(bk(gg * 4 + ti, 0, w), WG[:, l, d, gi, c, :], XAB.t[:, o:o + w], True, True, [R("WG")] + XAB.res(0, T), [bres(gg * 4 + ti)])
            dc = d * 4 + c
            ACT(tI.t[:, 0:T], grp_ap(gb_, 0, T), AF.Tanh, grp_res(gb_) + [R("HB")], tI.r, bias=HB[:, l, 1, dc:dc + 1], scale=0.5)
            ACT(tR.t[:, 0:T], grp_ap(ga, 0, T), AF.Tanh, grp_res(ga) + [R("HB")], tR.r, bias=HB[:, l, 0, dc:dc + 1], scale=0.5)
            STT("dve", tI.t[:, 0:T], tI.t[:, 0:T], 1.0, XA.t[:, 0:T], ALU.add, ALU.mult, tI.r + xr, tI.r)
            ACT(tS.t[:, 0:T], tR.t[:, 0:T], AF.Exp, tR.r + [R("NSP")], tS.r, bias=NSP[:, l, 1, dc:dc + 1], scale=NSP[:, l, 1, dc:dc + 1])
            ACT(tR.t[:, 0:T], tR.t[:, 0:T], AF.Exp, tR.r + [R("NSP")], tR.r, bias=NSP[:, l, 0, dc:dc + 1], scale=NSP[:, l, 0, dc:dc + 1])
            ACT(tS.t[:, 0:T], tS.t[:, 0:T], AF.Sqrt, tS.r, tS.r, bias=0.25, scale=-0.25)

        def scan_dve(c, d):
            tR, tI, tS = temps[3 * d], temps[3 * d + 1], temps[3 * d + 2]
            TT("pool" if d == 1 else "dve", tI.t[:, 0:T], tI.t[:, 0:T], tS.t[:, 0:T], ALU.mult, tI.r + tS.r, tI.r)
            for (o, L, h0f, bi) in seqs:
                if d == 0:
                    SCAN(tS.t[:, o:o + L], tR.t[:, o:o + L], tI.t[:, o:o + L], h0f(d, c), tR.r + tI.r + [R("HCTX")], tS.r)
                    last = tS.t[:, o + L - 1:o + L]
                else:
                    SCAN(rev(tS.t, o, L), rev(tR.t, o, L), rev(tI.t, o, L), h0f(d, c), tR.r + tI.r + [R("HCTX")], tS.r)
                    last = tS.t[:, o:o + 1]
                if bi is not None:
                    CP("dve", HCTX[:, l, bi, d, c:c + 1], last, tS.r, [R("HCTX")])

        def finish(c):
            if not full:
                return
            hf, hb = temps[2], temps[5]
            yres = [R("Y", c, ti) for ti in range(nt)]
            TT("dve", hb.t[:, 0:T], hb.t[:, 0:T], hf.t[:, 0:T], ALU.add, hb.r + hf.r, hb.r)
            STT("dve", Y[:, c, 0:T], hb.t[:, 0:T], 0.5, Y[:, c, 0:T], ALU.mult, ALU.mult, hb.r + yres, yres)

        prelude_x(0)
        prelude_g(0)
        for c in range(4):
            gate_act(c, 0)
            scan_dve(c, 0)
            gate_act(c, 1)
            if c < 3:
                prelude_x(c + 1)
                prelude_g(c + 1)
            scan_dve(c, 1)
            finish(c)
        if full:
            tap("ylru_l%d_T%d" % (l, T), Y[:, 0:4, 0:T], [128, 4, T], [R("Y", c, ti) for c in range(4) for ti in range(len(tiles))])

    def fourier(l, T, seqs, is_ctx):
        ZF = ABuf("ZF", [128, 2, SEQ], BF16, 0)
        AB = ABuf("AB", [128, 16, 512], BF16, 8192)
        RING = [ABuf("FR%d" % i, [128, SEQ], BF16, 24576 + i * 4096) for i in range(3)]
        tiles = tiles_of(T)
        ws = load_w(wsrc(win_d[l], 0, 8, 1024, 256), 8, 256)
        for oc in range(2):
            g = next_group()
            proj(ws, oc * 128, 8, u_rhs, u_res, tiles, g)
            for ti, (o, w) in enumerate(tiles):
                CP(evac_eng(), ZF.t[:, oc, o:o + w], bk(g * 4 + ti, 0, w), [bres(g * 4 + ti)], ZF.res(o, o + w, oc))
        rr = 0
        for (o, L, _h, _b) in seqs:
            ntc = L // 128
            scale = 1.0 / math.sqrt(L * 256.0)
            for tc in range(ntc):
                b = next_bank()
                to = o + tc * 128
                for kc in range(2):
                    MM(bk(b, 0, 512), ZF.t[:, kc, to:to + 128], DFTC[:, kc, :], kc == 0, kc == 1, ZF.res(to, to + 128, kc) + [R("DFTC")], [bres(b)])
                CP(evac_eng(), AB.t[:, tc, :], bk(b, 0, 512), [bres(b)], AB.res(0, 512, tc))
            nlt = (L + 511) // 512
            lw = min(L, 512)
            if is_ctx:
                cbanks = [next_bank() for _ in range(2)]
                bmap = lambda cc, lt: cbanks[cc]
            else:
                bmap = lambda cc, lt: cc * 4 + lt
            tab = dftlc_d if is_ctx else dftl_d
            for tc in range(ntc):
                for tb in range(2):
                    ring = RING[rr % 3]
                    rr += 1
                    DMA("sp", ring.t[:, 0:L], tab[tb, tc * 128:(tc + 1) * 128, :], (), ring.res(0, L), ring.res(0, L)[0])
                    for cc in range(2):
                        for lt in range(nlt):
                            b = bmap(cc, lt)
                            MM(bk(b, 0, lw), AB.t[:, tc, tb * 256 + cc * 128:tb * 256 + (cc + 1) * 128], ring.t[:, lt * 512:lt * 512 + lw],
                               tc == 0 and tb == 0, tc == ntc - 1 and tb == 1, AB.res(0, 512, tc) + ring.res(0, L), [bres(b)])
            for cc in range(2):
                for lt in range(nlt):
                    b = bmap(cc, lt)
                    yo = o + lt * 512
                    ACT(Y[:, 4 + cc, yo:yo + lw], bk(b, 0, lw), AF.Copy, [bres(b)], [R("Y", 4 + cc, yo // 512)], scale=scale)
        tap("yfou_l%d_T%d" % (l, T), Y[:, 4:6, 0:T], [128, 2, T], [R("Y", 4 + c, ti) for c in range(2) for ti in range(len(tiles))])

    def pool_mix(l, T, seqs, is_ctx):
        ZPT = ABuf("ZPT", [128, 16, 256], BF16, 0)
        PP = ABuf("PP", [128, 2, SEQ], BF16, 8192)
        tiles = tiles_of(T)
        if is_ctx:
            PM = ABuf("PMC", [128, 16, 128], BF16, 16384)
            DMA("sp", PM.t[:], pmc_d.rearrange("g a b p c -> p (g a b) c"), (), PM.res(0, 128 * 16), PM.res(0, 128 * 16)[0])
        else:
            PM = ABuf("PML", [128, 4, 128], BF16, 16384)
            DMA("sp", PM.t[:], pml_d.rearrange("g p c -> p g c"), (), PM.res(0, 128 * 4), PM.res(0, 128 * 4)[0])
        pm_res = PM.res(0, 128 * (16 if is_ctx else 4))
        ws = load_w(wsrc(win_d[l], 0, 8, 1280, 256), 8, 256)
        nchunk = T // 128
        for tc in range(nchunk):
            b = next_bank()
            for k in range(8):
                MM(bk(b, 0, 256), U[:, k, tc * 128:(tc + 1) * 128], ws[0][:, k, 0:256], k == 0, k == 7, ws[1] + [R("U", k, tc // 4)], [bres(b)])
            CP(evac_eng(), ZPT.t[:, tc, :], bk(b, 0, 256), [bres(b)], ZPT.res(0, 256, tc))
        for ti, (o, w) in enumerate(tiles):
            for cc in range(2):
                b = next_bank()
                for tl in range(w // 128):
                    tc = o // 128 + tl
                    if is_ctx:
                        base = (tc // 2) * 2
                        srcs = [(base + ti_in, tc % 2, ti_in) for ti_in in range(2)]
                    else:
                        srcs = [(tc, 0, 0)]
                    for gh in range(2):
                        gidx = cc * 2 + gh
                        for si, (tci, to_, ti_) in enumerate(srcs):
                            pm = PM.t[:, gidx * 4 + to_ * 2 + ti_, :] if is_ctx else PM.t[:, gidx, :]
                            MM(bk(b, tl * 128, (tl + 1) * 128, gh * 64, (gh + 1) * 64), ZPT.t[:, tci, gidx * 64:(gidx + 1) * 64], pm,
                               si == 0, si == len(srcs) - 1, ZPT.res(0, 256, tci) + pm_res, [bres(b)], tile_position=(0, gh * 64))
                CP(evac_eng(), PP.t[:, cc, o:o + w], bk(b, 0, w), [bres(b)], PP.res(o, o + w, cc))
        for cc in range(2):
            for ti, (o, w) in enumerate(tiles):
                b = next_bank()
                MM(bk(b, 0, w), PWB[:, l, cc, :], PP.t[:, cc, o:o + w], True, True, PP.res(o, o + w, cc) + [R("PWB")], [bres(b)])
                ACT(Y[:, 6 + cc, o:o + w], bk(b, 0, w), AF.Copy, [bres(b), R("VEC")], [R("Y", 6 + cc, ti)], scale=vcol(("pscale", l), cc))
        tap("ypool_l%d_T%d" % (l, T), Y[:, 6:8, 0:T], [128, 2, T], [R("Y", 6 + c, ti) for c in range(2) for ti in range(len(tiles))])

    def sconv(l, T, seqs):
        HS = ABuf("HS", [128, 2, SEQ], F32, 0)
        CV = ABuf("CV", [128, SEQ], F32, 16384)
        tiles = tiles_of(T)
        ws = load_w(wsrc(win_d[l], 0, 8, 1536 + 512, 256), 8, 256)
        for oc in range(2):
            g = next_group()
            proj(ws, oc * 128, 8, u_rhs, u_res, tiles, g)
            for ti, (o, w) in enumerate(tiles):
                CP(evac_eng(), HS.t[:, oc, o:o + w], bk(g * 4 + ti, 0, w), [bres(g * 4 + ti)], HS.res(o, o + w, oc))
        ws = load_w(wsrc(win_d[l], 0, 8, 1536 + 256, 256), 8, 256)
        for oc in range(2):
            g = next_group()
            proj(ws, oc * 128, 8, u_rhs, u_res, tiles, g)
            for ti, (o, w) in enumerate(tiles):
                TT("dve", HS.t[:, oc, o:o + w], HS.t[:, oc, o:o + w], bk(g * 4 + ti, 0, w), ALU.mult,
                   HS.res(o, o + w, oc) + [bres(g * 4 + ti)], HS.res(o, o + w, oc))
        ws = load_w(wsrc(win_d[l], 0, 8, 1536, 256), 8, 256)
        for oc in range(2):
            for (o, L, _h, _b) in seqs:
                mr = HS.res(o, o + L, oc)
                cr = CV.res(o, o + L)
                TS("dve", CV.t[:, o:o + L], HS.t[:, oc, o:o + L], vcol(("sw", l), 1 * 2 + oc), None, ALU.mult, None, mr + [R("VEC")], cr)
                STT("dve", CV.t[:, o + 1:o + L], HS.t[:, oc, o:o + L - 1], vcol(("sw", l), 0 * 2 + oc), CV.t[:, o + 1:o + L], ALU.mult, ALU.add,
                    mr + cr + [R("VEC")], cr)
                STT("dve", CV.t[:, o:o + L - 1], HS.t[:, oc, o + 1:o + L], vcol(("sw", l), 2 * 2 + oc), CV.t[:, o:o + L - 1], ALU.mult, ALU.add,
                    mr + cr + [R("VEC")], cr)
            g = next_group()
            proj(ws, oc * 128, 8, u_rhs, u_res, tiles, g)
            for ti, (o, w) in enumerate(tiles):
                TT("dve", Y[:, 8 + oc, o:o + w], CV.t[:, o:o + w], bk(g * 4 + ti, 0, w), ALU.mult, CV.res(o, o + w) + [bres(g * 4 + ti)], [R("Y", 8 + oc, ti)])
        tap("ysc_l%d_T%d" % (l, T), Y[:, 8:10, 0:T], [128, 2, T], [R("Y", 8 + c, ti) for c in range(2) for ti in range(len(tiles))])

    def pass_b(l, T, mi, nrm2, head_fn):
        MG = ABuf("MG", [128, 4, SEQ], BF16, 0)
        MACC = ABuf("MACC", [128, SEQ], F32, 16384)
        GT = ABuf("GT", [128, SEQ], BF16, 24576)
        TTB = [ABuf("TTB%d" % i, [128, 512], F32, 28672 + i * 2048) for i in range(2)]
        tiles = tiles_of(T)
        ybase = (0, 4, 6, 8)
        ykc = (4, 2, 2, 2)
        for half in range(2):
            for jj in range(4):
                j = half * 4 + jj
                for i in range(4):
                    wsg = load_w(wsrc(win_d[l], 0, 8, 2304 + i * 1024 + j * 128, 128), 8, 128)
                    wsb = load_w(wsrc(wbr_d[i][l], 0, ykc[i], j * 128, 128), ykc[i], 128)
                    ga = next_group()
                    proj(wsg, 0, 8, u_rhs, u_res, tiles, ga)
                    for ti, (o, w) in enumerate(tiles):
                        ACT(GT.t[:, o:o + w], bk(ga * 4 + ti, 0, w), AF.Sigmoid, [bres(ga * 4 + ti)], GT.res(o, o + w))
                    gb_ = next_group()
                    proj(wsb, 0, ykc[i], (lambda k, o, w, i=i: Y[:, ybase[i] + k, o:o + w]), (lambda k, ti, i=i: [R("Y", ybase[i] + k, ti)]), tiles, gb_)
                    for ti, (o, w) in enumerate(tiles):
                        b = gb_ * 4 + ti
                        if i == 0:
                            TT("dve", MACC.t[:, o:o + w], GT.t[:, o:o + w], bk(b, 0, w), ALU.mult, GT.res(o, o + w) + [bres(b)], MACC.res(o, o + w))
                        else:
                            tt = TTB[ti % 2]
                            TT("dve", tt.t[:, 0:w], GT.t[:, o:o + w], bk(b, 0, w), ALU.mult, GT.res(o, o + w) + [bres(b)], tt.res(0, w))
                            if i < 3:
                                TT("dve", MACC.t[:, o:o + w], MACC.t[:, o:o + w], tt.t[:, 0:w], ALU.add, MACC.res(o, o + w) + tt.res(0, w), MACC.res(o, o + w))
                            else:
                                TT("dve", MG.t[:, jj, o:o + w], MACC.t[:, o:o + w], tt.t[:, 0:w], ALU.add, MACC.res(o, o + w) + tt.res(0, w), MG.res(o, o + w, jj))
            if half == 0:
                tap("mg_l%d_T%d" % (l, T), MG.t[:, :, 0:T], [128, 4, T], [r for jj in range(4) for r in MG.res(0, T, jj)])
            if half == 1:
                wso = [load_w(wsrc(wout_d[l], 512, 4, jt * 256, 256), 4, 256) for jt in range(4)]

                def wo_tile(ti):
                    o, w = tiles[ti]
                    for jo in range(8):
                        ws = wso[jo // 2]
                        b = next_bank()
                        for k in range(4):
                            MM(bk(b, 0, w), ws[0][:, k, (jo % 2) * 128:(jo % 2) * 128 + 128], MG.t[:, k, o:o + w], k == 0, k == 3,
                               ws[1] + MG.res(o, o + w, k), [bres(b)])
                        STT("dve", XR[:, jo, o:o + w], bk(b, 0, w), modv(l, 2, jo, mi), XR[:, jo, o:o + w], ALU.mult, ALU.add,
                            [bres(b), R("XR", jo, ti), R("MODT")], [R("XR", jo, ti)])
                tail_schedule(len(tiles), wo_tile, nrm2, head=head_fn)
                break
            for jt in range(4):
                ws = load_w(wsrc(wout_d[l], half * 512, 4, jt * 256, 256), 4, 256)
                for oc in range(2):
                    jo = jt * 2 + oc
                    g = next_group()
                    proj(ws, oc * 128, 4, (lambda k, o, w: MG.t[:, k, o:o + w]), (lambda k, ti: MG.res(tiles[ti][0], tiles[ti][0] + tiles[ti][1], k)), tiles, g)
                    for ti, (o, w) in enumerate(tiles):
                        STT("dve", XR[:, jo, o:o + w], bk(g * 4 + ti, 0, w), modv(l, 2, jo, mi), XR[:, jo, o:o + w], ALU.mult, ALU.add,
                            [bres(g * 4 + ti), R("XR", jo, ti), R("MODT")], [R("XR", jo, ti)])

    def make_ffn(l, T, mi):
        RT = [ABuf("RT%d" % i, [128, 512], F32, i * 2048) for i in range(4)]
        tiles = tiles_of(T)
        rr = [0]
        head_ws = []

        def relu2(b, w, hc, o, ti):
            rt = RT[rr[0] % 4]
            rr[0] += 1
            ACT(rt.t[:, 0:w], bk(b, 0, w), AF.Relu, [bres(b)], rt.res(0, w))
            TT("dve", Y[:, hc, o:o + w], rt.t[:, 0:w], rt.t[:, 0:w], ALU.mult, rt.res(0, w), [R("Y", hc, ti)])

        def head(ti):
            if not head_ws:
                for jt in range(4):
                    head_ws.append(load_w(wsrc(wff1_d[l], 0, 8, jt * 256, 256), 8, 256))
            o, w = tiles[ti]
            for hc in range(8):
                ws = head_ws[hc // 2]
                b = next_bank()
                for k in range(8):
                    MM(bk(b, 0, w), ws[0][:, k, (hc % 2) * 128:(hc % 2) * 128 + 128], U[:, k, o:o + w], k == 0, k == 7,
                       ws[1] + [R("U", k, ti)], [bres(b)])
                relu2(b, w, hc, o, ti)

        def rest(tail_norm):
            for g4 in range(4):
                if g4 > 0:
                    for jt in range(4):
                        ws = load_w(wsrc(wff1_d[l], 0, 8, g4 * 1024 + jt * 256, 256), 8, 256)
                        for oc in range(2):
                            hc = jt * 2 + oc
                            g = next_group()
                            proj(ws, oc * 128, 8, u_rhs, u_res, tiles, g)
                            for ti, (o, w) in enumerate(tiles):
                                relu2(g * 4 + ti, w, hc, o, ti)
                if g4 < 3:
                    for jt in range(4):
                        ws = load_w(wsrc(wff2_d[l], g4 * 1024, 8, jt * 256, 256), 8, 256)
                        for oc in range(2):
                            jo = jt * 2 + oc
                            g = next_group()
                            proj(ws, oc * 128, 8, (lambda k, o, w: Y[:, k, o:o + w]), (lambda k, ti: [R("Y", k, ti)]), tiles, g)
                            for ti, (o, w) in enumerate(tiles):
                                STT("dve", XR[:, jo, o:o + w], bk(g * 4 + ti, 0, w), modv(l, 5, jo, mi), XR[:, jo, o:o + w], ALU.mult, ALU.add,
                                    [bres(g * 4 + ti), R("XR", jo, ti), R("MODT")], [R("XR", jo, ti)])
                else:
                    wst = [load_w(wsrc(wff2_d[l], g4 * 1024, 8, jt * 256, 256), 8, 256) for jt in range(4)]

                    def ff2_tile(ti):
                        o, w = tiles[ti]
                        for jo in range(8):
                            ws = wst[jo // 2]
                            b = next_bank()
                            for k in range(8):
                                MM(bk(b, 0, w), ws[0][:, k, (jo % 2) * 128:(jo % 2) * 128 + 128], Y[:, k, o:o + w], k == 0, k == 7,
                                   ws[1] + [R("Y", k, ti)], [bres(b)])
                            STT("dve", XR[:, jo, o:o + w], bk(b, 0, w), modv(l, 5, jo, mi), XR[:, jo, o:o + w], ALU.mult, ALU.add,
                                [bres(b), R("XR", jo, ti), R("MODT")], [R("XR", jo, ti)])
                    tail_schedule(len(tiles), ff2_tile, tail_norm)
        return head, rest

    def store_tokens(dst2d, T):
        OST = [ABuf("OST%d" % i, [128, 1024], F32, i * 4096) for i in range(4)]
        rr = 0
        for ti, (o, w) in enumerate(tiles_of(T)):
            for tc in range(w // 128):
                ost = OST[rr % 4]
                rr += 1
                for kh in range(2):
                    b = next_bank()
                    for kk in range(4):
                        k = kh * 4 + kk
                        TR(bk(b, kk * 128, (kk + 1) * 128), XR[:, k, o + tc * 128:o + (tc + 1) * 128], [R("XR", k, ti)], [bres(b)])
                    CP(evac_eng(), ost.t[:, kh * 512:(kh + 1) * 512], bk(b, 0, 512), [bres(b)], ost.res(kh * 512, (kh + 1) * 512))
                uniq[0] += 1
                op = DMA("sp", dst2d[o + tc * 128:o + (tc + 1) * 128, :], ost.t[:], ost.res(0, 1024), [R("outd", uniq[0])], ost.res(0, 1024)[0])
                final_ops.append(op)

    def mk_norm(T, l, mi, which):
        if which == "final":
            return Norm(T, (lambda k: G32[:, DEPTH * 16 + k:DEPTH * 16 + k + 1]), None, out_inplace=True)
        m_sh = 0 if which == 0 else 3
        return Norm(T, (lambda k: GS[:, l, mi, which, k:k + 1]), (lambda k: modv(l, m_sh, k, mi)))

    def layer(l, T, seqs, mi, is_ctx, state_only, next_norm):
        nt = len(tiles_of(T))

        def tapx(nm):
            tap("%s_l%d_T%d" % (nm, l, T), XR[:, :, 0:T], [128, 8, T], [R("XR", k, ti) for k in range(8) for ti in range(nt)])
        tap("u1_l%d_T%d" % (l, T), U[:, :, 0:T], [128, 8, T], [R("U", k, ti) for k in range(8) for ti in range(nt)])
        lru(l, T, seqs, save_state=state_only)
        if state_only:
            return
        fourier(l, T, seqs, is_ctx)
        pool_mix(l, T, seqs, is_ctx)
        sconv(l, T, seqs)
        ffn_head, ffn_rest = make_ffn(l, T, mi)
        pass_b(l, T, mi, mk_norm(T, l, mi, 1), ffn_head)
        tapx("xmid")
        ffn_rest(next_norm)
        tapx("xout")

    setup()
    tap("modt", MODT[:], [128, DEPTH, 48, NM], [R("MODT")])
    TC = NB * CTX
    load_tokens(ctx_d.rearrange("b t d -> (b t) d"), TC)
    mk_norm(TC, 0, NB, 0).run_all()
    for l in range(DEPTH):
        seqs = [(b * CTX, CTX, (lambda d, c: 0.0), b) for b in range(NB)]
        last = (l == DEPTH - 1)
        layer(l, TC, seqs, NB, True, last, None if last else mk_norm(TC, l + 1, NB, 0))
    tap("hctx", HCTX[:], [128, DEPTH, NB, 2, 4], [R("HCTX")])
    for b in range(NB):
        load_tokens(x_d[b], SEQ)
        mk_norm(SEQ, 0, b, 0).run_all()
        for l in range(DEPTH):
            seqs = [(0, SEQ, (lambda d, c, l=l, b=b: HCTX[:, l, b, d, c:c + 1]), None)]
            nxt = mk_norm(SEQ, l + 1, b, 0) if l + 1 < DEPTH else mk_norm(SEQ, 0, b, "final")
            layer(l, SEQ, seqs, b, False, False, nxt)
        store_tokens(out_d[b], SEQ)
    P.emit(final_wait_ops=final_ops)
    return nc, tap_d


def make_in_maps(inputs, n_cores, NB, DEPTH):
    cst = _consts()
    vecs, _ = pack_vecs(inputs, DEPTH)
    maps = []
    for ci in range(n_cores):
        bs = slice(ci * NB, (ci + 1) * NB)
        call = np.concatenate([np.asarray(inputs["c"][bs], np.float32), np.asarray(inputs["c_ctx"], np.float32)[None, :]], axis=0)
        NM = NB + 1
        ct = np.ascontiguousarray(call.reshape(NM, 8, 128).transpose(2, 1, 0).reshape(128, 8 * NM))
        m = {
            "x": np.ascontiguousarray(inputs["x"][bs], dtype=np.float32),
            "ctx": np.ascontiguousarray(inputs["ctx"][bs], dtype=np.float32),
            "cT": ct,
            "vecs": vecs,
        }
        for name in ("w_mod", "w_in", "lru_w_a", "lru_w_x", "pool_w", "w_br_lru", "w_br_fourier", "w_br_pool", "w_br_sconv", "w_out", "w_ff1", "w_ff2"):
            m[name] = np.ascontiguousarray(np.asarray(inputs[name], np.float32)[:DEPTH])
        for name in ("dft_c", "dft_l", "dft_lc", "pm_lat", "pm_ctx"):
            m[name] = cst[name]
        maps.append(m)
    return maps


def kernel(**inputs):
    n_cores, NB, DEPTH = 8, 4, 2
    inputs = {k: np.asarray(v) for k, v in inputs.items()}
    nc, _ = build_program(Cfg(NB=NB, DEPTH=DEPTH))
    maps = make_in_maps(inputs, n_cores, NB, DEPTH)
    res = run_bass_kernel_spmd(nc, maps, core_ids=list(range(n_cores)))
    out = np.concatenate([np.asarray(r["out"], dtype=np.float32) for r in res.results], axis=0)
    return out
```

```python
import math
from contextlib import ExitStack
import numpy as np
import ml_dtypes
import concourse.bass as bass
import concourse.mybir as mybir
from concourse.ap import AP
from concourse.bass_utils import run_bass_kernel_spmd

F32 = mybir.dt.float32
BF16 = mybir.dt.bfloat16
AF = mybir.ActivationFunctionType
ALU = mybir.AluOpType

D = 1024
SEQ = 2048
CTX = 256
NIN = 6400
DFF = 4096
EPS = 1e-6
ENGS = ("pe", "dve", "act", "pool", "sp")
SB_BASE = 16512
SB_END = 229376


class Res:
    __slots__ = ("name", "last_w", "readers", "dma_sem", "dma_cnt")

    def __init__(self, name, init_w=None):
        self.name = name
        self.last_w = init_w
        self.readers = []
        self.dma_sem = None
        self.dma_cnt = 0


class Op:
    __slots__ = ("eng", "fn", "deps", "is_dma", "sem_res", "dma_val", "sig", "sig_idx")

    def __init__(self, eng, fn, is_dma):
        self.eng = eng
        self.fn = fn
        self.deps = []
        self.is_dma = is_dma
        self.sem_res = None
        self.dma_val = 0
        self.sig = False
        self.sig_idx = 0


class Prog:
    def __init__(self, nc):
        self.nc = nc
        self.ops = []
        self.per_eng = {e: [] for e in ENGS}
        self.dma_res = []
        self.res = {}

    def R(self, *key):
        r = self.res.get(key)
        if r is None:
            r = Res("_".join(str(k) for k in key))
            self.res[key] = r
        return r

    def op(self, eng, meth, kw, reads=(), writes=(), dma_res=None):
        o = Op(eng, (meth, kw), dma_res is not None)
        seen = set()
        deps = o.deps
        for r in reads:
            d = r.last_w
            if d is not None and id(d) not in seen:
                seen.add(id(d))
                deps.append(d)
        for w in writes:
            d = w.last_w
            if d is not None and id(d) not in seen:
                seen.add(id(d))
                deps.append(d)
            for d in w.readers:
                if id(d) not in seen:
                    seen.add(id(d))
                    deps.append(d)
        for r in reads:
            r.readers.append(o)
        for w in writes:
            w.last_w = o
            w.readers = []
        if dma_res is not None:
            o.sem_res = dma_res
            dma_res.dma_cnt += 1
            o.dma_val = 16 * dma_res.dma_cnt
            if dma_res.dma_sem is None:
                dma_res.dma_sem = True
                self.dma_res.append(dma_res)
        self.per_eng[eng].append(o)
        self.ops.append(o)
        return o

    def emit(self, final_wait_ops=()):
        nc = self.nc
        for o in self.ops:
            for d in o.deps:
                if not d.is_dma and not (d.eng == "pe" and o.eng == "pe"):
                    d.sig = True
        for e in ENGS:
            c = 0
            for o in self.per_eng[e]:
                if o.sig and not o.is_dma:
                    c += 1
                    o.sig_idx = c
        with ExitStack() as st:
            esem = {e: st.enter_context(nc.semaphore("s_" + e)) for e in ENGS}
            for i, r in enumerate(self.dma_res):
                r.dma_sem = st.enter_context(nc.semaphore("d%d" % i))
            block = st.enter_context(nc.Block())

            def run_engine(e, engobj):
                waited = {}
                for o in self.per_eng[e]:
                    for d in o.deps:
                        if d.is_dma:
                            key = id(d.sem_res)
                            sem = d.sem_res.dma_sem
                            val = d.dma_val
                        else:
                            if d.eng == "pe" and e == "pe":
                                continue
                            key = d.eng
                            sem = esem[d.eng]
                            val = d.sig_idx
                        if waited.get(key, 0) >= val:
                            continue
                        waited[key] = val
                        engobj.wait_ge(sem, val)
                    ins = getattr(engobj, o.fn[0])(**o.fn[1])
                    if o.is_dma:
                        ins.then_inc(o.sem_res.dma_sem, 16)
                    elif o.sig:
                        ins.then_inc(esem[e], 1)
                if e == "sp":
                    for o in final_wait_ops:
                        engobj.wait_ge(o.sem_res.dma_sem, o.dma_val)

            block.tensor(lambda eng: run_engine("pe", eng))
            block.vector(lambda eng: run_engine("dve", eng))
            block.scalar(lambda eng: run_engine("act", eng))
            block.gpsimd(lambda eng: run_engine("pool", eng))
            block.sync(lambda eng: run_engine("sp", eng))


def _dft_tables(n):
    idx = np.arange(n, dtype=np.int64)
    m = (idx[:, None] * idx[None, :]) % n
    ang = 2.0 * np.pi * m.astype(np.float64) / n
    return np.cos(ang), np.sin(ang)


def _pool_matrix(w, dom):
    P = np.zeros((dom, dom), np.float64)
    for t in range(dom):
        lo = max(t - w // 2, 0)
        hi = min(t + w // 2, dom)
        P[lo:hi, t] += 1.0 / (hi - lo)
        P[t, t] -= 1.0
    return P


_CONST_CACHE = {}


def _consts():
    if _CONST_CACHE:
        return _CONST_CACHE
    bf = ml_dtypes.bfloat16
    cc, sc = _dft_tables(256)
    _CONST_CACHE["dft_c"] = np.concatenate([cc, sc], axis=1).astype(bf)
    cl, sl = _dft_tables(SEQ)
    _CONST_CACHE["dft_l"] = np.stack([cl, -sl]).astype(bf)
    cl, sl = _dft_tables(CTX)
    _CONST_CACHE["dft_lc"] = np.stack([cl, -sl]).astype(bf)
    pl = np.zeros((4, 128, 128), np.float64)
    pc = np.zeros((4, 2, 2, 128, 128), np.float64)
    for g, w in enumerate((2, 4, 8, 16)):
        p64 = _pool_matrix(w, 64)
        pl[g, :64, :64] = p64
        pl[g, 64:, 64:] = p64
        p256 = _pool_matrix(w, 256)
        for to in range(2):
            for ti in range(2):
                pc[g, to, ti] = p256[ti * 128:(ti + 1) * 128, to * 128:(to + 1) * 128]
    _CONST_CACHE["pm_lat"] = pl.astype(bf)
    _CONST_CACHE["pm_ctx"] = pc.astype(bf)
    return _CONST_CACHE


class VecLayout:
    def __init__(self, depth):
        self.col = {}
        n = 0

        def add(name, cnt):
            nonlocal n
            self.col[name] = n
            n += cnt
        for l in range(depth):
            add(("g1", l), 8)
            add(("g2", l), 8)
            add(("bmod", l), 48)
            add(("cw", l), 16)
            add(("cb", l), 4)
            add(("ba", l), 8)
            add(("bx", l), 8)
            add(("lam", l), 8)
            add(("pscale", l), 2)
            add(("sw", l), 6)
        add(("gf",), 8)
        self.n = n


def pack_vecs(inp, depth):
    vl = VecLayout(depth)
    v = np.zeros((128, vl.n), np.float32)

    def put(key, arr, cnt):
        a = np.asarray(arr, np.float32).reshape(cnt, 128).T
        v[:, vl.col[key]:vl.col[key] + cnt] = a
    for l in range(depth):
        put(("g1", l), inp["g_norm1"][l], 8)
        put(("g2", l), inp["g_norm2"][l], 8)
        put(("bmod", l), inp["b_mod"][l], 48)
        put(("cw", l), inp["lru_conv_w"][l], 16)
        put(("cb", l), inp["lru_conv_b"][l], 4)
        put(("ba", l), inp["lru_b_a"][l], 8)
        put(("bx", l), inp["lru_b_x"][l], 8)
        put(("lam", l), inp["lru_lam"][l], 8)
        put(("pscale", l), inp["pool_scale"][l], 2)
        put(("sw", l), inp["sconv_w"][l], 6)
    put(("gf",), inp["g_final"], 8)
    return v, vl


class Cfg:
    def __init__(self, NB=4, DEPTH=2, taps=()):
        self.NB = NB
        self.DEPTH = DEPTH
        self.taps = tuple(taps)


def build_program(cfg):
    NB, DEPTH = cfg.NB, cfg.DEPTH
    NM = NB + 1
    nc = bass.Bass("TRN2", target_bir_lowering=False)
    vl = VecLayout(DEPTH)

    def din(name, shape, dt=F32):
        return nc.dram_tensor(name, list(shape), dt, kind="ExternalInput").ap()

    x_d = din("x", [NB, SEQ, D])
    ctx_d = din("ctx", [NB, CTX, D])
    ct_d = din("cT", [128, 8 * NM])
    vecs_d = din("vecs", [128, vl.n])
    wmod_d = din("w_mod", [DEPTH, D, 6 * D])
    win_d = din("w_in", [DEPTH, D, NIN])
    wa_d = din("lru_w_a", [DEPTH, 2, 8, 64, 64])
    wx_d = din("lru_w_x", [DEPTH, 2, 8, 64, 64])
    pw_d = din("pool_w", [DEPTH, 4, 64, 64])
    wbr_d = [din("w_br_lru", [DEPTH, 512, D]), din("w_br_fourier", [DEPTH, 256, D]),
             din("w_br_pool", [DEPTH, 256, D]), din("w_br_sconv", [DEPTH, 256, D])]
    wout_d = din("w_out", [DEPTH, D, D])
    wff1_d = din("w_ff1", [DEPTH, D, DFF])
    wff2_d = din("w_ff2", [DEPTH, DFF, D])
    dftc_d = din("dft_c", [256, 512], BF16)
    dftl_d = din("dft_l", [2, SEQ, SEQ], BF16)
    dftlc_d = din("dft_lc", [2, CTX, CTX], BF16)
    pml_d = din("pm_lat", [4, 128, 128], BF16)
    pmc_d = din("pm_ctx", [4, 2, 2, 128, 128], BF16)
    out_d = nc.dram_tensor("out", [NB, SEQ, D], F32, kind="ExternalOutput").ap()
    tap_d = {}
    final_ops = []

    P = Prog(nc)
    R = P.R

    def MM(out, lhsT, rhs, start, stop, reads, writes, **kw):
        return P.op("pe", "matmul", dict(out=out, lhsT=lhsT, rhs=rhs, start=start, stop=stop, **kw), reads, writes)

    def TR(out, in_, reads, writes):
        return P.op("pe", "transpose", dict(out=out, in_=in_, identity=IDENT[:]), list(reads) + [R("IDENT")], writes)

    def ACT(out, in_, func, reads, writes, bias=None, scale=None):
        kw = dict(out=out, in_=in_, func=func)
        if bias is not None:
            kw["bias"] = bias
        if scale is not None:
            kw["scale"] = scale
        return P.op("act", "activation", kw, reads, writes)

    def CP(eng, out, in_, reads, writes):
        if eng == "act":
            return ACT(out, in_, AF.Copy, reads, writes)
        return P.op(eng, "tensor_copy", dict(out=out, in_=in_), reads, writes)

    def TT(eng, out, in0, in1, op, reads, writes):
        return P.op(eng, "tensor_tensor", dict(out=out, in0=in0, in1=in1, op=op), reads, writes)

    def TS(eng, out, in0, s1, s2, op0, op1, reads, writes):
        kw = dict(out=out, in0=in0, scalar1=s1, scalar2=s2, op0=op0)
        if op1 is not None:
            kw["op1"] = op1
        return P.op(eng, "tensor_scalar", kw, reads, writes)

    def STT(eng, out, in0, scalar, in1, op0, op1, reads, writes):
        return P.op(eng, "scalar_tensor_tensor", dict(out=out, in0=in0, scalar=scalar, in1=in1, op0=op0, op1=op1), reads, writes)

    def SCAN(out, a, b, init, reads, writes):
        return P.op("dve", "tensor_tensor_scan", dict(out=out, data0=a, data1=b, initial=init, op0=ALU.mult, op1=ALU.add), reads, writes)

    def DMA(eng, out, in_, reads, writes, dma_res):
        return P.op(eng, "dma_start", dict(out=out, in_=in_), reads, writes, dma_res=dma_res)

    def MEMSET(eng, ap, val, writes):
        return P.op(eng, "memset", dict(ap=ap, constant=val), (), writes)

    sb_off = [SB_BASE]

    def sb(name, shape, dt, off=None):
        nbytes = int(np.prod(shape[1:])) * (4 if dt == F32 else 2)
        if off is None:
            off = sb_off[0]
            sb_off[0] += (nbytes + 31) // 32 * 32
            assert sb_off[0] <= SB_END, ("SBUF overflow", name, sb_off[0])
        return nc.alloc_sbuf_tensor_at(name, list(shape), dt, offset=off)

    XR = sb("XR", [128, 8, SEQ], F32)
    U = sb("U", [128, 8, SEQ], BF16)
    y_base = sb_off[0]
    Y = sb("Y", [128, 10, SEQ], BF16)
    YT = [sb("YT%d" % i, [128, SEQ], F32, off=y_base + (4 + 2 * i) * SEQ * 2) for i in range(3)]
    wr_base = sb_off[0]
    WR = [sb("WR%d" % i, [128, 8, 256], BF16) for i in range(4)]
    WRH = [sb("WRH%d" % i, [128, 8, 128], BF16, off=wr_base + i * 2048) for i in range(8)]
    WRQ = [sb("WRQ%d" % i, [128, 4, 256], BF16, off=wr_base + i * 2048) for i in range(8)]
    IDENT = sb("IDENT", [128, 128], F32)
    ONES = sb("ONES", [128, 128], BF16)
    VEC = sb("VEC", [128, vl.n], F32)
    CT = sb("CT", [128, 8, NM], F32)
    MODT = sb("MODT", [128, DEPTH, 48, NM], F32)
    GS = sb("GS", [128, DEPTH, NM, 2, 8], F32)
    G32 = sb("G32", [128, DEPTH * 16 + 8], F32)
    NSP = sb("NSP", [128, DEPTH, 2, 8], F32)
    NBIAS = sb("NBIAS", [128, 4], F32)
    HB = sb("HB", [128, DEPTH, 2, 8], F32)
    WG = sb("WG", [128, DEPTH, 2, 2, 4, 128], BF16)
    PWB = sb("PWB", [128, DEPTH, 2, 128], BF16)
    DFTC = sb("DFTC", [128, 2, 512], BF16)
    HCTX = sb("HCTX", [128, DEPTH, NB, 2, 4], F32)
    ARENA_BYTES = 36864
    arena0 = sb_off[0]
    sb_off[0] += ARENA_BYTES
    assert sb_off[0] <= SB_END, ("SBUF overflow", sb_off[0])
    PS = nc.alloc_psum_tensor("PS", [128, 4096], F32)

    abuf_cache = {}

    class ABuf:
        def __init__(self, name, shape, dt, off):
            self.esz = 4 if dt == F32 else 2
            self.off = off
            self.W = shape[-1]
            nbytes = int(np.prod(shape[1:])) * self.esz
            assert off + nbytes <= ARENA_BYTES, name
            key = (name, tuple(shape), str(dt), off)
            if key not in abuf_cache:
                abuf_cache[key] = sb(name + "_%d" % len(abuf_cache), shape, dt, off=arena0 + off)
            self.t = abuf_cache[key]

        def res(self, c0, c1, i=0):
            b0 = (self.off + (i * self.W + c0) * self.esz) // 2048
            b1 = (self.off + (i * self.W + c1) * self.esz - 1) // 2048
            return [R("ar", b) for b in range(b0, b1 + 1)]

    def bk(b, c0, c1, p0=0, p1=128):
        return PS[p0:p1, b * 512 + c0:b * 512 + c1]

    def bres(i):
        return R("bank", i)

    bank_rr = [0]

    def next_bank():
        b = bank_rr[0]
        bank_rr[0] = (b + 1) % 8
        return b

    grp_rr = [0]

    def next_group():
        g = grp_rr[0]
        grp_rr[0] = 1 - g
        return g

    wr_rr = [0]
    uniq = [0]

    def vcol(key, j=0):
        c = vl.col[key] + j
        return VEC[:, c:c + 1]

    ev_rr = [0]

    def evac_eng():
        ev_rr[0] ^= 1
        return "act" if ev_rr[0] else "dve"

    def load_w(src_ap, kc, ncols):
        h = wr_rr[0]
        if ncols > 128 and kc <= 4:
            wr_rr[0] = (h + 1) % 8
            t, res = WRQ[h], [R("wrh", h)]
        elif ncols > 128:
            if h % 2:
                h = (h + 1) % 8
            wr_rr[0] = (h + 2) % 8
            t, res = WR[h // 2], [R("wrh", h), R("wrh", h + 1)]
        else:
            wr_rr[0] = (h + 1) % 8
            t, res = WRH[h], [R("wrh", h)]
        DMA("pool", t[:, 0:kc, 0:ncols], src_ap, (), res, res[0])
        return (t, res)

    def wsrc(w2d, r0, kc, c0, ncols):
        return w2d[r0:r0 + kc * 128, c0:c0 + ncols].rearrange("(k p) c -> p k c", p=128)

    def tap(name, src_ap, shape, reads):
        if name not in cfg.taps:
            return
        t = nc.dram_tensor("tap_" + name, list(shape), src_ap.dtype, kind="ExternalOutput").ap()
        tap_d[name] = t
        o = DMA("sp", t, src_ap, reads, [R("tap", name)], R("tap", name))
        final_ops.append(o)

    def tiles_of(T):
        return [(o, min(512, T - o)) for o in range(0, T, 512)]

    def setup():
        DMA("sp", VEC[:], vecs_d, (), [R("VEC")], R("VEC"))
        DMA("sp", CT[:], ct_d.rearrange("p (k b) -> p k b", b=NM), (), [R("CT")], R("CT"))
        DMA("sp", DFTC[:], dftc_d.rearrange("(k p) c -> p k c", p=128), (), [R("DFTC")], R("DFTC"))
        MEMSET("pool", IDENT[:], 1.0, [R("IDENT")])
        P.op("pool", "affine_select", dict(out=IDENT[:], in_=IDENT[:], pattern=[[-1, 128]], compare_op=ALU.is_equal,
                                           fill=0.0, base=0, channel_multiplier=1), [R("IDENT")], [R("IDENT")])
        MEMSET("dve", ONES[:], 1.0, [R("ONES")])
        MEMSET("dve", NBIAS[:, 0:1], 1.0, [R("NBIAS")])
        MEMSET("dve", NBIAS[:, 1:2], 1024.0 * EPS, [R("NBIAS")])
        MEMSET("dve", WG[:], 0.0, [R("WG")])
        MEMSET("dve", PWB[:], 0.0, [R("PWB")])
        for l in range(DEPTH):
            for d in range(2):
                for gi, wd in enumerate((wa_d, wx_d)):
                    for half in range(2):
                        src = wd[l, d].rearrange("(c two) i j -> two i c j", two=2)[half]
                        dst = WG[half * 64:(half + 1) * 64, l, d, gi, :, half * 64:(half + 1) * 64]
                        DMA("pool", dst, src, (), [R("WG")], R("WG"))
            for half in range(2):
                src = pw_d[l].rearrange("(c two) i j -> two i c j", two=2)[half]
                dst = PWB[half * 64:(half + 1) * 64, l, :, half * 64:(half + 1) * 64]
                DMA("pool", dst, src, (), [R("PWB")], R("PWB"))
        for l in range(DEPTH):
            c0 = vl.col[("g1", l)]
            TS("dve", G32[:, l * 16:l * 16 + 16], VEC[:, c0:c0 + 16], 32.0, None, ALU.mult, None, [R("VEC")], [R("G32")])
        c0 = vl.col[("gf",)]
        TS("dve", G32[:, DEPTH * 16:DEPTH * 16 + 8], VEC[:, c0:c0 + 8], 32.0, None, ALU.mult, None, [R("VEC")], [R("G32")])
        for l in range(DEPTH):
            c0 = vl.col[("lam", l)]
            ACT(NSP[:, l, 0, :], VEC[:, c0:c0 + 8], AF.Exp, [R("VEC")], [R("NSP")], scale=-1.0)
            ACT(NSP[:, l, 0, :], NSP[:, l, 0, :], AF.Ln, [R("NSP"), R("NBIAS")], [R("NSP")], bias=NBIAS[:, 0:1], scale=1.0)
            TS("dve", NSP[:, l, 1, :], NSP[:, l, 0, :], -8.0, None, ALU.mult, None, [R("NSP")], [R("NSP")])
            TS("dve", NSP[:, l, 0, :], NSP[:, l, 0, :], -4.0, None, ALU.mult, None, [R("NSP")], [R("NSP")])
            cba = vl.col[("ba", l)]
            TS("dve", HB[:, l, 0, :], VEC[:, cba:cba + 8], 0.5, None, ALU.mult, None, [R("VEC")], [R("HB")])
            cbx = vl.col[("bx", l)]
            TS("dve", HB[:, l, 1, :], VEC[:, cbx:cbx + 8], 0.5, None, ALU.mult, None, [R("VEC")], [R("HB")])
        ACT(CT[:], CT[:], AF.Silu, [R("CT")], [R("CT")])
        WM = [ABuf("WM%d" % i, [128, 8, 256], F32, i * 8192) for i in range(3)]
        wm_rr = 0
        for l in range(DEPTH):
            pb = next_bank()
            for jt in range(24):
                wm = WM[wm_rr % 3]
                wm_rr += 1
                DMA("sp", wm.t[:], wsrc(wmod_d[l], 0, 8, jt * 256, 256), (), wm.res(0, 2048), wm.res(0, 2048)[0])
                for oc in range(2):
                    mk = jt * 2 + oc
                    for k in range(8):
                        MM(bk(pb, mk * 8, mk * 8 + NM), wm.t[:, k, oc * 128:(oc + 1) * 128], CT[:, k, :], k == 0, k == 7,
                           wm.res(0, 2048) + [R("CT")], [bres(pb)])
            c0 = vl.col[("bmod", l)]
            full = bk(pb, 0, 384)
            src = AP(full.tensor, full.offset, [list(full.ap[0]), [8, 48], [1, NM]])
            bm = VEC[:, c0:c0 + 48]
            bmb = AP(bm.tensor, bm.offset, [list(bm.ap[0]), [1, 48], [0, NM]])
            TT("dve", MODT[:, l, :, :], src, bmb, ALU.add, [bres(pb), R("VEC")], [R("MODT")])
            for b in range(NM):
                for w in range(2):
                    m = 1 if w == 0 else 4
                    STT("dve", GS[:, l, b, w, :], MODT[:, l, m * 8:(m + 1) * 8, b], 1.0, G32[:, l * 16 + w * 8:l * 16 + w * 8 + 8], ALU.add, ALU.mult,
                        [R("MODT"), R("G32")], [R("GS")])

    def modv(l, m, k, b):
        return MODT[:, l, m * 8 + k, b:b + 1]

    def proj(ws, oc_col, kc, rhs_fn, rhs_res_fn, tiles, grp):
        for k in range(kc):
            for ti, (o, w) in enumerate(tiles):
                b = grp * 4 + ti
                MM(bk(b, 0, w), ws[0][:, k, oc_col:oc_col + 128], rhs_fn(k, o, w), k == 0, k == kc - 1,
                   ws[1] + rhs_res_fn(k, ti), [bres(b)])

    def u_rhs(k, o, w):
        return U[:, k, o:o + w]

    def u_res(k, ti):
        return [R("U", k, ti)]

    def make_loader(src2d, T):
        STG = [ABuf("STG%d" % i, [128, 1024], F32, i * 4096) for i in range(4)]
        tl = tiles_of(T)

        def load_tile(ti):
            o, w = tl[ti]
            ntc = w // 128
            st = []
            for tc in range(ntc):
                sg = STG[tc % 4]
                DMA("sp", sg.t[:], src2d[o + tc * 128:o + (tc + 1) * 128, :], (), sg.res(0, 1024), sg.res(0, 1024)[0])
                st.append(sg)
            for k in range(8):
                b = next_bank()
                for tc in range(ntc):
                    TR(bk(b, tc * 128, (tc + 1) * 128), st[tc].t[:, k * 128:(k + 1) * 128], st[tc].res(0, 1024), [bres(b)])
                CP(evac_eng(), XR[:, k, o:o + w], bk(b, 0, w), [bres(b)], [R("XR", k, ti)])
        return load_tile

    class Norm:
        def __init__(self, T, gs_fn, sh_fn, out_inplace=False):
            self.tl = tiles_of(T)
            self.gs_fn, self.sh_fn, self.inplace = gs_fn, sh_fn, out_inplace
            self.SQ = ABuf("SQ", [128, 8, 512], BF16, 16384)
            self.RS = [ABuf("RS%d" % i, [128, 512], F32, 24576 + i * 2048) for i in range(2)]
            self.TM = [ABuf("TM%d" % i, [128, 512], F32, 28672 + i * 2048) for i in range(2)]

        def a(self, ti):
            o, w = self.tl[ti]
            SQ = self.SQ
            b = next_bank()
            for k in range(8):
                ACT(SQ.t[:, k, 0:w], XR[:, k, o:o + w], AF.Square, [R("XR", k, ti)], SQ.res(0, w, k))
                MM(bk(b, 0, w), ONES[:], SQ.t[:, k, 0:w], k == 0, k == 7, SQ.res(0, w, k) + [R("ONES")], [bres(b)])
            rs = self.RS[ti % 2]
            ACT(rs.t[:, 0:w], bk(b, 0, w), AF.Sqrt, [bres(b), R("NBIAS")], rs.res(0, w), bias=NBIAS[:, 1:2], scale=1.0)
            P.op("dve", "reciprocal", dict(out=rs.t[:, 0:w], in_=rs.t[:, 0:w]), rs.res(0, w), rs.res(0, w))

        def b(self, ti):
            o, w = self.tl[ti]
            rs = self.RS[ti % 2]
            for k in range(8):
                if self.inplace:
                    STT("dve", XR[:, k, o:o + w], XR[:, k, o:o + w], self.gs_fn(k), rs.t[:, 0:w], ALU.mult, ALU.mult,
                        [R("XR", k, ti), R("GS"), R("G32")] + rs.res(0, w), [R("XR", k, ti)])
                else:
                    tm = self.TM[k % 2]
                    STT("dve", tm.t[:, 0:w], XR[:, k, o:o + w], self.gs_fn(k), rs.t[:, 0:w], ALU.mult, ALU.mult,
                        [R("XR", k, ti), R("GS")] + rs.res(0, w), tm.res(0, w))
                    ACT(U[:, k, o:o + w], tm.t[:, 0:w], AF.Identity, tm.res(0, w) + [R("MODT")], [R("U", k, ti)], bias=self.sh_fn(k), scale=1.0)

        def run_all(self):
            for ti in range(len(self.tl)):
                self.a(ti)
                self.b(ti)

    def tail_schedule(nt, work, nrm, head=None):
        seq = [("w", 0)]
        for ti in range(1, nt):
            seq.append(("w", ti))
            seq.append(("n", ti - 1))
            if head is not None and ti - 1 >= 2:
                seq.append(("h", ti - 3))
        seq.append(("n", nt - 1))
        done_h = max(0, nt - 3) if head is not None else 0
        if head is not None:
            if nt - 1 >= 2:
                seq.append(("h", nt - 3))
                done_h = nt - 2
            for ti in range(done_h, nt):
                seq.append(("h", ti))
        for kind, ti in seq:
            if kind == "w":
                work(ti)
            elif kind == "n":
                nrm.a(ti)
                nrm.b(ti)
            else:
                head(ti)

    def lru(l, T, seqs, save_state):
        XA = ABuf("XA", [128, SEQ], F32, 0)
        XAB = ABuf("XAB", [128, SEQ], BF16, 8192)
        AT = [ABuf("LA%d" % i, [128, SEQ], F32, 12288 + i * 8192) for i in range(3)]
        tiles = tiles_of(T)
        nt = len(tiles)
        full = not save_state

        class TB:
            pass
        temps = []
        for i in range(3):
            tb = TB()
            tb.t = AT[i].t
            tb.r = AT[i].res(0, T)
            temps.append(tb)
        for i in range(3):
            tb = TB()
            tb.t = YT[i]
            tb.r = [R("Y", 4 + 2 * i + h, ti) for h in range(2) for ti in range(4)]
            temps.append(tb)

        def rev(t, a, n):
            x = t[:, a:a + n]
            return AP(x.tensor, x.offset + n - 1, [list(x.ap[0]), [-1, n]])

        def grp_ap(g, a, n):
            return PS[:, g * 2048 + a:g * 2048 + a + n]

        def grp_res(g):
            return [bres(g * 4 + ti) for ti in range(nt)]
        wtiles = {}

        def get_w(kind, cp):
            key = (kind, cp)
            if key not in wtiles:
                col = (0 if kind == "x" else 512) + cp * 256
                wtiles[key] = load_w(wsrc(win_d[l], 0, 8, col, 256), 8, 256)
            return wtiles[key]
        xr = XA.res(0, T)
        for cp_ in range(2):
            get_w("x", cp_)
            if full:
                get_w("g", cp_)

        def prelude_x(c):
            cp, oc = c // 2, c % 2
            g = next_group()
            proj(get_w("x", cp), oc * 128, 8, u_rhs, u_res, tiles, g)
            for (o, L, h0f, bi) in seqs:
                ACT(XA.t[:, o:o + L], grp_ap(g, o, L), AF.Identity, grp_res(g) + [R("VEC")], xr, bias=vcol(("cb", l), c), scale=vcol(("cw", l), 2 * 4 + c))
                for j, sh in ((0, -2), (1, -1), (3, 1)):
                    if sh < 0:
                        oo, io, n = o - sh, o, L + sh
                    else:
                        oo, io, n = o, o + sh, L - sh
                    STT("dve", XA.t[:, oo:oo + n], grp_ap(g, io, n), vcol(("cw", l), j * 4 + c), XA.t[:, oo:oo + n], ALU.mult, ALU.add,
                        grp_res(g) + xr + [R("VEC")], xr)
            CP("act", XAB.t[:, 0:T], XA.t[:, 0:T], xr, XAB.res(0, T))
            if c == 0:
                tap("xa0_l%d_T%d" % (l, T), XA.t[:, 0:T], [128, T], xr)

        def prelude_g(c):
            if not full:
                return
            cp, oc = c // 2, c % 2
            g = next_group()
            proj(get_w("g", cp), oc * 128, 8, u_rhs, u_res, tiles, g)
            tb = temps[0]
            pz, pr = grp_ap(g, 0, T), grp_res(g)
            tw_ = tb.t[:, 0:T]
            yres = [R("Y", c, ti) for ti in range(nt)]
            ACT(tw_, pz, AF.Square, pr, tb.r, scale=0.21145921592590665)
            STT("dve", tw_, tw_, 1.0, pz, ALU.add, ALU.mult, tb.r + pr, tb.r)
            ACT(tw_, tw_, AF.Tanh, tb.r, tb.r, scale=0.7978845608028654)
            STT("dve", Y[:, c, 0:T], tw_, 1.0, pz, ALU.add, ALU.mult, tb.r + pr, yres)

        def gate_act(c, d):
            tR, tI, tS = temps[3 * d], temps[3 * d + 1], temps[3 * d + 2]
            ga, gb_ = next_group(), next_group()
            for gi, gg in ((0, ga), (1, gb_)):
                for ti, (o, w) in enumerate(tiles):
                    MM(bk(gg * 4 + ti, 0, w), WG[:, l, d, gi, c, :], XAB.t[:, o:o + w], True, True, [R("WG")] + XAB.res(0, T), [bres(gg * 4 + ti)])
            dc = d * 4 + c
            ACT(tI.t[:, 0:T], grp_ap(gb_, 0, T), AF.Tanh, grp_res(gb_) + [R("HB")], tI.r, bias=HB[:, l, 1, dc:dc + 1], scale=0.5)
            ACT(tR.t[:, 0:T], grp_ap(ga, 0, T), AF.Tanh, grp_res(ga) + [R("HB")], tR.r, bias=HB[:, l, 0, dc:dc + 1], scale=0.5)
            STT("dve", tI.t[:, 0:T], tI.t[:, 0:T], 1.0, XA.t[:, 0:T], ALU.add, ALU.mult, tI.r + xr, tI.r)
            ACT(tS.t[:, 0:T], tR.t[:, 0:T], AF.Exp, tR.r + [R("NSP")], tS.r, bias=NSP[:, l, 1, dc:dc + 1], scale=NSP[:, l, 1, dc:dc + 1])
            ACT(tR.t[:, 0:T], tR.t[:, 0:T], AF.Exp, tR.r + [R("NSP")], tR.r, bias=NSP[:, l, 0, dc:dc + 1], scale=NSP[:, l, 0, dc:dc + 1])
            ACT(tS.t[:, 0:T], tS.t[:, 0:T], AF.Sqrt, tS.r, tS.r, bias=0.25, scale=-0.25)

        def scan_dve(c, d):
            tR, tI, tS = temps[3 * d], temps[3 * d + 1], temps[3 * d + 2]
            TT("pool" if d == 1 else "dve", tI.t[:, 0:T], tI.t[:, 0:T], tS.t[:, 0:T], ALU.mult, tI.r + tS.r, tI.r)
            for (o, L, h0f, bi) in seqs:
                if d == 0:
                    SCAN(tS.t[:, o:o + L], tR.t[:, o:o + L], tI.t[:, o:o + L], h0f(d, c), tR.r + tI.r + [R("HCTX")], tS.r)
                    last = tS.t[:, o + L - 1:o + L]
                else:
                    SCAN(rev(tS.t, o, L), rev(tR.t, o, L), rev(tI.t, o, L), h0f(d, c), tR.r + tI.r + [R("HCTX")], tS.r)
                    last = tS.t[:, o:o + 1]
                if bi is not None:
                    CP("dve", HCTX[:, l, bi, d, c:c + 1], last, tS.r, [R("HCTX")])

        def finish(c):
            if not full:
                return
            hf, hb = temps[2], temps[5]
            yres = [R("Y", c, ti) for ti in range(nt)]
            TT("dve", hb.t[:, 0:T], hb.t[:, 0:T], hf.t[:, 0:T], ALU.add, hb.r + hf.r, hb.r)
            STT("dve", Y[:, c, 0:T], hb.t[:, 0:T], 0.5, Y[:, c, 0:T], ALU.mult, ALU.mult, hb.r + yres, yres)

        prelude_x(0)
        prelude_g(0)
        for c in range(4):
            gate_act(c, 0)
            scan_dve(c, 0)
            gate_act(c, 1)
            if c < 3:
                prelude_x(c + 1)
                prelude_g(c + 1)
            scan_dve(c, 1)
            finish(c)
        if full:
            tap("ylru_l%d_T%d" % (l, T), Y[:, 0:4, 0:T], [128, 4, T], [R("Y", c, ti) for c in range(4) for ti in range(len(tiles))])

    def fourier(l, T, seqs, is_ctx):
        ZF = ABuf("ZF", [128, 2, SEQ], BF16, 0)
        AB = ABuf("AB", [128, 16, 512], BF16, 8192)
        RING = [ABuf("FR%d" % i, [128, SEQ], BF16, 24576 + i * 4096) for i in range(3)]
        tiles = tiles_of(T)
        ws = load_w(wsrc(win_d[l], 0, 8, 1024, 256), 8, 256)
        for oc in range(2):
            g = next_group()
            proj(ws, oc * 128, 8, u_rhs, u_res, tiles, g)
            for ti, (o, w) in enumerate(tiles):
                CP(evac_eng(), ZF.t[:, oc, o:o + w], bk(g * 4 + ti, 0, w), [bres(g * 4 + ti)], ZF.res(o, o + w, oc))
        rr = 0
        for (o, L, _h, _b) in seqs:
            ntc = L // 128
            scale = 1.0 / math.sqrt(L * 256.0)
            for tc in range(ntc):
                b = next_bank()
                to = o + tc * 128
                for kc in range(2):
                    MM(bk(b, 0, 512), ZF.t[:, kc, to:to + 128], DFTC[:, kc, :], kc == 0, kc == 1, ZF.res(to, to + 128, kc) + [R("DFTC")], [bres(b)])
                CP(evac_eng(), AB.t[:, tc, :], bk(b, 0, 512), [bres(b)], AB.res(0, 512, tc))
            nlt = (L + 511) // 512
            lw = min(L, 512)
            if is_ctx:
                cbanks = [next_bank() for _ in range(2)]
                bmap = lambda cc, lt: cbanks[cc]
            else:
                bmap = lambda cc, lt: cc * 4 + lt
            tab = dftlc_d if is_ctx else dftl_d
            for tc in range(ntc):
                for tb in range(2):
                    ring = RING[rr % 3]
                    rr += 1
                    DMA("sp", ring.t[:, 0:L], tab[tb, tc * 128:(tc + 1) * 128, :], (), ring.res(0, L), ring.res(0, L)[0])
                    for cc in range(2):
                        for lt in range(nlt):
                            b = bmap(cc, lt)
                            MM(bk(b, 0, lw), AB.t[:, tc, tb * 256 + cc * 128:tb * 256 + (cc + 1) * 128], ring.t[:, lt * 512:lt * 512 + lw],
                               tc == 0 and tb == 0, tc == ntc - 1 and tb == 1, AB.res(0, 512, tc) + ring.res(0, L), [bres(b)])
            for cc in range(2):
                for lt in range(nlt):
                    b = bmap(cc, lt)
                    yo = o + lt * 512
                    ACT(Y[:, 4 + cc, yo:yo + lw], bk(b, 0, lw), AF.Copy, [bres(b)], [R("Y", 4 + cc, yo // 512)], scale=scale)
        tap("yfou_l%d_T%d" % (l, T), Y[:, 4:6, 0:T], [128, 2, T], [R("Y", 4 + c, ti) for c in range(2) for ti in range(len(tiles))])

    def pool_mix(l, T, seqs, is_ctx):
        ZPT = ABuf("ZPT", [128, 16, 256], BF16, 0)
        PP = ABuf("PP", [128, 2, SEQ], BF16, 8192)
        tiles = tiles_of(T)
        if is_ctx:
            PM = ABuf("PMC", [128, 16, 128], BF16, 16384)
            DMA("sp", PM.t[:], pmc_d.rearrange("g a b p c -> p (g a b) c"), (), PM.res(0, 128 * 16), PM.res(0, 128 * 16)[0])
        else:
            PM = ABuf("PML", [128, 4, 128], BF16, 16384)
            DMA("sp", PM.t[:], pml_d.rearrange("g p c -> p g c"), (), PM.res(0, 128 * 4), PM.res(0, 128 * 4)[0])
        pm_res = PM.res(0, 128 * (16 if is_ctx else 4))
        ws = load_w(wsrc(win_d[l], 0, 8, 1280, 256), 8, 256)
        nchunk = T // 128
        for tc in range(nchunk):
            b = next_bank()
            for k in range(8):
                MM(bk(b, 0, 256), U[:, k, tc * 128:(tc + 1) * 128], ws[0][:, k, 0:256], k == 0, k == 7, ws[1] + [R("U", k, tc // 4)], [bres(b)])
            CP(evac_eng(), ZPT.t[:, tc, :], bk(b, 0, 256), [bres(b)], ZPT.res(0, 256, tc))
        for ti, (o, w) in enumerate(tiles):
            for cc in range(2):
                b = next_bank()
                for tl in range(w // 128):
                    tc = o // 128 + tl
                    if is_ctx:
                        base = (tc // 2) * 2
                        srcs = [(base + ti_in, tc % 2, ti_in) for ti_in in range(2)]
                    else:
                        srcs = [(tc, 0, 0)]
                    for gh in range(2):
                        gidx = cc * 2 + gh
                        for si, (tci, to_, ti_) in enumerate(srcs):
                            pm = PM.t[:, gidx * 4 + to_ * 2 + ti_, :] if is_ctx else PM.t[:, gidx, :]
                            MM(bk(b, tl * 128, (tl + 1) * 128, gh * 64, (gh + 1) * 64), ZPT.t[:, tci, gidx * 64:(gidx + 1) * 64], pm,
                               si == 0, si == len(srcs) - 1, ZPT.res(0, 256, tci) + pm_res, [bres(b)], tile_position=(0, gh * 64))
                CP(evac_eng(), PP.t[:, cc, o:o + w], bk(b, 0, w), [bres(b)], PP.res(o, o + w, cc))
        for cc in range(2):
            for ti, (o, w) in enumerate(tiles):
                b = next_bank()
                MM(bk(b, 0, w), PWB[:, l, cc, :], PP.t[:, cc, o:o + w], True, True, PP.res(o, o + w, cc) + [R("PWB")], [bres(b)])
                ACT(Y[:, 6 + cc, o:o + w], bk(b, 0, w), AF.Copy, [bres(b), R("VEC")], [R("Y", 6 + cc, ti)], scale=vcol(("pscale", l), cc))
        tap("ypool_l%d_T%d" % (l, T), Y[:, 6:8, 0:T], [128, 2, T], [R("Y", 6 + c, ti) for c in range(2) for ti in range(len(tiles))])

    def sconv(l, T, seqs):
        HS = ABuf("HS", [128, 2, SEQ], F32, 0)
        CV = ABuf("CV", [128, SEQ], F32, 16384)
        tiles = tiles_of(T)
        ws = load_w(wsrc(win_d[l], 0, 8, 1536 + 512, 256), 8, 256)
        for oc in range(2):
            g = next_group()
            proj(ws, oc * 128, 8, u_rhs, u_res, tiles, g)
            for ti, (o, w) in enumerate(tiles):
                CP(evac_eng(), HS.t[:, oc, o:o + w], bk(g * 4 + ti, 0, w), [bres(g * 4 + ti)], HS.res(o, o + w, oc))
        ws = load_w(wsrc(win_d[l], 0, 8, 1536 + 256, 256), 8, 256)
        for oc in range(2):
            g = next_group()
            proj(ws, oc * 128, 8, u_rhs, u_res, tiles, g)
            for ti, (o, w) in enumerate(tiles):
                TT("dve", HS.t[:, oc, o:o + w], HS.t[:, oc, o:o + w], bk(g * 4 + ti, 0, w), ALU.mult,
                   HS.res(o, o + w, oc) + [bres(g * 4 + ti)], HS.res(o, o + w, oc))
        ws = load_w(wsrc(win_d[l], 0, 8, 1536, 256), 8, 256)
        for oc in range(2):
            for (o, L, _h, _b) in seqs:
                mr = HS.res(o, o + L, oc)
                cr = CV.res(o, o + L)
                TS("dve", CV.t[:, o:o + L], HS.t[:, oc, o:o + L], vcol(("sw", l), 1 * 2 + oc), None, ALU.mult, None, mr + [R("VEC")], cr)
                STT("dve", CV.t[:, o + 1:o + L], HS.t[:, oc, o:o + L - 1], vcol(("sw", l), 0 * 2 + oc), CV.t[:, o + 1:o + L], ALU.mult, ALU.add,
                    mr + cr + [R("VEC")], cr)
                STT("dve", CV.t[:, o:o + L - 1], HS.t[:, oc, o + 1:o + L], vcol(("sw", l), 2 * 2 + oc), CV.t[:, o:o + L - 1], ALU.mult, ALU.add,
                    mr + cr + [R("VEC")], cr)
            g = next_group()
            proj(ws, oc * 128, 8, u_rhs, u_res, tiles, g)
            for ti, (o, w) in enumerate(tiles):
                TT("dve", Y[:, 8 + oc, o:o + w], CV.t[:, o:o + w], bk(g * 4 + ti, 0, w), ALU.mult, CV.res(o, o + w) + [bres(g * 4 + ti)], [R("Y", 8 + oc, ti)])
        tap("ysc_l%d_T%d" % (l, T), Y[:, 8:10, 0:T], [128, 2, T], [R("Y", 8 + c, ti) for c in range(2) for ti in range(len(tiles))])

    def pass_b(l, T, mi, nrm2, head_fn):
        MG = ABuf("MG", [128, 4, SEQ], BF16, 0)
        MACC = ABuf("MACC", [128, SEQ], F32, 16384)
        GT = ABuf("GT", [128, SEQ], BF16, 24576)
        TTB = [ABuf("TTB%d" % i, [128, 512], F32, 28672 + i * 2048) for i in range(2)]
        tiles = tiles_of(T)
        ybase = (0, 4, 6, 8)
        ykc = (4, 2, 2, 2)
        for half in range(2):
            for jj in range(4):
                j = half * 4 + jj
                for i in range(4):
                    wsg = load_w(wsrc(win_d[l], 0, 8, 2304 + i * 1024 + j * 128, 128), 8, 128)
                    wsb = load_w(wsrc(wbr_d[i][l], 0, ykc[i], j * 128, 128), ykc[i], 128)
                    ga = next_group()
                    proj(wsg, 0, 8, u_rhs, u_res, tiles, ga)
                    for ti, (o, w) in enumerate(tiles):
                        ACT(GT.t[:, o:o + w], bk(ga * 4 + ti, 0, w), AF.Sigmoid, [bres(ga * 4 + ti)], GT.res(o, o + w))
                    gb_ = next_group()
                    proj(wsb, 0, ykc[i], (lambda k, o, w, i=i: Y[:, ybase[i] + k, o:o + w]), (lambda k, ti, i=i: [R("Y", ybase[i] + k, ti)]), tiles, gb_)
                    for ti, (o, w) in enumerate(tiles):
                        b = gb_ * 4 + ti
                        if i == 0:
                            TT("dve", MACC.t[:, o:o + w], GT.t[:, o:o + w], bk(b, 0, w), ALU.mult, GT.res(o, o + w) + [bres(b)], MACC.res(o, o + w))
                        else:
                            tt = TTB[ti % 2]
                            TT("dve", tt.t[:, 0:w], GT.t[:, o:o + w], bk(b, 0, w), ALU.mult, GT.res(o, o + w) + [bres(b)], tt.res(0, w))
                            if i < 3:
                                TT("dve", MACC.t[:, o:o + w], MACC.t[:, o:o + w], tt.t[:, 0:w], ALU.add, MACC.res(o, o + w) + tt.res(0, w), MACC.res(o, o + w))
                            else:
                                TT("dve", MG.t[:, jj, o:o + w], MACC.t[:, o:o + w], tt.t[:, 0:w], ALU.add, MACC.res(o, o + w) + tt.res(0, w), MG.res(o, o + w, jj))
            if half == 0:
                tap("mg_l%d_T%d" % (l, T), MG.t[:, :, 0:T], [128, 4, T], [r for jj in range(4) for r in MG.res(0, T, jj)])
            if half == 1:
                wso = [load_w(wsrc(wout_d[l], 512, 4, jt * 256, 256), 4, 256) for jt in range(4)]

                def wo_tile(ti):
                    o, w = tiles[ti]
                    for jo in range(8):
                        ws = wso[jo // 2]
                        b = next_bank()
                        for k in range(4):
                            MM(bk(b, 0, w), ws[0][:, k, (jo % 2) * 128:(jo % 2) * 128 + 128], MG.t[:, k, o:o + w], k == 0, k == 3,
                               ws[1] + MG.res(o, o + w, k), [bres(b)])
                        STT("dve", XR[:, jo, o:o + w], bk(b, 0, w), modv(l, 2, jo, mi), XR[:, jo, o:o + w], ALU.mult, ALU.add,
                            [bres(b), R("XR", jo, ti), R("MODT")], [R("XR", jo, ti)])
                tail_schedule(len(tiles), wo_tile, nrm2, head=head_fn)
                break
            for jt in range(4):
                ws = load_w(wsrc(wout_d[l], half * 512, 4, jt * 256, 256), 4, 256)
                for oc in range(2):
                    jo = jt * 2 + oc
                    g = next_group()
                    proj(ws, oc * 128, 4, (lambda k, o, w: MG.t[:, k, o:o + w]), (lambda k, ti: MG.res(tiles[ti][0], tiles[ti][0] + tiles[ti][1], k)), tiles, g)
                    for ti, (o, w) in enumerate(tiles):
                        STT("dve", XR[:, jo, o:o + w], bk(g * 4 + ti, 0, w), modv(l, 2, jo, mi), XR[:, jo, o:o + w], ALU.mult, ALU.add,
                            [bres(g * 4 + ti), R("XR", jo, ti), R("MODT")], [R("XR", jo, ti)])

    def make_ffn(l, T, mi):
        RT = [ABuf("RT%d" % i, [128, 512], F32, i * 2048) for i in range(4)]
        tiles = tiles_of(T)
        rr = [0]
        head_ws = []

        def relu2(b, w, hc, o, ti):
            rt = RT[rr[0] % 4]
            rr[0] += 1
            ACT(rt.t[:, 0:w], bk(b, 0, w), AF.Relu, [bres(b)], rt.res(0, w))
            TT("dve", Y[:, hc, o:o + w], rt.t[:, 0:w], rt.t[:, 0:w], ALU.mult, rt.res(0, w), [R("Y", hc, ti)])

        def head(ti):
            if not head_ws:
                for jt in range(4):
                    head_ws.append(load_w(wsrc(wff1_d[l], 0, 8, jt * 256, 256), 8, 256))
            o, w = tiles[ti]
            for hc in range(8):
                ws = head_ws[hc // 2]
                b = next_bank()
                for k in range(8):
                    MM(bk(b, 0, w), ws[0][:, k, (hc % 2) * 128:(hc % 2) * 128 + 128], U[:, k, o:o + w], k == 0, k == 7,
                       ws[1] + [R("U", k, ti)], [bres(b)])
                relu2(b, w, hc, o, ti)

        def rest(tail_norm, tail_head=None):
            for g4 in range(4):
                if g4 > 0:
                    for jt in range(4):
                        ws = load_w(wsrc(wff1_d[l], 0, 8, g4 * 1024 + jt * 256, 256), 8, 256)
                        for oc in range(2):
                            hc = jt * 2 + oc
                            g = next_group()
                            proj(ws, oc * 128, 8, u_rhs, u_res, tiles, g)
                            for ti, (o, w) in enumerate(tiles):
                                relu2(g * 4 + ti, w, hc, o, ti)
                if g4 < 3:
                    for jt in range(4):
                        ws = load_w(wsrc(wff2_d[l], g4 * 1024, 8, jt * 256, 256), 8, 256)
                        for oc in range(2):
                            jo = jt * 2 + oc
                            g = next_group()
                            proj(ws, oc * 128, 8, (lambda k, o, w: Y[:, k, o:o + w]), (lambda k, ti: [R("Y", k, ti)]), tiles, g)
                            for ti, (o, w) in enumerate(tiles):
                                STT("dve", XR[:, jo, o:o + w], bk(g * 4 + ti, 0, w), modv(l, 5, jo, mi), XR[:, jo, o:o + w], ALU.mult, ALU.add,
                                    [bres(g * 4 + ti), R("XR", jo, ti), R("MODT")], [R("XR", jo, ti)])
                else:
                    wst = [load_w(wsrc(wff2_d[l], g4 * 1024, 8, jt * 256, 256), 8, 256) for jt in range(4)]

                    def ff2_tile(ti):
                        o, w = tiles[ti]
                        for jo in range(8):
                            ws = wst[jo // 2]
                            b = next_bank()
                            for k in range(8):
                                MM(bk(b, 0, w), ws[0][:, k, (jo % 2) * 128:(jo % 2) * 128 + 128], Y[:, k, o:o + w], k == 0, k == 7,
                                   ws[1] + [R("Y", k, ti)], [bres(b)])
                            STT("dve", XR[:, jo, o:o + w], bk(b, 0, w), modv(l, 5, jo, mi), XR[:, jo, o:o + w], ALU.mult, ALU.add,
                                [bres(b), R("XR", jo, ti), R("MODT")], [R("XR", jo, ti)])
                    tail_schedule(len(tiles), ff2_tile, tail_norm, head=tail_head)
        return head, rest

    def make_storer(dst2d, T):
        OST = [ABuf("OST%d" % i, [128, 1024], F32, i * 4096) for i in range(4)]
        tl = tiles_of(T)

        def store_tile(ti):
            o, w = tl[ti]
            for tc in range(w // 128):
                ost = OST[tc % 4]
                for kh in range(2):
                    b = next_bank()
                    for kk in range(4):
                        k = kh * 4 + kk
                        TR(bk(b, kk * 128, (kk + 1) * 128), XR[:, k, o + tc * 128:o + (tc + 1) * 128], [R("XR", k, ti)], [bres(b)])
                    CP(evac_eng(), ost.t[:, kh * 512:(kh + 1) * 512], bk(b, 0, 512), [bres(b)], ost.res(kh * 512, (kh + 1) * 512))
                uniq[0] += 1
                op = DMA("sp", dst2d[o + tc * 128:o + (tc + 1) * 128, :], ost.t[:], ost.res(0, 1024), [R("outd", uniq[0])], ost.res(0, 1024)[0])
                final_ops.append(op)
        return store_tile

    def mk_norm(T, l, mi, which):
        if which == "final":
            return Norm(T, (lambda k: G32[:, DEPTH * 16 + k:DEPTH * 16 + k + 1]), None, out_inplace=True)
        m_sh = 0 if which == 0 else 3
        return Norm(T, (lambda k: GS[:, l, mi, which, k:k + 1]), (lambda k: modv(l, m_sh, k, mi)))

    def layer(l, T, seqs, mi, is_ctx, state_only, next_norm, tail_head=None):
        nt = len(tiles_of(T))

        def tapx(nm):
            tap("%s_l%d_T%d" % (nm, l, T), XR[:, :, 0:T], [128, 8, T], [R("XR", k, ti) for k in range(8) for ti in range(nt)])
        tap("u1_l%d_T%d" % (l, T), U[:, :, 0:T], [128, 8, T], [R("U", k, ti) for k in range(8) for ti in range(nt)])
        lru(l, T, seqs, save_state=state_only)
        if state_only:
            return
        fourier(l, T, seqs, is_ctx)
        pool_mix(l, T, seqs, is_ctx)
        sconv(l, T, seqs)
        ffn_head, ffn_rest = make_ffn(l, T, mi)
        pass_b(l, T, mi, mk_norm(T, l, mi, 1), ffn_head)
        tapx("xmid")
        ffn_rest(next_norm, tail_head)
        tapx("xout")

    setup()
    tap("modt", MODT[:], [128, DEPTH, 48, NM], [R("MODT")])
    TC = NB * CTX
    tail_schedule(len(tiles_of(TC)), make_loader(ctx_d.rearrange("b t d -> (b t) d"), TC), mk_norm(TC, 0, NB, 0))
    for l in range(DEPTH):
        seqs = [(b * CTX, CTX, (lambda d, c: 0.0), b) for b in range(NB)]
        last = (l == DEPTH - 1)
        layer(l, TC, seqs, NB, True, last, None if last else mk_norm(TC, l + 1, NB, 0))
    tap("hctx", HCTX[:], [128, DEPTH, NB, 2, 4], [R("HCTX")])
    for b in range(NB):
        tail_schedule(len(tiles_of(SEQ)), make_loader(x_d[b], SEQ), mk_norm(SEQ, 0, b, 0))
        for l in range(DEPTH):
            seqs = [(0, SEQ, (lambda d, c, l=l, b=b: HCTX[:, l, b, d, c:c + 1]), None)]
            if l + 1 < DEPTH:
                layer(l, SEQ, seqs, b, False, False, mk_norm(SEQ, l + 1, b, 0))
            else:
                layer(l, SEQ, seqs, b, False, False, mk_norm(SEQ, 0, b, "final"), make_storer(out_d[b], SEQ))
    P.emit(final_wait_ops=final_ops)
    return nc, tap_d


def make_in_maps(inputs, n_cores, NB, DEPTH):
    cst = _consts()
    vecs, _ = pack_vecs(inputs, DEPTH)
    maps = []
    for ci in range(n_cores):
        bs = slice(ci * NB, (ci + 1) * NB)
        call = np.concatenate([np.asarray(inputs["c"][bs], np.float32), np.asarray(inputs["c_ctx"], np.float32)[None, :]], axis=0)
        NM = NB + 1
        ct = np.ascontiguousarray(call.reshape(NM, 8, 128).transpose(2, 1, 0).reshape(128, 8 * NM))
        m = {
            "x": np.ascontiguousarray(inputs["x"][bs], dtype=np.float32),
            "ctx": np.ascontiguousarray(inputs["ctx"][bs], dtype=np.float32),
            "cT": ct,
            "vecs": vecs,
        }
        for name in ("w_mod", "w_in", "lru_w_a", "lru_w_x", "pool_w", "w_br_lru", "w_br_fourier", "w_br_pool", "w_br_sconv", "w_out", "w_ff1", "w_ff2"):
            m[name] = np.ascontiguousarray(np.asarray(inputs[name], np.float32)[:DEPTH])
        for name in ("dft_c", "dft_l", "dft_lc", "pm_lat", "pm_ctx"):
            m[name] = cst[name]
        maps.append(m)
    return maps


def kernel(**inputs):
    n_cores, NB, DEPTH = 8, 4, 2
    inputs = {k: np.asarray(v) for k, v in inputs.items()}
    nc, _ = build_program(Cfg(NB=NB, DEPTH=DEPTH))
    maps = make_in_maps(inputs, n_cores, NB, DEPTH)
    res = run_bass_kernel_spmd(nc, maps, core_ids=list(range(n_cores)))
    out = np.concatenate([np.asarray(r["out"], dtype=np.float32) for r in res.results], axis=0)
    return out
```

```python
import math
from contextlib import ExitStack
import numpy as np
import ml_dtypes
import concourse.bass as bass
import concourse.mybir as mybir
from concourse.ap import AP
from concourse.bass_utils import run_bass_kernel_spmd

F32 = mybir.dt.float32
BF16 = mybir.dt.bfloat16
AF = mybir.ActivationFunctionType
ALU = mybir.AluOpType

D = 1024
SEQ = 2048
CTX = 256
NIN = 6400
DFF = 4096
EPS = 1e-6
ENGS = ("pe", "dve", "act", "pool", "sp")
SB_BASE = 16512
SB_END = 229376


class Res:
    __slots__ = ("name", "last_w", "readers", "dma_sem", "dma_cnt")

    def __init__(self, name, init_w=None):
        self.name = name
        self.last_w = init_w
        self.readers = []
        self.dma_sem = None
        self.dma_cnt = 0


class Op:
    __slots__ = ("eng", "fn", "deps", "is_dma", "sem_res", "dma_val", "sig", "sig_idx")

    def __init__(self, eng, fn, is_dma):
        self.eng = eng
        self.fn = fn
        self.deps = []
        self.is_dma = is_dma
        self.sem_res = None
        self.dma_val = 0
        self.sig = False
        self.sig_idx = 0


class Prog:
    def __init__(self, nc):
        self.nc = nc
        self.ops = []
        self.per_eng = {e: [] for e in ENGS}
        self.dma_res = []
        self.res = {}

    def R(self, *key):
        r = self.res.get(key)
        if r is None:
            r = Res("_".join(str(k) for k in key))
            self.res[key] = r
        return r

    def op(self, eng, meth, kw, reads=(), writes=(), dma_res=None):
        o = Op(eng, (meth, kw), dma_res is not None)
        seen = set()
        deps = o.deps
        for r in reads:
            d = r.last_w
            if d is not None and id(d) not in seen:
                seen.add(id(d))
                deps.append(d)
        for w in writes:
            d = w.last_w
            if d is not None and id(d) not in seen:
                seen.add(id(d))
                deps.append(d)
            for d in w.readers:
                if id(d) not in seen:
                    seen.add(id(d))
                    deps.append(d)
        for r in reads:
            r.readers.append(o)
        for w in writes:
            w.last_w = o
            w.readers = []
        if dma_res is not None:
            o.sem_res = dma_res
            dma_res.dma_cnt += 1
            o.dma_val = 16 * dma_res.dma_cnt
            if dma_res.dma_sem is None:
                dma_res.dma_sem = True
                self.dma_res.append(dma_res)
        self.per_eng[eng].append(o)
        self.ops.append(o)
        return o

    def emit(self, final_wait_ops=()):
        nc = self.nc
        for o in self.ops:
            for d in o.deps:
                if not d.is_dma and not (d.eng == "pe" and o.eng == "pe"):
                    d.sig = True
        for e in ENGS:
            c = 0
            for o in self.per_eng[e]:
                if o.sig and not o.is_dma:
                    c += 1
                    o.sig_idx = c
        with ExitStack() as st:
            esem = {e: st.enter_context(nc.semaphore("s_" + e)) for e in ENGS}
            for i, r in enumerate(self.dma_res):
                r.dma_sem = st.enter_context(nc.semaphore("d%d" % i))
            block = st.enter_context(nc.Block())

            def run_engine(e, engobj):
                waited = {}
                for o in self.per_eng[e]:
                    for d in o.deps:
                        if d.is_dma:
                            key = id(d.sem_res)
                            sem = d.sem_res.dma_sem
                            val = d.dma_val
                        else:
                            if d.eng == "pe" and e == "pe":
                                continue
                            key = d.eng
                            sem = esem[d.eng]
                            val = d.sig_idx
                        if waited.get(key, 0) >= val:
                            continue
                        waited[key] = val
                        engobj.wait_ge(sem, val)
                    ins = getattr(engobj, o.fn[0])(**o.fn[1])
                    if o.is_dma:
                        ins.then_inc(o.sem_res.dma_sem, 16)
                    elif o.sig:
                        ins.then_inc(esem[e], 1)
                if e == "sp":
                    for o in final_wait_ops:
                        engobj.wait_ge(o.sem_res.dma_sem, o.dma_val)

            block.tensor(lambda eng: run_engine("pe", eng))
            block.vector(lambda eng: run_engine("dve", eng))
            block.scalar(lambda eng: run_engine("act", eng))
            block.gpsimd(lambda eng: run_engine("pool", eng))
            block.sync(lambda eng: run_engine("sp", eng))


def _dft_tables(n):
    idx = np.arange(n, dtype=np.int64)
    m = (idx[:, None] * idx[None, :]) % n
    ang = 2.0 * np.pi * m.astype(np.float64) / n
    return np.cos(ang), np.sin(ang)


def _pool_matrix(w, dom):
    P = np.zeros((dom, dom), np.float64)
    for t in range(dom):
        lo = max(t - w // 2, 0)
        hi = min(t + w // 2, dom)
        P[lo:hi, t] += 1.0 / (hi - lo)
        P[t, t] -= 1.0
    return P


_CONST_CACHE = {}


def _consts():
    if _CONST_CACHE:
        return _CONST_CACHE
    bf = ml_dtypes.bfloat16
    cc, sc = _dft_tables(256)
    _CONST_CACHE["dft_c"] = np.concatenate([cc, sc], axis=1).astype(bf)
    cl, sl = _dft_tables(SEQ)
    _CONST_CACHE["dft_l"] = np.stack([cl, -sl]).astype(bf)
    cl, sl = _dft_tables(CTX)
    _CONST_CACHE["dft_lc"] = np.stack([cl, -sl]).astype(bf)
    pl = np.zeros((4, 128, 128), np.float64)
    pc = np.zeros((4, 2, 2, 128, 128), np.float64)
    for g, w in enumerate((2, 4, 8, 16)):
        p64 = _pool_matrix(w, 64)
        pl[g, :64, :64] = p64
        pl[g, 64:, 64:] = p64
        p256 = _pool_matrix(w, 256)
        for to in range(2):
            for ti in range(2):
                pc[g, to, ti] = p256[ti * 128:(ti + 1) * 128, to * 128:(to + 1) * 128]
    _CONST_CACHE["pm_lat"] = pl.astype(bf)
    _CONST_CACHE["pm_ctx"] = pc.astype(bf)
    return _CONST_CACHE


class VecLayout:
    def __init__(self, depth):
        self.col = {}
        n = 0

        def add(name, cnt):
            nonlocal n
            self.col[name] = n
            n += cnt
        for l in range(depth):
            add(("g1", l), 8)
            add(("g2", l), 8)
            add(("bmod", l), 48)
            add(("cw", l), 16)
            add(("cb", l), 4)
            add(("ba", l), 8)
            add(("bx", l), 8)
            add(("lam", l), 8)
            add(("pscale", l), 2)
            add(("sw", l), 6)
        add(("gf",), 8)
        self.n = n


def pack_vecs(inp, depth):
    vl = VecLayout(depth)
    v = np.zeros((128, vl.n), np.float32)

    def put(key, arr, cnt):
        a = np.asarray(arr, np.float32).reshape(cnt, 128).T
        v[:, vl.col[key]:vl.col[key] + cnt] = a
    for l in range(depth):
        put(("g1", l), inp["g_norm1"][l], 8)
        put(("g2", l), inp["g_norm2"][l], 8)
        put(("bmod", l), inp["b_mod"][l], 48)
        put(("cw", l), inp["lru_conv_w"][l], 16)
        put(("cb", l), inp["lru_conv_b"][l], 4)
        put(("ba", l), inp["lru_b_a"][l], 8)
        put(("bx", l), inp["lru_b_x"][l], 8)
        put(("lam", l), inp["lru_lam"][l], 8)
        put(("pscale", l), inp["pool_scale"][l], 2)
        put(("sw", l), inp["sconv_w"][l], 6)
    put(("gf",), inp["g_final"], 8)
    return v, vl


class Cfg:
    def __init__(self, NB=4, DEPTH=2, taps=()):
        self.NB = NB
        self.DEPTH = DEPTH
        self.taps = tuple(taps)


def build_program(cfg):
    NB, DEPTH = cfg.NB, cfg.DEPTH
    NM = NB + 1
    nc = bass.Bass("TRN2", target_bir_lowering=False)
    vl = VecLayout(DEPTH)

    def din(name, shape, dt=F32):
        return nc.dram_tensor(name, list(shape), dt, kind="ExternalInput").ap()

    x_d = din("x", [NB, SEQ, D])
    ctx_d = din("ctx", [NB, CTX, D])
    ct_d = din("cT", [128, 8 * NM])
    vecs_d = din("vecs", [128, vl.n])
    wmod_d = din("w_mod", [DEPTH, D, 6 * D])
    win_d = din("w_in", [DEPTH, D, NIN])
    wa_d = din("lru_w_a", [DEPTH, 2, 8, 64, 64])
    wx_d = din("lru_w_x", [DEPTH, 2, 8, 64, 64])
    pw_d = din("pool_w", [DEPTH, 4, 64, 64])
    wbr_d = [din("w_br_lru", [DEPTH, 512, D]), din("w_br_fourier", [DEPTH, 256, D]),
             din("w_br_pool", [DEPTH, 256, D]), din("w_br_sconv", [DEPTH, 256, D])]
    wout_d = din("w_out", [DEPTH, D, D])
    wff1_d = din("w_ff1", [DEPTH, D, DFF])
    wff2_d = din("w_ff2", [DEPTH, DFF, D])
    dftc_d = din("dft_c", [256, 512], BF16)
    dftl_d = din("dft_l", [2, SEQ, SEQ], BF16)
    dftlc_d = din("dft_lc", [2, CTX, CTX], BF16)
    pml_d = din("pm_lat", [4, 128, 128], BF16)
    pmc_d = din("pm_ctx", [4, 2, 2, 128, 128], BF16)
    out_d = nc.dram_tensor("out", [NB, SEQ, D], F32, kind="ExternalOutput").ap()
    tap_d = {}
    final_ops = []

    P = Prog(nc)
    R = P.R

    def MM(out, lhsT, rhs, start, stop, reads, writes, **kw):
        return P.op("pe", "matmul", dict(out=out, lhsT=lhsT, rhs=rhs, start=start, stop=stop, **kw), reads, writes)

    def TR(out, in_, reads, writes):
        return P.op("pe", "transpose", dict(out=out, in_=in_, identity=IDENT[:]), list(reads) + [R("IDENT")], writes)

    def ACT(out, in_, func, reads, writes, bias=None, scale=None):
        kw = dict(out=out, in_=in_, func=func)
        if bias is not None:
            kw["bias"] = bias
        if scale is not None:
            kw["scale"] = scale
        return P.op("act", "activation", kw, reads, writes)

    def CP(eng, out, in_, reads, writes):
        if eng == "act":
            return ACT(out, in_, AF.Copy, reads, writes)
        return P.op(eng, "tensor_copy", dict(out=out, in_=in_), reads, writes)

    def TT(eng, out, in0, in1, op, reads, writes):
        return P.op(eng, "tensor_tensor", dict(out=out, in0=in0, in1=in1, op=op), reads, writes)

    def TS(eng, out, in0, s1, s2, op0, op1, reads, writes):
        kw = dict(out=out, in0=in0, scalar1=s1, scalar2=s2, op0=op0)
        if op1 is not None:
            kw["op1"] = op1
        return P.op(eng, "tensor_scalar", kw, reads, writes)

    def STT(eng, out, in0, scalar, in1, op0, op1, reads, writes):
        return P.op(eng, "scalar_tensor_tensor", dict(out=out, in0=in0, scalar=scalar, in1=in1, op0=op0, op1=op1), reads, writes)

    def SCAN(out, a, b, init, reads, writes):
        return P.op("dve", "tensor_tensor_scan", dict(out=out, data0=a, data1=b, initial=init, op0=ALU.mult, op1=ALU.add), reads, writes)

    def DMA(eng, out, in_, reads, writes, dma_res):
        return P.op(eng, "dma_start", dict(out=out, in_=in_), reads, writes, dma_res=dma_res)

    def MEMSET(eng, ap, val, writes):
        return P.op(eng, "memset", dict(ap=ap, constant=val), (), writes)

    sb_off = [SB_BASE]

    def sb(name, shape, dt, off=None):
        nbytes = int(np.prod(shape[1:])) * (4 if dt == F32 else 2)
        if off is None:
            off = sb_off[0]
            sb_off[0] += (nbytes + 31) // 32 * 32
            assert sb_off[0] <= SB_END, ("SBUF overflow", name, sb_off[0])
        return nc.alloc_sbuf_tensor_at(name, list(shape), dt, offset=off)

    XR = sb("XR", [128, 8, SEQ], F32)
    U = sb("U", [128, 8, SEQ], BF16)
    y_base = sb_off[0]
    Y = sb("Y", [128, 10, SEQ], BF16)
    YT = [sb("YT%d" % i, [128, SEQ], F32, off=y_base + (4 + 2 * i) * SEQ * 2) for i in range(3)]
    wr_base = sb_off[0]
    WR = [sb("WR%d" % i, [128, 8, 256], BF16) for i in range(4)]
    WRH = [sb("WRH%d" % i, [128, 8, 128], BF16, off=wr_base + i * 2048) for i in range(8)]
    WRQ = [sb("WRQ%d" % i, [128, 4, 256], BF16, off=wr_base + i * 2048) for i in range(8)]
    IDENT = sb("IDENT", [128, 128], F32)
    ONES = sb("ONES", [128, 128], BF16)
    VEC = sb("VEC", [128, vl.n], F32)
    CT = sb("CT", [128, 8, NM], F32)
    CTB = sb("CTB", [128, 8, NM], BF16)
    MODT = sb("MODT", [128, DEPTH, 48, NM], F32)
    GS = sb("GS", [128, DEPTH, NM, 2, 8], F32)
    G32 = sb("G32", [128, DEPTH * 16 + 8], F32)
    NSP = sb("NSP", [128, DEPTH, 2, 8], F32)
    NBIAS = sb("NBIAS", [128, 4], F32)
    HB = sb("HB", [128, DEPTH, 2, 8], F32)
    WG = sb("WG", [128, DEPTH, 2, 2, 4, 128], BF16)
    PWB = sb("PWB", [128, DEPTH, 2, 128], BF16)
    DFTC = sb("DFTC", [128, 2, 512], BF16)
    HCTX = sb("HCTX", [128, DEPTH, NB, 2, 4], F32)
    ARENA_BYTES = 36864
    arena0 = sb_off[0]
    sb_off[0] += ARENA_BYTES
    assert sb_off[0] <= SB_END, ("SBUF overflow", sb_off[0])
    PS = nc.alloc_psum_tensor("PS", [128, 4096], F32)

    abuf_cache = {}

    class ABuf:
        def __init__(self, name, shape, dt, off):
            self.esz = 4 if dt == F32 else 2
            self.off = off
            self.W = shape[-1]
            nbytes = int(np.prod(shape[1:])) * self.esz
            assert off + nbytes <= ARENA_BYTES, name
            key = (name, tuple(shape), str(dt), off)
            if key not in abuf_cache:
                abuf_cache[key] = sb(name + "_%d" % len(abuf_cache), shape, dt, off=arena0 + off)
            self.t = abuf_cache[key]

        def res(self, c0, c1, i=0):
            b0 = (self.off + (i * self.W + c0) * self.esz) // 2048
            b1 = (self.off + (i * self.W + c1) * self.esz - 1) // 2048
            return [R("ar", b) for b in range(b0, b1 + 1)]

    def bk(b, c0, c1, p0=0, p1=128):
        return PS[p0:p1, b * 512 + c0:b * 512 + c1]

    def bres(i):
        return R("bank", i)

    bank_rr = [0]

    def next_bank():
        b = bank_rr[0]
        bank_rr[0] = (b + 1) % 8
        return b

    grp_rr = [0]

    def next_group():
        g = grp_rr[0]
        grp_rr[0] = 1 - g
        return g

    wr_rr = [0]
    uniq = [0]

    def vcol(key, j=0):
        c = vl.col[key] + j
        return VEC[:, c:c + 1]

    ev_rr = [0]

    def evac_eng():
        ev_rr[0] ^= 1
        return "act" if ev_rr[0] else "dve"

    def load_w(src_ap, kc, ncols):
        h = wr_rr[0]
        if ncols > 128 and kc <= 4:
            wr_rr[0] = (h + 1) % 8
            t, res = WRQ[h], [R("wrh", h)]
        elif ncols > 128:
            if h % 2:
                h = (h + 1) % 8
            wr_rr[0] = (h + 2) % 8
            t, res = WR[h // 2], [R("wrh", h), R("wrh", h + 1)]
        else:
            wr_rr[0] = (h + 1) % 8
            t, res = WRH[h], [R("wrh", h)]
        DMA("pool", t[:, 0:kc, 0:ncols], src_ap, (), res, res[0])
        return (t, res)

    def wsrc(w2d, r0, kc, c0, ncols):
        return w2d[r0:r0 + kc * 128, c0:c0 + ncols].rearrange("(k p) c -> p k c", p=128)

    def tap(name, src_ap, shape, reads):
        if name not in cfg.taps:
            return
        t = nc.dram_tensor("tap_" + name, list(shape), src_ap.dtype, kind="ExternalOutput").ap()
        tap_d[name] = t
        o = DMA("sp", t, src_ap, reads, [R("tap", name)], R("tap", name))
        final_ops.append(o)

    def tiles_of(T):
        return [(o, min(512, T - o)) for o in range(0, T, 512)]

    def setup():
        DMA("sp", VEC[:], vecs_d, (), [R("VEC")], R("VEC"))
        DMA("sp", CT[:], ct_d.rearrange("p (k b) -> p k b", b=NM), (), [R("CT")], R("CT"))
        DMA("sp", DFTC[:], dftc_d.rearrange("(k p) c -> p k c", p=128), (), [R("DFTC")], R("DFTC"))
        MEMSET("pool", IDENT[:], 1.0, [R("IDENT")])
        P.op("pool", "affine_select", dict(out=IDENT[:], in_=IDENT[:], pattern=[[-1, 128]], compare_op=ALU.is_equal,
                                           fill=0.0, base=0, channel_multiplier=1), [R("IDENT")], [R("IDENT")])
        MEMSET("dve", ONES[:], 1.0, [R("ONES")])
        MEMSET("dve", NBIAS[:, 0:1], 1.0, [R("NBIAS")])
        MEMSET("dve", NBIAS[:, 1:2], 1024.0 * EPS, [R("NBIAS")])
        MEMSET("dve", WG[:], 0.0, [R("WG")])
        MEMSET("dve", PWB[:], 0.0, [R("PWB")])
        for l in range(DEPTH):
            for d in range(2):
                for gi, wd in enumerate((wa_d, wx_d)):
                    for half in range(2):
                        src = wd[l, d].rearrange("(c two) i j -> two i c j", two=2)[half]
                        dst = WG[half * 64:(half + 1) * 64, l, d, gi, :, half * 64:(half + 1) * 64]
                        DMA("pool", dst, src, (), [R("WG")], R("WG"))
            for half in range(2):
                src = pw_d[l].rearrange("(c two) i j -> two i c j", two=2)[half]
                dst = PWB[half * 64:(half + 1) * 64, l, :, half * 64:(half + 1) * 64]
                DMA("pool", dst, src, (), [R("PWB")], R("PWB"))
        for l in range(DEPTH):
            c0 = vl.col[("g1", l)]
            TS("dve", G32[:, l * 16:l * 16 + 16], VEC[:, c0:c0 + 16], 32.0, None, ALU.mult, None, [R("VEC")], [R("G32")])
        c0 = vl.col[("gf",)]
        TS("dve", G32[:, DEPTH * 16:DEPTH * 16 + 8], VEC[:, c0:c0 + 8], 32.0, None, ALU.mult, None, [R("VEC")], [R("G32")])
        for l in range(DEPTH):
            c0 = vl.col[("lam", l)]
            ACT(NSP[:, l, 0, :], VEC[:, c0:c0 + 8], AF.Exp, [R("VEC")], [R("NSP")], scale=-1.0)
            ACT(NSP[:, l, 0, :], NSP[:, l, 0, :], AF.Ln, [R("NSP"), R("NBIAS")], [R("NSP")], bias=NBIAS[:, 0:1], scale=1.0)
            TS("dve", NSP[:, l, 1, :], NSP[:, l, 0, :], -8.0, None, ALU.mult, None, [R("NSP")], [R("NSP")])
            TS("dve", NSP[:, l, 0, :], NSP[:, l, 0, :], -4.0, None, ALU.mult, None, [R("NSP")], [R("NSP")])
            cba = vl.col[("ba", l)]
            TS("dve", HB[:, l, 0, :], VEC[:, cba:cba + 8], 0.5, None, ALU.mult, None, [R("VEC")], [R("HB")])
            cbx = vl.col[("bx", l)]
            TS("dve", HB[:, l, 1, :], VEC[:, cbx:cbx + 8], 0.5, None, ALU.mult, None, [R("VEC")], [R("HB")])
        ACT(CT[:], CT[:], AF.Silu, [R("CT")], [R("CT")])
        CP("dve", CTB[:], CT[:], [R("CT")], [R("CTB")])
        WM = [ABuf("WMB%d" % i, [128, 8, 256], BF16, i * 4096) for i in range(6)]
        wm_rr = 0
        for l in range(DEPTH):
            pb = next_bank()
            for jt in range(24):
                wm = WM[wm_rr % 6]
                wm_rr += 1
                DMA("pool", wm.t[:], wsrc(wmod_d[l], 0, 8, jt * 256, 256), (), wm.res(0, 2048), wm.res(0, 2048)[0])
                for oc in range(2):
                    mk = jt * 2 + oc
                    for k in range(8):
                        MM(bk(pb, mk * 8, mk * 8 + NM), wm.t[:, k, oc * 128:(oc + 1) * 128], CTB[:, k, :], k == 0, k == 7,
                           wm.res(0, 2048) + [R("CTB")], [bres(pb)])
            c0 = vl.col[("bmod", l)]
            full = bk(pb, 0, 384)
            src = AP(full.tensor, full.offset, [list(full.ap[0]), [8, 48], [1, NM]])
            bm = VEC[:, c0:c0 + 48]
            bmb = AP(bm.tensor, bm.offset, [list(bm.ap[0]), [1, 48], [0, NM]])
            TT("dve", MODT[:, l, :, :], src, bmb, ALU.add, [bres(pb), R("VEC")], [R("MODT")])
            for b in range(NM):
                for w in range(2):
                    m = 1 if w == 0 else 4
                    STT("dve", GS[:, l, b, w, :], MODT[:, l, m * 8:(m + 1) * 8, b], 1.0, G32[:, l * 16 + w * 8:l * 16 + w * 8 + 8], ALU.add, ALU.mult,
                        [R("MODT"), R("G32")], [R("GS")])

    def modv(l, m, k, b):
        return MODT[:, l, m * 8 + k, b:b + 1]

    def proj(ws, oc_col, kc, rhs_fn, rhs_res_fn, tiles, grp):
        for k in range(kc):
            for ti, (o, w) in enumerate(tiles):
                b = grp * 4 + ti
                MM(bk(b, 0, w), ws[0][:, k, oc_col:oc_col + 128], rhs_fn(k, o, w), k == 0, k == kc - 1,
                   ws[1] + rhs_res_fn(k, ti), [bres(b)])

    def u_rhs(k, o, w):
        return U[:, k, o:o + w]

    def u_res(k, ti):
        return [R("U", k, ti)]

    def load_tokens(src2d, T):
        STG = [ABuf("STG%d" % i, [128, 1024], F32, i * 4096) for i in range(8)]
        rr = 0
        for ti, (o, w) in enumerate(tiles_of(T)):
            ntc = w // 128
            st = []
            for tc in range(ntc):
                s = STG[rr % 8]
                rr += 1
                DMA("sp", s.t[:], src2d[o + tc * 128:o + (tc + 1) * 128, :], (), s.res(0, 1024), s.res(0, 1024)[0])
                st.append(s)
            for k in range(8):
                b = next_bank()
                for tc in range(ntc):
                    TR(bk(b, tc * 128, (tc + 1) * 128), st[tc].t[:, k * 128:(k + 1) * 128], st[tc].res(0, 1024), [bres(b)])
                CP(evac_eng(), XR[:, k, o:o + w], bk(b, 0, w), [bres(b)], [R("XR", k, ti)])

    class Norm:
        def __init__(self, T, gs_fn, sh_fn, out_inplace=False):
            self.tl = tiles_of(T)
            self.gs_fn, self.sh_fn, self.inplace = gs_fn, sh_fn, out_inplace
            self.SQ = ABuf("SQ", [128, 8, 512], BF16, 16384)
            self.RS = [ABuf("RS%d" % i, [128, 512], F32, 24576 + i * 2048) for i in range(2)]
            self.TM = [ABuf("TM%d" % i, [128, 512], F32, 28672 + i * 2048) for i in range(2)]

        def a(self, ti):
            o, w = self.tl[ti]
            SQ = self.SQ
            b = next_bank()
            for k in range(8):
                ACT(SQ.t[:, k, 0:w], XR[:, k, o:o + w], AF.Square, [R("XR", k, ti)], SQ.res(0, w, k))
                MM(bk(b, 0, w), ONES[:], SQ.t[:, k, 0:w], k == 0, k == 7, SQ.res(0, w, k) + [R("ONES")], [bres(b)])
            rs = self.RS[ti % 2]
            ACT(rs.t[:, 0:w], bk(b, 0, w), AF.Sqrt, [bres(b), R("NBIAS")], rs.res(0, w), bias=NBIAS[:, 1:2], scale=1.0)
            P.op("dve", "reciprocal", dict(out=rs.t[:, 0:w], in_=rs.t[:, 0:w]), rs.res(0, w), rs.res(0, w))

        def b(self, ti):
            o, w = self.tl[ti]
            rs = self.RS[ti % 2]
            for k in range(8):
                if self.inplace:
                    STT("dve", XR[:, k, o:o + w], XR[:, k, o:o + w], self.gs_fn(k), rs.t[:, 0:w], ALU.mult, ALU.mult,
                        [R("XR", k, ti), R("GS"), R("G32")] + rs.res(0, w), [R("XR", k, ti)])
                else:
                    tm = self.TM[k % 2]
                    STT("dve", tm.t[:, 0:w], XR[:, k, o:o + w], self.gs_fn(k), rs.t[:, 0:w], ALU.mult, ALU.mult,
                        [R("XR", k, ti), R("GS")] + rs.res(0, w), tm.res(0, w))
                    ACT(U[:, k, o:o + w], tm.t[:, 0:w], AF.Identity, tm.res(0, w) + [R("MODT")], [R("U", k, ti)], bias=self.sh_fn(k), scale=1.0)

        def run_all(self):
            for ti in range(len(self.tl)):
                self.a(ti)
                self.b(ti)

    def tail_schedule(nt, work, nrm, head=None):
        seq = [("w", 0)]
        for ti in range(1, nt):
            seq.append(("w", ti))
            seq.append(("n", ti - 1))
            if head is not None and ti - 1 >= 2:
                seq.append(("h", ti - 3))
        seq.append(("n", nt - 1))
        done_h = max(0, nt - 3) if head is not None else 0
        if head is not None:
            if nt - 1 >= 2:
                seq.append(("h", nt - 3))
                done_h = nt - 2
            for ti in range(done_h, nt):
                seq.append(("h", ti))
        for kind, ti in seq:
            if kind == "w":
                work(ti)
            elif kind == "n":
                nrm.a(ti)
                nrm.b(ti)
            else:
                head(ti)

    def lru(l, T, seqs, save_state):
        XA = ABuf("XA", [128, SEQ], F32, 0)
        XAB = ABuf("XAB", [128, SEQ], BF16, 8192)
        AT = [ABuf("LA%d" % i, [128, SEQ], F32, 12288 + i * 8192) for i in range(3)]
        tiles = tiles_of(T)
        nt = len(tiles)
        full = not save_state

        class TB:
            pass
        temps = []
        for i in range(3):
            tb = TB()
            tb.t = AT[i].t
            tb.r = AT[i].res(0, T)
            temps.append(tb)
        for i in range(3):
            tb = TB()
            tb.t = YT[i]
            tb.r = [R("Y", 4 + 2 * i + h, ti) for h in range(2) for ti in range(4)]
            temps.append(tb)

        def rev(t, a, n):
            x = t[:, a:a + n]
            return AP(x.tensor, x.offset + n - 1, [list(x.ap[0]), [-1, n]])

        def grp_ap(g, a, n):
            return PS[:, g * 2048 + a:g * 2048 + a + n]

        def grp_res(g):
            return [bres(g * 4 + ti) for ti in range(nt)]
        wtiles = {}

        def get_w(kind, cp):
            key = (kind, cp)
            if key not in wtiles:
                col = (0 if kind == "x" else 512) + cp * 256
                wtiles[key] = load_w(wsrc(win_d[l], 0, 8, col, 256), 8, 256)
            return wtiles[key]
        xr = XA.res(0, T)
        for cp_ in range(2):
            get_w("x", cp_)
            if full:
                get_w("g", cp_)

        def prelude_x(c):
            cp, oc = c // 2, c % 2
            g = next_group()
            proj(get_w("x", cp), oc * 128, 8, u_rhs, u_res, tiles, g)
            for (o, L, h0f, bi) in seqs:
                ACT(XA.t[:, o:o + L], grp_ap(g, o, L), AF.Identity, grp_res(g) + [R("VEC")], xr, bias=vcol(("cb", l), c), scale=vcol(("cw", l), 2 * 4 + c))
                for j, sh in ((0, -2), (1, -1), (3, 1)):
                    if sh < 0:
                        oo, io, n = o - sh, o, L + sh
                    else:
                        oo, io, n = o, o + sh, L - sh
                    STT("dve", XA.t[:, oo:oo + n], grp_ap(g, io, n), vcol(("cw", l), j * 4 + c), XA.t[:, oo:oo + n], ALU.mult, ALU.add,
                        grp_res(g) + xr + [R("VEC")], xr)
            CP("act", XAB.t[:, 0:T], XA.t[:, 0:T], xr, XAB.res(0, T))
            if c == 0:
                tap("xa0_l%d_T%d" % (l, T), XA.t[:, 0:T], [128, T], xr)

        def prelude_g(c):
            if not full:
                return
            cp, oc = c // 2, c % 2
            g = next_group()
            proj(get_w("g", cp), oc * 128, 8, u_rhs, u_res, tiles, g)
            tb = temps[0]
            pz, pr = grp_ap(g, 0, T), grp_res(g)
            tw_ = tb.t[:, 0:T]
            yres = [R("Y", c, ti) for ti in range(nt)]
            ACT(tw_, pz, AF.Square, pr, tb.r, scale=0.21145921592590665)
            STT("dve", tw_, tw_, 1.0, pz, ALU.add, ALU.mult, tb.r + pr, tb.r)
            ACT(tw_, tw_, AF.Tanh, tb.r, tb.r, scale=0.7978845608028654)
            STT("dve", Y[:, c, 0:T], tw_, 1.0, pz, ALU.add, ALU.mult, tb.r + pr, yres)

        def gate_act(c, d):
            tR, tI, tS = temps[3 * d], temps[3 * d + 1], temps[3 * d + 2]
            ga, gb_ = next_group(), next_group()
            for gi, gg in ((0, ga), (1, gb_)):
                for ti, (o, w) in enumerate(tiles):
                    MM(bk(gg * 4 + ti, 0, w), WG[:, l, d, gi, c, :], XAB.t[:, o:o + w], True, True, [R("WG")] + XAB.res(0, T), [bres(gg * 4 + ti)])
            dc = d * 4 + c
            ACT(tI.t[:, 0:T], grp_ap(gb_, 0, T), AF.Tanh, grp_res(gb_) + [R("HB")], tI.r, bias=HB[:, l, 1, dc:dc + 1], scale=0.5)
            ACT(tR.t[:, 0:T], grp_ap(ga, 0, T), AF.Tanh, grp_res(ga) + [R("HB")], tR.r, bias=HB[:, l, 0, dc:dc + 1], scale=0.5)
            STT("dve", tI.t[:, 0:T], tI.t[:, 0:T], 1.0, XA.t[:, 0:T], ALU.add, ALU.mult, tI.r + xr, tI.r)
            ACT(tS.t[:, 0:T], tR.t[:, 0:T], AF.Exp, tR.r + [R("NSP")], tS.r, bias=NSP[:, l, 1, dc:dc + 1], scale=NSP[:, l, 1, dc:dc + 1])
            ACT(tR.t[:, 0:T], tR.t[:, 0:T], AF.Exp, tR.r + [R("NSP")], tR.r, bias=NSP[:, l, 0, dc:dc + 1], scale=NSP[:, l, 0, dc:dc + 1])
            ACT(tS.t[:, 0:T], tS.t[:, 0:T], AF.Sqrt, tS.r, tS.r, bias=0.25, scale=-0.25)

        def scan_dve(c, d):
            tR, tI, tS = temps[3 * d], temps[3 * d + 1], temps[3 * d + 2]
            TT("pool" if d == 1 else "dve", tI.t[:, 0:T], tI.t[:, 0:T], tS.t[:, 0:T], ALU.mult, tI.r + tS.r, tI.r)
            for (o, L, h0f, bi) in seqs:
                if d == 0:
                    SCAN(tS.t[:, o:o + L], tR.t[:, o:o + L], tI.t[:, o:o + L], h0f(d, c), tR.r + tI.r + [R("HCTX")], tS.r)
                    last = tS.t[:, o + L - 1:o + L]
                else:
                    SCAN(rev(tS.t, o, L), rev(tR.t, o, L), rev(tI.t, o, L), h0f(d, c), tR.r + tI.r + [R("HCTX")], tS.r)
                    last = tS.t[:, o:o + 1]
                if bi is not None:
                    CP("dve", HCTX[:, l, bi, d, c:c + 1], last, tS.r, [R("HCTX")])

        def finish(c):
            if not full:
                return
            hf, hb = temps[2], temps[5]
            yres = [R("Y", c, ti) for ti in range(nt)]
            TT("dve", hb.t[:, 0:T], hb.t[:, 0:T], hf.t[:, 0:T], ALU.add, hb.r + hf.r, hb.r)
            STT("dve", Y[:, c, 0:T], hb.t[:, 0:T], 0.5, Y[:, c, 0:T], ALU.mult, ALU.mult, hb.r + yres, yres)

        prelude_x(0)
        prelude_g(0)
        for c in range(4):
            gate_act(c, 0)
            scan_dve(c, 0)
            gate_act(c, 1)
            if c < 3:
                prelude_x(c + 1)
                prelude_g(c + 1)
            scan_dve(c, 1)
            finish(c)
        if full:
            tap("ylru_l%d_T%d" % (l, T), Y[:, 0:4, 0:T], [128, 4, T], [R("Y", c, ti) for c in range(4) for ti in range(len(tiles))])

    def fourier(l, T, seqs, is_ctx):
        ZF = ABuf("ZF", [128, 2, SEQ], BF16, 0)
        AB = ABuf("AB", [128, 16, 512], BF16, 8192)
        RING = [ABuf("FR%d" % i, [128, SEQ], BF16, 24576 + i * 4096) for i in range(3)]
        tiles = tiles_of(T)
        ws = load_w(wsrc(win_d[l], 0, 8, 1024, 256), 8, 256)
        for oc in range(2):
            g = next_group()
            proj(ws, oc * 128, 8, u_rhs, u_res, tiles, g)
            for ti, (o, w) in enumerate(tiles):
                CP(evac_eng(), ZF.t[:, oc, o:o + w], bk(g * 4 + ti, 0, w), [bres(g * 4 + ti)], ZF.res(o, o + w, oc))
        rr = 0
        for (o, L, _h, _b) in seqs:
            ntc = L // 128
            scale = 1.0 / math.sqrt(L * 256.0)
            for tc in range(ntc):
                b = next_bank()
                to = o + tc * 128
                for kc in range(2):
                    MM(bk(b, 0, 512), ZF.t[:, kc, to:to + 128], DFTC[:, kc, :], kc == 0, kc == 1, ZF.res(to, to + 128, kc) + [R("DFTC")], [bres(b)])
                CP(evac_eng(), AB.t[:, tc, :], bk(b, 0, 512), [bres(b)], AB.res(0, 512, tc))
            nlt = (L + 511) // 512
            lw = min(L, 512)
            if is_ctx:
                cbanks = [next_bank() for _ in range(2)]
                bmap = lambda cc, lt: cbanks[cc]
            else:
                bmap = lambda cc, lt: cc * 4 + lt
            tab = dftlc_d if is_ctx else dftl_d
            for tc in range(ntc):
                for tb in range(2):
                    ring = RING[rr % 3]
                    rr += 1
                    DMA("sp", ring.t[:, 0:L], tab[tb, tc * 128:(tc + 1) * 128, :], (), ring.res(0, L), ring.res(0, L)[0])
                    for cc in range(2):
                        for lt in range(nlt):
                            b = bmap(cc, lt)
                            MM(bk(b, 0, lw), AB.t[:, tc, tb * 256 + cc * 128:tb * 256 + (cc + 1) * 128], ring.t[:, lt * 512:lt * 512 + lw],
                               tc == 0 and tb == 0, tc == ntc - 1 and tb == 1, AB.res(0, 512, tc) + ring.res(0, L), [bres(b)])
            for cc in range(2):
                for lt in range(nlt):
                    b = bmap(cc, lt)
                    yo = o + lt * 512
                    ACT(Y[:, 4 + cc, yo:yo + lw], bk(b, 0, lw), AF.Copy, [bres(b)], [R("Y", 4 + cc, yo // 512)], scale=scale)
        tap("yfou_l%d_T%d" % (l, T), Y[:, 4:6, 0:T], [128, 2, T], [R("Y", 4 + c, ti) for c in range(2) for ti in range(len(tiles))])

    def pool_mix(l, T, seqs, is_ctx):
        ZPT = ABuf("ZPT", [128, 16, 256], BF16, 0)
        PP = ABuf("PP", [128, 2, SEQ], BF16, 8192)
        tiles = tiles_of(T)
        if is_ctx:
            PM = ABuf("PMC", [128, 16, 128], BF16, 16384)
            DMA("sp", PM.t[:], pmc_d.rearrange("g a b p c -> p (g a b) c"), (), PM.res(0, 128 * 16), PM.res(0, 128 * 16)[0])
        else:
            PM = ABuf("PML", [128, 4, 128], BF16, 16384)
            DMA("sp", PM.t[:], pml_d.rearrange("g p c -> p g c"), (), PM.res(0, 128 * 4), PM.res(0, 128 * 4)[0])
        pm_res = PM.res(0, 128 * (16 if is_ctx else 4))
        ws = load_w(wsrc(win_d[l], 0, 8, 1280, 256), 8, 256)
        nchunk = T // 128
        for tc in range(nchunk):
            b = next_bank()
            for k in range(8):
                MM(bk(b, 0, 256), U[:, k, tc * 128:(tc + 1) * 128], ws[0][:, k, 0:256], k == 0, k == 7, ws[1] + [R("U", k, tc // 4)], [bres(b)])
            CP(evac_eng(), ZPT.t[:, tc, :], bk(b, 0, 256), [bres(b)], ZPT.res(0, 256, tc))
        for ti, (o, w) in enumerate(tiles):
            for cc in range(2):
                b = next_bank()
                for tl in range(w // 128):
                    tc = o // 128 + tl
                    if is_ctx:
                        base = (tc // 2) * 2
                        srcs = [(base + ti_in, tc % 2, ti_in) for ti_in in range(2)]
                    else:
                        srcs = [(tc, 0, 0)]
                    for gh in range(2):
                        gidx = cc * 2 + gh
                        for si, (tci, to_, ti_) in enumerate(srcs):
                            pm = PM.t[:, gidx * 4 + to_ * 2 + ti_, :] if is_ctx else PM.t[:, gidx, :]
                            MM(bk(b, tl * 128, (tl + 1) * 128, gh * 64, (gh + 1) * 64), ZPT.t[:, tci, gidx * 64:(gidx + 1) * 64], pm,
                               si == 0, si == len(srcs) - 1, ZPT.res(0, 256, tci) + pm_res, [bres(b)], tile_position=(0, gh * 64))
                CP(evac_eng(), PP.t[:, cc, o:o + w], bk(b, 0, w), [bres(b)], PP.res(o, o + w, cc))
        for cc in range(2):
            for ti, (o, w) in enumerate(tiles):
                b = next_bank()
                MM(bk(b, 0, w), PWB[:, l, cc, :], PP.t[:, cc, o:o + w], True, True, PP.res(o, o + w, cc) + [R("PWB")], [bres(b)])
                ACT(Y[:, 6 + cc, o:o + w], bk(b, 0, w), AF.Copy, [bres(b), R("VEC")], [R("Y", 6 + cc, ti)], scale=vcol(("pscale", l), cc))
        tap("ypool_l%d_T%d" % (l, T), Y[:, 6:8, 0:T], [128, 2, T], [R("Y", 6 + c, ti) for c in range(2) for ti in range(len(tiles))])

    def sconv(l, T, seqs):
        HS = ABuf("HS", [128, 2, SEQ], F32, 0)
        CV = ABuf("CV", [128, SEQ], F32, 16384)
        tiles = tiles_of(T)
        ws = load_w(wsrc(win_d[l], 0, 8, 1536 + 512, 256), 8, 256)
        for oc in range(2):
            g = next_group()
            proj(ws, oc * 128, 8, u_rhs, u_res, tiles, g)
            for ti, (o, w) in enumerate(tiles):
                CP(evac_eng(), HS.t[:, oc, o:o + w], bk(g * 4 + ti, 0, w), [bres(g * 4 + ti)], HS.res(o, o + w, oc))
        ws = load_w(wsrc(win_d[l], 0, 8, 1536 + 256, 256), 8, 256)
        for oc in range(2):
            g = next_group()
            proj(ws, oc * 128, 8, u_rhs, u_res, tiles, g)
            for ti, (o, w) in enumerate(tiles):
                TT("dve", HS.t[:, oc, o:o + w], HS.t[:, oc, o:o + w], bk(g * 4 + ti, 0, w), ALU.mult,
                   HS.res(o, o + w, oc) + [bres(g * 4 + ti)], HS.res(o, o + w, oc))
        ws = load_w(wsrc(win_d[l], 0, 8, 1536, 256), 8, 256)
        for oc in range(2):
            for (o, L, _h, _b) in seqs:
                mr = HS.res(o, o + L, oc)
                cr = CV.res(o, o + L)
                TS("dve", CV.t[:, o:o + L], HS.t[:, oc, o:o + L], vcol(("sw", l), 1 * 2 + oc), None, ALU.mult, None, mr + [R("VEC")], cr)
                STT("dve", CV.t[:, o + 1:o + L], HS.t[:, oc, o:o + L - 1], vcol(("sw", l), 0 * 2 + oc), CV.t[:, o + 1:o + L], ALU.mult, ALU.add,
                    mr + cr + [R("VEC")], cr)
                STT("dve", CV.t[:, o:o + L - 1], HS.t[:, oc, o + 1:o + L], vcol(("sw", l), 2 * 2 + oc), CV.t[:, o:o + L - 1], ALU.mult, ALU.add,
                    mr + cr + [R("VEC")], cr)
            g = next_group()
            proj(ws, oc * 128, 8, u_rhs, u_res, tiles, g)
            for ti, (o, w) in enumerate(tiles):
                TT("dve", Y[:, 8 + oc, o:o + w], CV.t[:, o:o + w], bk(g * 4 + ti, 0, w), ALU.mult, CV.res(o, o + w) + [bres(g * 4 + ti)], [R("Y", 8 + oc, ti)])
        tap("ysc_l%d_T%d" % (l, T), Y[:, 8:10, 0:T], [128, 2, T], [R("Y", 8 + c, ti) for c in range(2) for ti in range(len(tiles))])

    def pass_b(l, T, mi, nrm2, head_fn):
        MG = ABuf("MG", [128, 4, SEQ], BF16, 0)
        MACC = ABuf("MACC", [128, SEQ], F32, 16384)
        GT = ABuf("GT", [128, SEQ], BF16, 24576)
        TTB = [ABuf("TTB%d" % i, [128, 512], F32, 28672 + i * 2048) for i in range(2)]
        tiles = tiles_of(T)
        ybase = (0, 4, 6, 8)
        ykc = (4, 2, 2, 2)
        for half in range(2):
            for jj in range(4):
                j = half * 4 + jj
                for i in range(4):
                    wsg = load_w(wsrc(win_d[l], 0, 8, 2304 + i * 1024 + j * 128, 128), 8, 128)
                    wsb = load_w(wsrc(wbr_d[i][l], 0, ykc[i], j * 128, 128), ykc[i], 128)
                    ga = next_group()
                    proj(wsg, 0, 8, u_rhs, u_res, tiles, ga)
                    for ti, (o, w) in enumerate(tiles):
                        ACT(GT.t[:, o:o + w], bk(ga * 4 + ti, 0, w), AF.Sigmoid, [bres(ga * 4 + ti)], GT.res(o, o + w))
                    gb_ = next_group()
                    proj(wsb, 0, ykc[i], (lambda k, o, w, i=i: Y[:, ybase[i] + k, o:o + w]), (lambda k, ti, i=i: [R("Y", ybase[i] + k, ti)]), tiles, gb_)
                    for ti, (o, w) in enumerate(tiles):
                        b = gb_ * 4 + ti
                        if i == 0:
                            TT("dve", MACC.t[:, o:o + w], GT.t[:, o:o + w], bk(b, 0, w), ALU.mult, GT.res(o, o + w) + [bres(b)], MACC.res(o, o + w))
                        else:
                            tt = TTB[ti % 2]
                            TT("dve", tt.t[:, 0:w], GT.t[:, o:o + w], bk(b, 0, w), ALU.mult, GT.res(o, o + w) + [bres(b)], tt.res(0, w))
                            if i < 3:
                                TT("dve", MACC.t[:, o:o + w], MACC.t[:, o:o + w], tt.t[:, 0:w], ALU.add, MACC.res(o, o + w) + tt.res(0, w), MACC.res(o, o + w))
                            else:
                                TT("dve", MG.t[:, jj, o:o + w], MACC.t[:, o:o + w], tt.t[:, 0:w], ALU.add, MACC.res(o, o + w) + tt.res(0, w), MG.res(o, o + w, jj))
            if half == 0:
                tap("mg_l%d_T%d" % (l, T), MG.t[:, :, 0:T], [128, 4, T], [r for jj in range(4) for r in MG.res(0, T, jj)])
            if half == 1:
                wso = [load_w(wsrc(wout_d[l], 512, 4, jt * 256, 256), 4, 256) for jt in range(4)]

                def wo_tile(ti):
                    o, w = tiles[ti]
                    for jo in range(8):
                        ws = wso[jo // 2]
                        b = next_bank()
                        for k in range(4):
                            MM(bk(b, 0, w), ws[0][:, k, (jo % 2) * 128:(jo % 2) * 128 + 128], MG.t[:, k, o:o + w], k == 0, k == 3,
                               ws[1] + MG.res(o, o + w, k), [bres(b)])
                        STT("dve", XR[:, jo, o:o + w], bk(b, 0, w), modv(l, 2, jo, mi), XR[:, jo, o:o + w], ALU.mult, ALU.add,
                            [bres(b), R("XR", jo, ti), R("MODT")], [R("XR", jo, ti)])
                tail_schedule(len(tiles), wo_tile, nrm2, head=head_fn)
                break
            for jt in range(4):
                ws = load_w(wsrc(wout_d[l], half * 512, 4, jt * 256, 256), 4, 256)
                for oc in range(2):
                    jo = jt * 2 + oc
                    g = next_group()
                    proj(ws, oc * 128, 4, (lambda k, o, w: MG.t[:, k, o:o + w]), (lambda k, ti: MG.res(tiles[ti][0], tiles[ti][0] + tiles[ti][1], k)), tiles, g)
                    for ti, (o, w) in enumerate(tiles):
                        STT("dve", XR[:, jo, o:o + w], bk(g * 4 + ti, 0, w), modv(l, 2, jo, mi), XR[:, jo, o:o + w], ALU.mult, ALU.add,
                            [bres(g * 4 + ti), R("XR", jo, ti), R("MODT")], [R("XR", jo, ti)])

    def make_ffn(l, T, mi):
        RT = [ABuf("RT%d" % i, [128, 512], F32, i * 2048) for i in range(4)]
        tiles = tiles_of(T)
        rr = [0]
        head_ws = []

        def relu2(b, w, hc, o, ti):
            rt = RT[rr[0] % 4]
            rr[0] += 1
            ACT(rt.t[:, 0:w], bk(b, 0, w), AF.Relu, [bres(b)], rt.res(0, w))
            TT("dve", Y[:, hc, o:o + w], rt.t[:, 0:w], rt.t[:, 0:w], ALU.mult, rt.res(0, w), [R("Y", hc, ti)])

        def head(ti):
            if not head_ws:
                for jt in range(4):
                    head_ws.append(load_w(wsrc(wff1_d[l], 0, 8, jt * 256, 256), 8, 256))
            o, w = tiles[ti]
            for hc in range(8):
                ws = head_ws[hc // 2]
                b = next_bank()
                for k in range(8):
                    MM(bk(b, 0, w), ws[0][:, k, (hc % 2) * 128:(hc % 2) * 128 + 128], U[:, k, o:o + w], k == 0, k == 7,
                       ws[1] + [R("U", k, ti)], [bres(b)])
                relu2(b, w, hc, o, ti)

        def rest(tail_norm):
            for g4 in range(4):
                if g4 > 0:
                    for jt in range(4):
                        ws = load_w(wsrc(wff1_d[l], 0, 8, g4 * 1024 + jt * 256, 256), 8, 256)
                        for oc in range(2):
                            hc = jt * 2 + oc
                            g = next_group()
                            proj(ws, oc * 128, 8, u_rhs, u_res, tiles, g)
                            for ti, (o, w) in enumerate(tiles):
                                relu2(g * 4 + ti, w, hc, o, ti)
                if g4 < 3:
                    for jt in range(4):
                        ws = load_w(wsrc(wff2_d[l], g4 * 1024, 8, jt * 256, 256), 8, 256)
                        for oc in range(2):
                            jo = jt * 2 + oc
                            g = next_group()
                            proj(ws, oc * 128, 8, (lambda k, o, w: Y[:, k, o:o + w]), (lambda k, ti: [R("Y", k, ti)]), tiles, g)
                            for ti, (o, w) in enumerate(tiles):
                                STT("dve", XR[:, jo, o:o + w], bk(g * 4 + ti, 0, w), modv(l, 5, jo, mi), XR[:, jo, o:o + w], ALU.mult, ALU.add,
                                    [bres(g * 4 + ti), R("XR", jo, ti), R("MODT")], [R("XR", jo, ti)])
                else:
                    wst = [load_w(wsrc(wff2_d[l], g4 * 1024, 8, jt * 256, 256), 8, 256) for jt in range(4)]

                    def ff2_tile(ti):
                        o, w = tiles[ti]
                        for jo in range(8):
                            ws = wst[jo // 2]
                            b = next_bank()
                            for k in range(8):
                                MM(bk(b, 0, w), ws[0][:, k, (jo % 2) * 128:(jo % 2) * 128 + 128], Y[:, k, o:o + w], k == 0, k == 7,
                                   ws[1] + [R("Y", k, ti)], [bres(b)])
                            STT("dve", XR[:, jo, o:o + w], bk(b, 0, w), modv(l, 5, jo, mi), XR[:, jo, o:o + w], ALU.mult, ALU.add,
                                [bres(b), R("XR", jo, ti), R("MODT")], [R("XR", jo, ti)])
                    tail_schedule(len(tiles), ff2_tile, tail_norm)
        return head, rest

    def store_tokens(dst2d, T):
        OST = [ABuf("OST%d" % i, [128, 1024], F32, i * 4096) for i in range(4)]
        rr = 0
        for ti, (o, w) in enumerate(tiles_of(T)):
            for tc in range(w // 128):
                ost = OST[rr % 4]
                rr += 1
                for kh in range(2):
                    b = next_bank()
                    for kk in range(4):
                        k = kh * 4 + kk
                        TR(bk(b, kk * 128, (kk + 1) * 128), XR[:, k, o + tc * 128:o + (tc + 1) * 128], [R("XR", k, ti)], [bres(b)])
                    CP(evac_eng(), ost.t[:, kh * 512:(kh + 1) * 512], bk(b, 0, 512), [bres(b)], ost.res(kh * 512, (kh + 1) * 512))
                uniq[0] += 1
                op = DMA("sp", dst2d[o + tc * 128:o + (tc + 1) * 128, :], ost.t[:], ost.res(0, 1024), [R("outd", uniq[0])], ost.res(0, 1024)[0])
                final_ops.append(op)

    def mk_norm(T, l, mi, which):
        if which == "final":
            return Norm(T, (lambda k: G32[:, DEPTH * 16 + k:DEPTH * 16 + k + 1]), None, out_inplace=True)
        m_sh = 0 if which == 0 else 3
        return Norm(T, (lambda k: GS[:, l, mi, which, k:k + 1]), (lambda k: modv(l, m_sh, k, mi)))

    def layer(l, T, seqs, mi, is_ctx, state_only, next_norm):
        nt = len(tiles_of(T))

        def tapx(nm):
            tap("%s_l%d_T%d" % (nm, l, T), XR[:, :, 0:T], [128, 8, T], [R("XR", k, ti) for k in range(8) for ti in range(nt)])
        tap("u1_l%d_T%d" % (l, T), U[:, :, 0:T], [128, 8, T], [R("U", k, ti) for k in range(8) for ti in range(nt)])
        lru(l, T, seqs, save_state=state_only)
        if state_only:
            return
        fourier(l, T, seqs, is_ctx)
        pool_mix(l, T, seqs, is_ctx)
        sconv(l, T, seqs)
        ffn_head, ffn_rest = make_ffn(l, T, mi)
        pass_b(l, T, mi, mk_norm(T, l, mi, 1), ffn_head)
        tapx("xmid")
        ffn_rest(next_norm)
        tapx("xout")

    setup()
    tap("modt", MODT[:], [128, DEPTH, 48, NM], [R("MODT")])
    TC = NB * CTX
    load_tokens(ctx_d.rearrange("b t d -> (b t) d"), TC)
    mk_norm(TC, 0, NB, 0).run_all()
    for l in range(DEPTH):
        seqs = [(b * CTX, CTX, (lambda d, c: 0.0), b) for b in range(NB)]
        last = (l == DEPTH - 1)
        layer(l, TC, seqs, NB, True, last, None if last else mk_norm(TC, l + 1, NB, 0))
    tap("hctx", HCTX[:], [128, DEPTH, NB, 2, 4], [R("HCTX")])
    for b in range(NB):
        load_tokens(x_d[b], SEQ)
        mk_norm(SEQ, 0, b, 0).run_all()
        for l in range(DEPTH):
            seqs = [(0, SEQ, (lambda d, c, l=l, b=b: HCTX[:, l, b, d, c:c + 1]), None)]
            nxt = mk_norm(SEQ, l + 1, b, 0) if l + 1 < DEPTH else mk_norm(SEQ, 0, b, "final")
            layer(l, SEQ, seqs, b, False, False, nxt)
        store_tokens(out_d[b], SEQ)
    P.emit(final_wait_ops=final_ops)
    return nc, tap_d


def make_in_maps(inputs, n_cores, NB, DEPTH):
    cst = _consts()
    vecs, _ = pack_vecs(inputs, DEPTH)
    maps = []
    for ci in range(n_cores):
        bs = slice(ci * NB, (ci + 1) * NB)
        call = np.concatenate([np.asarray(inputs["c"][bs], np.float32), np.asarray(inputs["c_ctx"], np.float32)[None, :]], axis=0)
        NM = NB + 1
        ct = np.ascontiguousarray(call.reshape(NM, 8, 128).transpose(2, 1, 0).reshape(128, 8 * NM))
        m = {
            "x": np.ascontiguousarray(inputs["x"][bs], dtype=np.float32),
            "ctx": np.ascontiguousarray(inputs["ctx"][bs], dtype=np.float32),
            "cT": ct,
            "vecs": vecs,
        }
        for name in ("w_mod", "w_in", "lru_w_a", "lru_w_x", "pool_w", "w_br_lru", "w_br_fourier", "w_br_pool", "w_br_sconv", "w_out", "w_ff1", "w_ff2"):
            m[name] = np.ascontiguousarray(np.asarray(inputs[name], np.float32)[:DEPTH])
        for name in ("dft_c", "dft_l", "dft_lc", "pm_lat", "pm_ctx"):
            m[name] = cst[name]
        maps.append(m)
    return maps


def kernel(**inputs):
    n_cores, NB, DEPTH = 8, 4, 2
    inputs = {k: np.asarray(v) for k, v in inputs.items()}
    nc, _ = build_program(Cfg(NB=NB, DEPTH=DEPTH))
    maps = make_in_maps(inputs, n_cores, NB, DEPTH)
    res = run_bass_kernel_spmd(nc, maps, core_ids=list(range(n_cores)))
    out = np.concatenate([np.asarray(r["out"], dtype=np.float32) for r in res.results], axis=0)
    return out
```
